# Optimizing a Trainium2 kernel written in Bass

```python
import math
import jax, jax.numpy as jnp
from jax import lax
import numpy as np

D_MODEL = 2048
BATCH = 2
SEQ = 8192
DEPTH = 1

MEM_LEN = 256
A_HEADS = 6
A_HEAD_DIM = 128
A_KV_RANK = 256
IDX_HEADS = 16
IDX_DIM = 64
TOPK_MAX = 256
Q_BLOCK = 128
B_HEADS = 12
B_KV_HEADS = 2
B_HEAD_DIM = 64
WINDOW = 128
BAND_BLOCK = 128
C_HEADS = 4
C_HEAD_DIM = 128
NUM_BUCKETS = 32
MAX_DISTANCE = 128
EPS = 1e-6

A_WIDTH = A_HEADS * A_HEAD_DIM
B_WIDTH = B_HEADS * B_HEAD_DIM
B_KV_WIDTH = B_KV_HEADS * B_HEAD_DIM
C_WIDTH = C_HEADS * C_HEAD_DIM
N_BRANCH = 3
IN_SIZES = (A_WIDTH, A_KV_RANK, IDX_HEADS * IDX_DIM, IDX_DIM, IDX_HEADS, A_WIDTH,
            B_WIDTH, B_KV_WIDTH, B_KV_WIDTH, B_WIDTH,
            C_WIDTH, C_WIDTH,
            N_BRANCH * D_MODEL)
IN_COLS = sum(IN_SIZES)

kernel_name = "hybrid_dsa_swa_memory_gated"


def rmsnorm(x, g):
    xf = x.astype(jnp.float32)
    y = xf * lax.rsqrt(jnp.mean(xf * xf, axis=-1, keepdims=True) + EPS)
    return (y * g.astype(jnp.float32)).astype(x.dtype)


def layernorm(x, g, b):
    xf = x.astype(jnp.float32)
    mu = jnp.mean(xf, axis=-1, keepdims=True)
    var = jnp.mean(jnp.square(xf - mu), axis=-1, keepdims=True)
    y = (xf - mu) * lax.rsqrt(var + EPS)
    return (y * g.astype(jnp.float32) + b.astype(jnp.float32)).astype(x.dtype)


def split_cols(y, sizes):
    cuts = np.cumsum(np.array(sizes))[:-1].tolist()
    return jnp.split(y, cuts, axis=-1)


def t5_bucket(dist):
    n = jnp.maximum(dist, 0)
    max_exact = NUM_BUCKETS // 2
    nf = jnp.maximum(n, 1).astype(jnp.float32)
    large = max_exact + (jnp.log(nf / max_exact) / math.log(MAX_DISTANCE / max_exact)
                         * (NUM_BUCKETS - max_exact)).astype(jnp.int32)
    large = jnp.minimum(large, NUM_BUCKETS - 1)
    return jnp.where(n < max_exact, n, large)


def dsa_attention(q, k, v, iq, ik, iw, bias_tab, topk):
    bsz, seq, nh, dh = q.shape
    nb = seq // Q_BLOCK
    key_pos = jnp.arange(seq, dtype=jnp.int32)

    def to_blocks(t):
        return jnp.moveaxis(t.reshape((bsz, nb, Q_BLOCK) + t.shape[2:]), 1, 0)

    def block_fn(args):
        qb, iqb, iwb, t0 = args
        tpos = t0 + jnp.arange(Q_BLOCK, dtype=jnp.int32)
        rel = jax.nn.relu(jnp.einsum('bthd,bsd->bths', iqb, ik))
        score = jnp.einsum('bths,bth->bts', rel, iwb).astype(jnp.float32)
        causal = key_pos[None, :] <= tpos[:, None]
        score = jnp.where(causal[None], score, -jnp.inf)
        _, idx = lax.top_k(score, topk)
        kg = jax.vmap(lambda kk, ii: kk[ii])(k, idx)
        vg = jax.vmap(lambda vv, ii: vv[ii])(v, idx)
        logits = jnp.einsum('bthd,btkhd->bthk', qb, kg).astype(jnp.float32) * (dh ** -0.5)
        dist = tpos[None, :, None] - idx
        bias = jnp.transpose(bias_tab[t5_bucket(dist)], (0, 1, 3, 2)).astype(jnp.float32)
        logits = jnp.where((dist >= 0)[:, :, None, :], logits + bias, -jnp.inf)
        p = jax.nn.softmax(logits, axis=-1).astype(vg.dtype)
        return jnp.einsum('bthk,btkhd->bthd', p, vg)

    t0s = jnp.arange(nb, dtype=jnp.int32) * Q_BLOCK
    out = lax.map(block_fn, (to_blocks(q), to_blocks(iq), to_blocks(iw), t0s))
    return jnp.moveaxis(out, 0, 1).reshape(bsz, seq, nh * dh)


def sliding_window_attention(q, k, v, sinks, bias_tab):
    bsz, seq, hq, dh = q.shape
    hkv = k.shape[2]
    g = hq // hkv
    blk = BAND_BLOCK
    nb = seq // blk
    qb = q.reshape(bsz, nb, blk, hkv, g, dh)

    def band(t):
        tb = t.reshape(bsz, nb, blk, hkv, dh)
        prev = jnp.concatenate([jnp.zeros_like(tb[:, :1]), tb[:, :-1]], axis=1)
        return jnp.concatenate([prev, tb], axis=2)

    kband, vband = band(k), band(v)
    logits = jnp.einsum('bnqhgd,bnkhd->bnhgqk', qb, kband).astype(jnp.float32) * (dh ** -0.5)
    qi = jnp.arange(blk, dtype=jnp.int32)[:, None]
    kj = jnp.arange(2 * blk, dtype=jnp.int32)[None, :]
    dist = blk + qi - kj
    bias = jnp.transpose(bias_tab[t5_bucket(dist)], (2, 0, 1)).reshape(hkv, g, blk, 2 * blk)
    key_pos = jnp.arange(nb, dtype=jnp.int32)[:, None, None] * blk + kj[None] - blk
    valid = (dist >= 0) & (dist < WINDOW) & (key_pos >= 0)
    logits = jnp.where(valid[None, :, None, None], logits + bias.astype(jnp.float32), -jnp.inf)
    sink = jnp.broadcast_to(sinks.astype(jnp.float32).reshape(hkv, g, 1, 1), logits.shape[:-1] + (1,))
    p = jax.nn.softmax(jnp.concatenate([logits, sink], axis=-1), axis=-1)[..., :-1]
    out = jnp.einsum('bnhgqk,bnkhd->bnqhgd', p.astype(v.dtype), vband)
    return out.reshape(bsz, seq, hq * dh)


def memory_attention(q, k, v):
    bsz, seq, nh, dh = q.shape
    logits = jnp.einsum('bthd,bmhd->bhtm', q, k).astype(jnp.float32) * (dh ** -0.5)
    p = jax.nn.softmax(logits, axis=-1).astype(v.dtype)
    return jnp.einsum('bhtm,bmhd->bthd', p, v).reshape(bsz, seq, nh * dh)


def setup_inputs(seed: int = 0) -> dict:
    key = jax.random.key(seed)
    ks = jax.random.split(key, 24)
    f32 = jnp.float32

    def w(k, shape, fan_in):
        return jax.random.normal(k, shape, f32) * (fan_in ** -0.5)

    def gain(k, shape):
        return 1.0 + 0.05 * jax.random.normal(k, shape, f32)

    return {
        "x": jax.random.normal(ks[0], (BATCH, SEQ, D_MODEL), f32),
        "mem": jax.random.normal(ks[1], (BATCH, MEM_LEN, D_MODEL), f32),
        "norm_g": gain(ks[2], (DEPTH, D_MODEL)),
        "w_in": w(ks[3], (DEPTH, D_MODEL, IN_COLS), D_MODEL),
        "kv_norm_g": gain(ks[4], (DEPTH, A_KV_RANK)),
        "w_kv_up": w(ks[5], (DEPTH, A_KV_RANK, 2 * A_WIDTH), A_KV_RANK),
        "idx_k_ln_g": gain(ks[6], (DEPTH, IDX_DIM)),
        "idx_k_ln_b": 0.02 * jax.random.normal(ks[7], (DEPTH, IDX_DIM), f32),
        "q_norm_a": gain(ks[8], (DEPTH, A_HEAD_DIM)),
        "k_norm_a": gain(ks[9], (DEPTH, A_HEAD_DIM)),
        "q_norm_b": gain(ks[10], (DEPTH, B_HEAD_DIM)),
        "k_norm_b": gain(ks[11], (DEPTH, B_HEAD_DIM)),
        "sinks_b": 0.5 * jax.random.normal(ks[12], (DEPTH, B_HEADS), f32),
        "mem_norm_g": gain(ks[13], (DEPTH, D_MODEL)),
        "w_mem_kv": w(ks[14], (DEPTH, D_MODEL, 2 * C_WIDTH), D_MODEL),
        "q_norm_c": gain(ks[15], (DEPTH, C_HEAD_DIM)),
        "k_norm_c": gain(ks[16], (DEPTH, C_HEAD_DIM)),
        "w_up_a": w(ks[17], (DEPTH, A_WIDTH, D_MODEL), A_WIDTH),
        "w_up_b": w(ks[18], (DEPTH, B_WIDTH, D_MODEL), B_WIDTH),
        "w_up_c": w(ks[19], (DEPTH, C_WIDTH, D_MODEL), C_WIDTH),
        "gate_bias": 0.02 * jax.random.normal(ks[20], (DEPTH, N_BRANCH * D_MODEL), f32),
        "w_o": w(ks[21], (DEPTH, D_MODEL, D_MODEL), D_MODEL),
        "rel_bias": 0.5 * jax.random.normal(ks[22], (NUM_BUCKETS, A_HEADS + B_HEADS), f32),
    }


def reference(x, mem, norm_g, w_in, kv_norm_g, w_kv_up, idx_k_ln_g, idx_k_ln_b,
              q_norm_a, k_norm_a, q_norm_b, k_norm_b, sinks_b, mem_norm_g, w_mem_kv,
              q_norm_c, k_norm_c, w_up_a, w_up_b, w_up_c, gate_bias, w_o, rel_bias):
    bsz, seq, _ = x.shape
    mlen = mem.shape[1]
    topk = min(TOPK_MAX, seq // 4)
    bias_a = rel_bias[:, :A_HEADS]
    bias_b = rel_bias[:, A_HEADS:]
    for l in range(DEPTH):
        h = rmsnorm(x, norm_g[l])
        (a_q, a_ckv, i_q, i_k, i_w, a_gate,
         b_q, b_k, b_v, b_gate,
         c_q, c_gate, mix_gate) = split_cols(h @ w_in[l], IN_SIZES)

        ckv = rmsnorm(a_ckv, kv_norm_g[l])
        a_k, a_v = jnp.split(ckv @ w_kv_up[l], 2, axis=-1)
        qa = rmsnorm(a_q.reshape(bsz, seq, A_HEADS, A_HEAD_DIM), q_norm_a[l])
        ka = rmsnorm(a_k.reshape(bsz, seq, A_HEADS, A_HEAD_DIM), k_norm_a[l])
        va = a_v.reshape(bsz, seq, A_HEADS, A_HEAD_DIM)
        iq = i_q.reshape(bsz, seq, IDX_HEADS, IDX_DIM)
        ik = layernorm(i_k, idx_k_ln_g[l], idx_k_ln_b[l])
        iw = i_w * (IDX_HEADS ** -0.5) * (IDX_DIM ** -0.5)
        oa = dsa_attention(qa, ka, va, iq, ik, iw, bias_a, topk) * jax.nn.silu(a_gate)

        qb = rmsnorm(b_q.reshape(bsz, seq, B_HEADS, B_HEAD_DIM), q_norm_b[l])
        kb = rmsnorm(b_k.reshape(bsz, seq, B_KV_HEADS, B_HEAD_DIM), k_norm_b[l])
        vb = b_v.reshape(bsz, seq, B_KV_HEADS, B_HEAD_DIM)
        ob = sliding_window_attention(qb, kb, vb, sinks_b[l], bias_b) * jax.nn.silu(b_gate)

        m_k, m_v = jnp.split(rmsnorm(mem, mem_norm_g[l]) @ w_mem_kv[l], 2, axis=-1)
        qc = rmsnorm(c_q.reshape(bsz, seq, C_HEADS, C_HEAD_DIM), q_norm_c[l])
        kc = rmsnorm(m_k.reshape(bsz, mlen, C_HEADS, C_HEAD_DIM), k_norm_c[l])
        vc = m_v.reshape(bsz, mlen, C_HEADS, C_HEAD_DIM)
        oc = memory_attention(qc, kc, vc) * jax.nn.silu(c_gate)

        g_a, g_b, g_c = jnp.split(jax.nn.sigmoid(mix_gate + gate_bias[l]), N_BRANCH, axis=-1)
        merged = g_a * (oa @ w_up_a[l]) + g_b * (ob @ w_up_b[l]) + g_c * (oc @ w_up_c[l])
        x = x + merged @ w_o[l]
    return x
```

```python
import numpy as np
import concourse.bass as bass
import concourse.mybir as mybir
from concourse.bass_utils import run_bass_kernel_spmd

F32 = mybir.dt.float32
BF16 = mybir.dt.bfloat16
AF = mybir.ActivationFunctionType
ALU = mybir.AluOpType
AX = mybir.AxisListType


class Res:
    __slots__ = ("name", "w", "r")

    def __init__(self, name):
        self.name = name
        self.w = None
        self.r = {}


class Op:
    __slots__ = ("eng", "fn", "deps", "dma", "sig", "sem", "val", "idx")

    def __init__(self, eng, fn, deps, dma):
        self.eng = eng
        self.fn = fn
        self.deps = deps
        self.dma = dma
        self.sig = False
        self.sem = None
        self.val = 0


def L(name, *a, **k):
    return lambda e: getattr(e, name)(*a, **k)


class Sched:
    ENGS = ("sync", "scalar", "vector", "gpsimd", "tensor")
    NPOOL = 12

    def __init__(self, nc):
        self.nc = nc
        self.ops = {e: [] for e in self.ENGS}
        self.nres = 0

    def res(self, name=None):
        self.nres += 1
        return Res(name or f"r{self.nres}")

    def _deps(self, eng, reads, writes):
        deps = []
        for r in list(reads) + list(writes):
            if r.w is not None:
                deps.append(r.w)
        for w in writes:
            deps.extend(w.r.values())
        out = []
        seen = set()
        for d in deps:
            if id(d) in seen:
                continue
            seen.add(id(d))
            if d.eng == eng and eng == "tensor" and not d.dma:
                continue
            out.append(d)
        return out

    def _record(self, op, reads, writes):
        self.ops[op.eng].append(op)
        for r in reads:
            r.r[(op.eng, id(op)) if op.dma else (op.eng,)] = op
        for w in writes:
            w.w = op
            w.r = {}
        return op

    def op(self, eng, fn, reads=(), writes=()):
        o = Op(eng, fn, self._deps(eng, reads, writes), False)
        return self._record(o, reads, writes)

    def dma(self, eng, out, in_, reads=(), writes=(), **kw):
        o = Op(eng, L("dma_start", out=out, in_=in_, **kw),
               self._deps(eng, reads, writes), True)
        return self._record(o, reads, writes)

    def wait_all(self, eng, ops):
        o = Op(eng, None, list(ops), False)
        self.ops[eng].append(o)
        return o

    def barrier(self):
        lasts = []
        for e in self.ENGS:
            ops = self.ops[e]
            comp = [o for o in ops if (not o.dma) and o.fn is not None]
            if comp:
                lasts.append(comp[-1])
            lasts += [o for o in ops if o.dma][-self.NPOOL:]
        for e in self.ENGS:
            self.wait_all(e, lasts)

    def emit(self):
        nc = self.nc
        for e in self.ENGS:
            dmas = [o for o in self.ops[e] if o.dma]
            for k, o in enumerate(dmas):
                o.sig = True
                if k >= self.NPOOL:
                    o.deps.append(dmas[k - self.NPOOL])
        for e in self.ENGS:
            for o in self.ops[e]:
                for d in o.deps:
                    d.sig = True
        import contextlib
        with contextlib.ExitStack() as st:
            esem = {e: st.enter_context(nc.semaphore(f"s_{e}")) for e in self.ENGS}
            pools = {e: [st.enter_context(nc.semaphore(f"d_{e}{i}")) for i in range(self.NPOOL)]
                     for e in self.ENGS if any(o.dma for o in self.ops[e])}
            for e in self.ENGS:
                c = 0
                k = 0
                for o in self.ops[e]:
                    if o.dma:
                        o.sem = pools[e][k % self.NPOOL]
                        o.val = 16 * (k // self.NPOOL + 1)
                        k += 1
                    elif o.sig:
                        c += 1
                        o.sem = esem[e]
                        o.val = c
            block = st.enter_context(nc.Block())

            def run(e_name):
                def body(e):
                    waited = {}
                    for o in self.ops[e_name]:
                        for d in o.deps:
                            key = id(d.sem)
                            if waited.get(key, 0) < d.val:
                                e.wait_ge(d.sem, d.val)
                                waited[key] = d.val
                        if o.fn is None:
                            continue
                        inst = o.fn(e)
                        if o.sig:
                            inst.then_inc(o.sem, 16 if o.dma else 1)
                return body

            for e_name in self.ENGS:
                if self.ops[e_name]:
                    getattr(block, e_name)(run(e_name))


D = 2048
SEQ = 8192
NB = 64
NQ = 16
KC = 16
EPS = 1e-6
NIT = 20
LIM = 16.0
NEG = -1.0e30
O_AQ, O_CKV, O_IQ, O_IK, O_IW, O_AG, O_BQ, O_BK, O_BV, O_BG, O_CQ, O_CG, O_MIX = (
    0, 768, 1024, 2048, 2112, 2128, 2896, 3664, 3792, 3920, 4688, 5200, 5712)
KINDS = (["aq"] * 6 + ["cq"] * 4 + ["bq"] * 6 + ["iq"] * 8 + ["silu"] * 16 + ["mix"] * 48)
C_AQ, C_CQ, C_BQ, C_IQ, C_AG, C_BG, C_CG, C_MIX = 0, 6, 10, 16, 24, 30, 36, 40
NCH = 88


def _bucket(dist):
    n = np.maximum(dist, 0)
    nf = np.maximum(n, 1).astype(np.float32)
    large = 16 + (np.log(nf / np.float32(16)) / np.float32(np.log(128 / 16)) * np.float32(16)).astype(np.int32)
    large = np.minimum(large, 31)
    return np.where(n < 16, n, large)


def _bperm():
    idx = []
    for c in range(6):
        idx += list(range(c * 64, c * 64 + 64)) + list(range((6 + c) * 64, (6 + c) * 64 + 64))
    return np.array(idx)


def prep_inputs(inp):
    f = lambda a: np.ascontiguousarray(np.asarray(a, dtype=np.float32))
    x = f(inp["x"]); mem = f(inp["mem"]); w_in = f(inp["w_in"])[0]
    bp = _bperm()
    cols = np.concatenate([
        np.arange(O_AQ, O_AQ + 768), np.arange(O_CQ, O_CQ + 512), O_BQ + bp,
        np.arange(O_IQ, O_IQ + 1024), np.arange(O_AG, O_AG + 768), O_BG + bp,
        np.arange(O_CG, O_CG + 512), np.arange(O_MIX, O_MIX + 6144)])
    sh = {}
    sh["wq"] = f(w_in[:, cols])
    sh["wiw"] = f(w_in[:, O_IW:O_IW + 16])
    kcols = np.concatenate([np.arange(O_CKV, O_CKV + 256), np.arange(O_BK, O_BK + 128),
                            np.arange(O_BV, O_BV + 128), np.arange(O_IK, O_IK + 64)])
    sh["wk"] = f(w_in[:, kcols])
    pk = lambda v: f(np.asarray(v).reshape(-1, 128).T)
    sh["normg"] = pk(inp["norm_g"][0]); sh["memg"] = pk(inp["mem_norm_g"][0])
    sh["wmem"] = f(inp["w_mem_kv"][0])
    wkv = f(inp["w_kv_up"][0])
    sh["wkvu"] = wkv
    sh["wkT"] = f(wkv[:, :768].reshape(256, 6, 128).transpose(2, 1, 0))
    sh["kvg"] = pk(inp["kv_norm_g"][0])
    sh["gkvbc"] = f(np.broadcast_to(np.asarray(inp["kv_norm_g"][0])[None, :], (128, 256)))
    col = lambda v: f(np.asarray(v).reshape(128, 1))
    t2 = lambda v: np.concatenate([np.asarray(v), np.asarray(v)])
    sh["gv"] = f(np.concatenate([col(inp["q_norm_a"][0]), col(inp["k_norm_a"][0]),
                                 col(t2(inp["q_norm_b"][0])), col(t2(inp["k_norm_b"][0])),
                                 col(inp["q_norm_c"][0]), col(inp["k_norm_c"][0])], axis=1))
    sh["lng"] = f(np.broadcast_to(np.asarray(inp["idx_k_ln_g"][0])[None, :], (128, 64)))
    sh["lnb"] = f(np.broadcast_to(np.asarray(inp["idx_k_ln_b"][0])[None, :], (128, 64)))
    sh["gbias"] = pk(inp["gate_bias"][0])
    sk = np.asarray(inp["sinks_b"][0])
    sh["sk"] = f(np.concatenate([np.broadcast_to(sk[None, 0:6], (64, 6)),
                                 np.broadcast_to(sk[None, 6:12], (64, 6))], axis=0))
    rb = np.asarray(inp["rel_bias"], dtype=np.float32)
    s = np.arange(128)[:, None]; t = np.arange(128)[None, :]
    d0 = t - s; d1 = 128 + t - s
    bk = np.stack([_bucket(d0), _bucket(d1)], axis=1)
    sh["ba"] = f(rb[bk][:, :, :, :6].transpose(0, 1, 3, 2))
    sh["ca"] = f(np.broadcast_to(rb[31, :6][None, None, :, None], (128, 2, 6, 128)))
    sh["va"] = f(np.stack([(d0 >= 0), np.ones_like(d0, bool)], axis=1))
    bkb = np.stack([_bucket(d1), _bucket(d0)], axis=1)
    sh["bb"] = f(rb[bkb][:, :, :, 6:18].reshape(128, 2, 128, 2, 6).transpose(0, 1, 3, 4, 2))
    sh["vb01"] = f(np.stack([(d1 < 128), (d0 >= 0)], axis=1))
    sh["cmask"] = f(np.where(np.arange(128)[None, :] <= np.arange(128)[:, None], 0.0, NEG))
    ident = np.eye(128, dtype=np.float32)
    blk64 = np.kron(np.eye(2, dtype=np.float32), np.ones((64, 64), np.float32))
    onesh = np.zeros((128, 2, 128), np.float32); onesh[:, 0, :64] = 1; onesh[:, 1, 64:] = 1
    sh["cst"] = f(np.concatenate([ident, np.ones((128, 128), np.float32), blk64,
                                  onesh.reshape(128, 256), np.tile(ident * 32768.0, (1, 3))], axis=1))
    sh["wupa"] = f(inp["w_up_a"][0]); sh["wupb"] = f(inp["w_up_b"][0][bp]); sh["wupc"] = f(inp["w_up_c"][0])
    sh["wo"] = f(inp["w_o"][0])
    per = []
    for c in range(8):
        b, j = c // 4, c % 4
        npad = (3 - j) * 128
        xk = np.zeros((SEQ, D), np.float32)
        xk[npad:] = x[b, :SEQ - npad]
        padm = np.zeros((128, 384), np.float32); padm[:, :npad] = NEG
        bpv = np.full((128, 1), 0.0 if j == 0 else 1.0, np.float32)
        per.append({"xk": xk, "mem": f(mem[b]), "padm": padm, "bpv": bpv})
    return sh, per


def build(debug=False, phases="QKAM"):
    import contextlib
    nc = bass.Bass("TRN2", target_bir_lowering=False)
    dbg_kind = "ExternalOutput" if debug else "Internal"

    def din(name, shape, dt=F32):
        return nc.dram_tensor(name, list(shape), dt, kind="ExternalInput").ap()

    xk = din("xk", [SEQ, D]); memd = din("mem", [256, D])
    wq = din("wq", [D, NCH * 128]); wiw = din("wiw", [D, 16]); wk = din("wk", [D, 576])
    normg = din("normg", [128, 16]); memg = din("memg", [128, 16]); wmem = din("wmem", [D, 1024])
    wkvu = din("wkvu", [256, 1536]); wkTd = din("wkT", [128, 6, 256]); kvg = din("kvg", [128, 2])
    gkvbc = din("gkvbc", [128, 256]); gvd = din("gv", [128, 6]); lngd = din("lng", [128, 64]); lnbd = din("lnb", [128, 64])
    gbiasd = din("gbias", [128, 48]); skd = din("sk", [128, 6])
    bad = din("ba", [128, 2, 6, 128]); cad = din("ca", [128, 2, 6, 128]); vad = din("va", [128, 2, 128])
    bbd = din("bb", [128, 2, 2, 6, 128]); vb01d = din("vb01", [128, 2, 128]); cmaskd = din("cmask", [128, 128])
    cstd = din("cst", [128, 1024]); padmd = din("padm", [128, 384]); bpvd = din("bpv", [128, 1])
    wupd = [din("wupa", [768, D]), din("wupb", [768, D]), din("wupc", [512, D])]
    wod = din("wo", [D, D])
    outd = nc.dram_tensor("out", [NQ * 128, D], F32, kind="ExternalOutput").ap()
    QS = nc.dram_tensor("QS", [NCH, NQ, 128, 128], BF16, kind=dbg_kind).ap()
    IWS = nc.dram_tensor("IWS", [NQ, 16, 128], BF16, kind=dbg_kind).ap()
    OS = nc.dram_tensor("OS", [16, NQ, 128, 128], BF16, kind=dbg_kind).ap()
    DBG = nc.dram_tensor("DBG", [128, 16384], F32, kind=dbg_kind).ap() if debug else None

    S = Sched(nc)
    top = contextlib.ExitStack()
    with top:
        uid = [0]

        def SB(es, name, shape, dt):
            uid[0] += 1
            return es.enter_context(nc.sbuf_tensor(f"{name}_{uid[0]}", list(shape), dt))

        PS = [top.enter_context(nc.psum_tensor(f"ps{b}", [128, 512], F32)) for b in range(8)]
        RPS = [S.res(f"ps{b}") for b in range(8)]
        PSB = PS[7][:].bitcast(BF16)

        CST = SB(top, "CST", [128, 1024], BF16)
        NEGH = SB(top, "NEGH", [128, 8], F32)
        GV = SB(top, "GV", [128, 8], F32)
        KCT = SB(top, "KCT", [128, 4, 256], BF16)
        VC = SB(top, "VC", [128, 2, 512], BF16)
        Rc = S.res("const")
        IDENT = CST[:, 0:128]; ONES = CST[:, 128:256]; BLK64 = CST[:, 256:384]
        ONESH = CST[:, 384:640]; IREP = CST[:, 640:1024]
        with contextlib.ExitStack() as es:
            stg = SB(es, "cstg", [128, 1024], F32)
            r = S.res()
            S.dma("sync", stg[:], cstd[:, :], writes=[r])
            S.op("vector", L("tensor_copy", out=CST[:], in_=stg[:]), reads=[r], writes=[Rc])
            S.op("vector", L("memset", NEGH[:], -0.5), writes=[Rc])
            S.dma("sync", GV[:, 0:6], gvd[:, :], reads=[], writes=[Rc])
            S.op("vector", L("tensor_tensor", out=GV[:, 6:7], in0=GV[:, 2:3], in1=GV[:, 3:4], op=ALU.mult), reads=[Rc], writes=[Rc])
            S.op("vector", L("tensor_tensor", out=GV[:, 7:8], in0=GV[:, 4:5], in1=GV[:, 5:6], op=ALU.mult), reads=[Rc], writes=[Rc])

        rot = {}

        def nxt(key, n):
            rot[key] = (rot.get(key, -1) + 1) % n
            return rot[key]

        def pow_rstd(out_ap, in_ap, rres, wres, n):
            shp = in_ap.shape
            S.op("gpsimd", L("tensor_tensor", out=out_ap, in0=in_ap, in1=NEGH[:shp[0], 0:n], op=ALU.pow),
                 reads=rres + [Rc], writes=wres)

        class NT:
            def __init__(self, es):
                self.xs = [SB(es, f"nt_xs{i}", [128, D], F32) for i in range(2)]
                self.hb = [SB(es, f"nt_hb{i}", [128, D], BF16) for i in range(2)]
                self.st = [SB(es, f"nt_st{i}", [128, 4], F32) for i in range(2)]
                self.rx = [S.res() for _ in range(2)]; self.rh = [S.res() for _ in range(2)]
                self.rs = [S.res() for _ in range(2)]

            def run(self, src_rows, dst, rdst):
                b = nxt("nt", 2)
                xs, hb, st = self.xs[b], self.hb[b], self.st[b]
                rx, rh, rs = self.rx[b], self.rh[b], self.rs[b]
                S.dma("sync", xs[:], src_rows, writes=[rx])
                S.op("vector", L("scalar_tensor_tensor", out=hb[:], in0=xs[:], scalar=1.0, in1=xs[:], op0=ALU.mult,
                                                                op1=ALU.mult, accum_out=st[:, 0:1]), reads=[rx], writes=[rh, rs])
                S.op("vector", L("tensor_scalar", out=st[:, 1:2], in0=st[:, 0:1], scalar1=1.0 / D, scalar2=EPS,
                                                         op0=ALU.mult, op1=ALU.add), reads=[rs], writes=[rs])
                pow_rstd(st[:, 2:3], st[:, 1:2], [rs], [rs], 1)
                S.op("vector", L("tensor_scalar", out=hb[:], in0=xs[:], scalar1=st[:, 2:3], scalar2=None, op0=ALU.mult),
                     reads=[rx, rs], writes=[rh])
                for half in range(2):
                    for k in range(8):
                        kk = half * 8 + k
                        S.op("tensor", L("transpose", out=PSB[:, k * 128:(k + 1) * 128], in_=hb[:, kk * 128:(kk + 1) * 128],
                                                                         identity=IDENT), reads=[rh, Rc], writes=[RPS[7]])
                    S.op("scalar", L("copy", out=dst[:, half * 8:(half + 1) * 8, :],
                                                               in_=PSB.rearrange("p (k t) -> p k t", k=8)),
                         reads=[RPS[7]], writes=[rdst])

        def load_cast_w(es_stage, dst, src_dram_rows, ncols, gscale, rdst, nk=KC, chunkc=256, eng="vector"):
            stg = es_stage["stg"]; rstg = es_stage["rstg"]
            for c0 in range(0, ncols, chunkc):
                cw = min(chunkc, ncols - c0)
                b = nxt("wstg", 2)
                S.dma("sync", stg[b][:, 0:nk, 0:cw], src_dram_rows[:, c0:c0 + cw].rearrange("(k p) c -> p k c", p=128), writes=[rstg[b]])
                for k in range(nk):
                    if gscale is None:
                        S.op(eng, L("tensor_copy", out=dst[:, k, c0:c0 + cw], in_=stg[b][:, k, 0:cw]),
                             reads=[rstg[b]], writes=[rdst])
                    else:
                        S.op(eng, L("tensor_scalar", out=dst[:, k, c0:c0 + cw], in0=stg[b][:, k, 0:cw],
                                                                                   scalar1=gscale[:, k:k + 1], scalar2=None, op0=ALU.mult),
                             reads=[rstg[b], Rc], writes=[rdst])

        def mk_stage(es):
            return {"stg": [SB(es, f"wstg{i}", [128, KC, 256], F32) for i in range(2)], "rstg": [S.res() for _ in range(2)]}

        out_ops = []
        rKC = S.res()
        rQS = {}; rIWS = {}; rOS = {}

        if "Q" in phases:
            with contextlib.ExitStack() as es:
                HT = SB(es, "HT", [128, KC, NQ * 128], BF16); rHT = S.res()
                GN = SB(es, "GN", [128, 16], F32); MG = SB(es, "MG", [128, 16], F32); GBI = SB(es, "GBI", [128, 48], F32)
                S.dma("sync", GN[:], normg[:, :], writes=[Rc]); S.dma("sync", MG[:], memg[:, :], writes=[Rc])
                S.dma("sync", GBI[:], gbiasd[:, :], writes=[Rc])
                nt = NT(es)
                stage = mk_stage(es)
                SQ = [SB(es, f"SQ{i}", [128, 512], BF16) for i in range(2)]; rSQ = [S.res() for _ in range(2)]
                T1 = [SB(es, f"T1{i}", [128, 512], F32) for i in range(2)]; rT1 = [S.res() for _ in range(2)]
                T2 = [SB(es, f"T2{i}", [128, 512], F32) for i in range(2)]; rT2 = [S.res() for _ in range(2)]
                OT = [SB(es, f"OT{i}", [128, 512], BF16) for i in range(3)]; rOT = [S.res() for _ in range(3)]
                ST4 = SB(es, "ST4", [128, 16], F32); rST4 = S.res()
                with contextlib.ExitStack() as es2:
                    WM = SB(es2, "WM", [128, KC, 1024], BF16); rWM = S.res()
                    HTM = SB(es2, "HTM", [128, KC, 128], BF16); rHTM = S.res()
                    KCN = SB(es2, "KCN", [128, 512], BF16); rKCN = S.res()
                    load_cast_w(stage, WM, wmem, 1024, MG, rWM)
                    for mb in range(2):
                        nt.run(memd[mb * 128:(mb + 1) * 128, :], HTM, rHTM)
                        for n in range(2):
                            for k in range(KC):
                                S.op("tensor", L("matmul", PS[n][:, :], lhsT=HTM[:, k, :], rhs=WM[:, k, n * 512:(n + 1) * 512],
                                                                            start=(k == 0), stop=(k == KC - 1)), reads=[rHTM, rWM], writes=[RPS[n]])
                        for h in range(4):
                            S.op("scalar", L("activation", out=T1[0][:, 0:128], in_=PS[0][:, h * 128:(h + 1) * 128], func=AF.Square,
                                                                       accum_out=ST4[:, h:h + 1]), reads=[RPS[0]], writes=[rT1[0], rST4])
                        S.op("vector", L("tensor_scalar", out=ST4[:, 4:8], in0=ST4[:, 0:4], scalar1=1.0 / 128, scalar2=EPS, op0=ALU.mult, op1=ALU.add),
                             reads=[rST4], writes=[rST4])
                        pow_rstd(ST4[:, 8:12], ST4[:, 4:8], [rST4], [rST4], 4)
                        for h in range(4):
                            S.op("vector", L("tensor_scalar", out=KCN[:, h * 128:(h + 1) * 128], in0=PS[0][:, h * 128:(h + 1) * 128],
                                                                          scalar1=ST4[:, 8 + h:9 + h], scalar2=None, op0=ALU.mult),
                                 reads=[RPS[0], rST4], writes=[rKCN])
                        for h in range(4):
                            S.op("tensor", L("transpose", out=PSB[:, h * 128:(h + 1) * 128], in_=KCN[:, h * 128:(h + 1) * 128], identity=IDENT),
                                 reads=[rKCN, Rc], writes=[RPS[7]])
                        S.op("scalar", L("copy", out=KCT[:, :, mb * 128:(mb + 1) * 128], in_=PSB[:, 0:512].rearrange("p (h t) -> p h t", h=4)),
                             reads=[RPS[7]], writes=[rKC])
                        S.op("scalar", L("copy", out=VC[:, mb, :], in_=PS[1][:, :]), reads=[RPS[1]], writes=[rKC])
                S.barrier()
                for i in range(NQ):
                    p = 4 * i + 3
                    nt.run(xk[p * 128:(p + 1) * 128, :], HT[:, :, i * 128:(i + 1) * 128], rHT)
                WB = [SB(es, f"WB{i}", [128, KC, 256], BF16) for i in range(2)]; rWB = [S.res() for _ in range(2)]
                for g in range(NCH // 2):
                    b = nxt("wb", 2)
                    load_cast_w(stage, WB[b], wq[:, g * 256:(g + 1) * 256], 256, GN, rWB[b])
                    for cc in range(2):
                        ch = 2 * g + cc
                        kind = KINDS[ch]
                        for m in range(4):
                            pb = nxt("qacc", 3)
                            for k in range(KC):
                                S.op("tensor", L("matmul", PS[pb][:, :], lhsT=WB[b][:, k, cc * 128:(cc + 1) * 128], rhs=HT[:, k, m * 512:(m + 1) * 512],
                                    start=(k == 0), stop=(k == KC - 1)), reads=[rWB[b], rHT], writes=[RPS[pb]])
                            ob = nxt("ot", 3)
                            if kind == "iq":
                                S.op("scalar", L("copy", out=OT[ob][:], in_=PS[pb][:, :]), reads=[RPS[pb]], writes=[rOT[ob]])
                            elif kind == "silu":
                                S.op("scalar", L("activation", out=OT[ob][:], in_=PS[pb][:, :], func=AF.Silu),
                                     reads=[RPS[pb]], writes=[rOT[ob]])
                            elif kind == "mix":
                                S.op("scalar", L("activation", out=OT[ob][:], in_=PS[pb][:, :], func=AF.Sigmoid,
                                                                                          bias=GBI[:, ch - C_MIX:ch - C_MIX + 1]),
                                     reads=[RPS[pb], Rc], writes=[rOT[ob]])
                            else:
                                dd = 64 if kind == "bq" else 128
                                gcol = {"aq": 0, "cq": 7, "bq": 6}[kind]
                                lh = BLK64 if kind == "bq" else ONES
                                sb_ = nxt("sq", 2)
                                sbank = 3 + nxt("ss", 2)
                                S.op("scalar", L("activation", out=SQ[sb_][:], in_=PS[pb][:, :], func=AF.Square),
                                     reads=[RPS[pb]], writes=[rSQ[sb_]])
                                S.op("tensor", L("matmul", PS[sbank][:, :], lhsT=lh, rhs=SQ[sb_][:], start=True, stop=True),
                                     reads=[rSQ[sb_], Rc], writes=[RPS[sbank]])
                                S.op("scalar", L("activation", out=T1[sb_][:], in_=PS[sbank][:, :], func=AF.Ln, bias=EPS, scale=1.0 / dd),
                                     reads=[RPS[sbank]], writes=[rT1[sb_]])
                                S.op("scalar", L("activation", out=T2[sb_][:], in_=T1[sb_][:], func=AF.Exp, scale=-0.5), reads=[rT1[sb_]], writes=[rT2[sb_]])
                                S.op("vector", L("scalar_tensor_tensor", out=OT[ob][:], in0=PS[pb][:, :], scalar=GV[:, gcol:gcol + 1], in1=T2[sb_][:], op0=ALU.mult, op1=ALU.mult),
                                    reads=[RPS[pb], rT2[sb_], Rc], writes=[rOT[ob]])
                            rQS[(ch, m)] = S.res()
                            S.dma("gpsimd", QS[ch, 4 * m:4 * m + 4].rearrange("q p t -> p q t"), OT[ob][:].rearrange("p (q t) -> p q t", q=4),
                                  reads=[rOT[ob]], writes=[rQS[(ch, m)]])
                WIW = SB(es, "WIW", [128, KC, 16], BF16); rWIW = S.res()
                OTW = SB(es, "OTW", [16, 512], BF16); rOTW = S.res()
                load_cast_w(stage, WIW, wiw, 16, GN, rWIW)
                for m in range(4):
                    pb = nxt("qacc", 3)
                    for k in range(KC):
                        S.op("tensor", L("matmul", PS[pb][0:16, :], lhsT=WIW[:, k, :], rhs=HT[:, k, m * 512:(m + 1) * 512],
                                                                           start=(k == 0), stop=(k == KC - 1)), reads=[rWIW, rHT], writes=[RPS[pb]])
                    S.op("scalar", L("mul", out=OTW[:], in_=PS[pb][0:16, :], mul=1.0 / 32.0), reads=[RPS[pb]], writes=[rOTW])
                    rIWS[m] = S.res()
                    S.dma("gpsimd", IWS[4 * m:4 * m + 4].rearrange("q h t -> h q t"), OTW[:].rearrange("h (q t) -> h q t", q=4),
                          reads=[rOTW], writes=[rIWS[m]])

        S.barrier()
        mid = contextlib.ExitStack()
        top.enter_context(mid)
        CKVNT = SB(mid, "CKVNT", [128, 2, SEQ], BF16); rCKVNT = S.res()
        CKVN = SB(mid, "CKVN", [128, NB, 256], BF16); rCKVN = S.res()
        IKT = SB(mid, "IKT", [128, SEQ], BF16); rIKT = S.res()
        KBT = SB(mid, "KBT", [128, 32, 128], BF16); rKBT = S.res()
        VB = SB(mid, "VB", [128, 32, 128], BF16); rVB = S.res()
        RK = SB(mid, "RK", [128, NB, 6], F32); rRK = S.res()
        WV = SB(mid, "WV", [128, 2, 768], BF16)
        WKT = SB(mid, "WKT", [128, 6, 256], BF16)

        if "K" in phases:
            with contextlib.ExitStack() as es:
                nt = NT(es)
                stage = mk_stage(es)
                GN = SB(es, "GNk", [128, 16], F32); KVG = SB(es, "KVG", [128, 2], F32)
                LNG = SB(es, "LNG", [128, 64], F32); LNB = SB(es, "LNB", [128, 64], F32); GKV = SB(es, "GKV", [128, 256], F32)
                S.dma("sync", GN[:], normg[:, :], writes=[Rc]); S.dma("sync", KVG[:], kvg[:, :], writes=[Rc])
                S.dma("sync", LNG[:], lngd[:, :], writes=[Rc]); S.dma("sync", LNB[:], lnbd[:, :], writes=[Rc])
                S.dma("sync", GKV[:], gkvbc[:, :], writes=[Rc])
                WKb = SB(es, "WKb", [128, KC, 576], BF16); rWKb = S.res()
                WKUK = SB(es, "WKUK", [128, 2, 768], BF16); rWKU = S.res()
                load_cast_w(stage, WKb, wk, 576, GN, rWKb, chunkc=192)
                load_cast_w(stage, WKUK, wkvu[:, 0:768], 768, KVG, rWKU, nk=2)
                load_cast_w(stage, WV, wkvu[:, 768:1536], 768, KVG, rWKU, nk=2)
                WT32 = SB(es, "WT32", [128, 6, 256], F32); rWT = S.res()
                S.dma("sync", WT32[:], wkTd[:, :, :], writes=[rWT])
                for h in range(6):
                    S.op("vector", L("scalar_tensor_tensor", out=WKT[:, h, :], in0=WT32[:, h, :], scalar=GV[:, 1:2], in1=GKV[:],
                                                                         op0=ALU.mult, op1=ALU.mult), reads=[rWT, Rc], writes=[rWKU])
                HTB = [SB(es, f"HTB{i}", [128, KC, 128], BF16) for i in range(2)]; rHTB = [S.res() for _ in range(2)]
                JK = SB(es, "JK", [128, 256], BF16); rJK = S.res()
                SK_ = [SB(es, f"SKt{i}", [128, 32], F32) for i in range(2)]; rSK = [S.res() for _ in range(2)]
                dbgK = (SK_, rSK, JK, rJK)
                IK32 = SB(es, "IK32", [128, 64], F32); rIK32 = S.res()
                IKD = SB(es, "IKD", [128, 128], BF16); rIKD = S.res()
                KBD = SB(es, "KBD", [128, 128], BF16); rKBD = S.res()
                PSB6 = PS[6][:].bitcast(BF16)

                def k_A(p):
                        b = p % 2
                        kA, kB = (0, 1) if p % 2 == 0 else (4, 5)
                        hT = HTB[b]; rh = rHTB[b]
                        nt.run(xk[p * 128:(p + 1) * 128, :], hT, rh)

                def k_B(p):
                        b = p % 2
                        kA, kB = (0, 1) if p % 2 == 0 else (4, 5)
                        hT = HTB[b]; rh = rHTB[b]
                        for k in range(KC):
                            S.op("tensor", L("matmul", PS[kA][:, :], lhsT=hT[:, k, :], rhs=WKb[:, k, 0:512], start=(k == 0), stop=(k == KC - 1)),
                                 reads=[rh, rWKb], writes=[RPS[kA]])
                        for k in range(KC):
                            S.op("tensor", L("matmul", PS[kB][:, 0:64], lhsT=hT[:, k, :], rhs=WKb[:, k, 512:576], start=(k == 0), stop=(k == KC - 1)),
                                 reads=[rh, rWKb], writes=[RPS[kB]])


                def k_C1a(p):
                        sb_ = p % 2
                        kA, kB = (0, 1) if p % 2 == 0 else (4, 5)
                        if p not in RP:
                            RP[p] = tuple(S.res() for _ in range(6))
                        rCKVNp, rCKVNTp, rIKTp, rKBTp, rVBp, rRKp = RP[p]
                        st = SK_[sb_]; rs = rSK[sb_]
                        S.op("scalar", L("activation", out=JK[:], in_=PS[kA][:, 0:256], func=AF.Square, accum_out=st[:, 0:1]),
                             reads=[RPS[kA]], writes=[rs])
                        S.op("vector", L("tensor_scalar", out=st[:, 1:2], in0=st[:, 0:1], scalar1=1.0 / 256, scalar2=EPS, op0=ALU.mult, op1=ALU.add),
                             reads=[rs], writes=[rs])
                        pow_rstd(st[:, 2:3], st[:, 1:2], [rs], [rs], 1)
                        S.op("vector", L("tensor_scalar", out=CKVN[:, p, :], in0=PS[kA][:, 0:256], scalar1=st[:, 2:3], scalar2=None, op0=ALU.mult),
                             reads=[RPS[kA], rs], writes=[rCKVNp])
                        for rc in range(2):
                            S.op("tensor", L("transpose", out=PSB6[:, rc * 128:(rc + 1) * 128], in_=CKVN[:, p, rc * 128:(rc + 1) * 128], identity=IDENT),
                                 reads=[rCKVNp, Rc], writes=[RPS[6]])

                def k_C1b(p):
                        sb_ = p % 2
                        kA, kB = (0, 1) if p % 2 == 0 else (4, 5)
                        if p not in RP:
                            RP[p] = tuple(S.res() for _ in range(6))
                        rCKVNp, rCKVNTp, rIKTp, rKBTp, rVBp, rRKp = RP[p]
                        st = SK_[sb_]; rs = rSK[sb_]
                        S.op("scalar", L("copy", out=CKVNT[:, :, p * 128:(p + 1) * 128], in_=PSB6[:, 0:256].rearrange("q (c t) -> q c t", c=2)),
                             reads=[RPS[6]], writes=[rCKVNTp])
                        for n, (c0, cw) in enumerate(((0, 512), (512, 256))):
                            for rc in range(2):
                                S.op("tensor", L("matmul", PS[2 + n][:, 0:cw], lhsT=CKVNT[:, rc, p * 128:(p + 1) * 128],
                                                                                               rhs=WKUK[:, rc, c0:c0 + cw], start=(rc == 0), stop=(rc == 1)),
                                     reads=[rCKVNTp, rWKU], writes=[RPS[2 + n]])


                def k_C2(p):
                        sb_ = p % 2
                        kA, kB = (0, 1) if p % 2 == 0 else (4, 5)
                        if p not in RP:
                            RP[p] = tuple(S.res() for _ in range(6))
                        rCKVNp, rCKVNTp, rIKTp, rKBTp, rVBp, rRKp = RP[p]
                        st = SK_[sb_]; rs = rSK[sb_]
                        S.op("vector", L("tensor_scalar", out=JK[:, 0:64], in0=PS[kB][:, 0:64], scalar1=1.0, scalar2=None, op0=ALU.mult, op1=ALU.add,
                                                                        accum_out=st[:, 16:17]), reads=[RPS[kB]], writes=[rs])
                        S.op("scalar", L("activation", out=JK[:, 0:64], in_=PS[kB][:, 0:64], func=AF.Square, accum_out=st[:, 17:18]),
                             reads=[RPS[kB]], writes=[rs])
                        S.op("vector", L("tensor_scalar", out=st[:, 18:20], in0=st[:, 16:18], scalar1=1.0 / 64, scalar2=None, op0=ALU.mult),
                             reads=[rs], writes=[rs])
                        S.op("vector", L("tensor_tensor", out=st[:, 20:21], in0=st[:, 18:19], in1=st[:, 18:19], op=ALU.mult), reads=[rs], writes=[rs])
                        S.op("vector", L("scalar_tensor_tensor", out=st[:, 21:22], in0=st[:, 19:20], scalar=EPS, in1=st[:, 20:21], op0=ALU.add,
                                                                               op1=ALU.subtract), reads=[rs], writes=[rs])
                        pow_rstd(st[:, 22:23], st[:, 21:22], [rs], [rs], 1)
                        S.op("vector", L("tensor_scalar", out=IK32[:], in0=PS[kB][:, 0:64], scalar1=st[:, 18:19], scalar2=st[:, 22:23],
                                                                        op0=ALU.subtract, op1=ALU.mult), reads=[RPS[kB], rs], writes=[rIK32])
                        S.op("vector", L("tensor_tensor", out=IK32[:], in0=IK32[:], in1=LNG[:], op=ALU.mult), reads=[rIK32, Rc], writes=[rIK32])
                        S.op("vector", L("tensor_tensor", out=IKD[:, 0:64], in0=IK32[:], in1=LNB[:], op=ALU.add), reads=[rIK32, Rc], writes=[rIKD])
                        S.op("vector", L("tensor_tensor", out=IKD[:, 64:128], in0=IK32[:], in1=LNB[:], op=ALU.add), reads=[rIK32, Rc], writes=[rIKD])
                        S.op("tensor", L("transpose", out=PSB6[:, 256:384], in_=IKD[:], identity=IDENT), reads=[rIKD, Rc], writes=[RPS[6]])
                        S.op("scalar", L("copy", out=IKT[:, p * 128:(p + 1) * 128], in_=PSB6[:, 256:384]), reads=[RPS[6]], writes=[rIKTp])
                        if p % 4 >= 2:
                            slot = (p // 4) * 2 + (p % 4 - 2)
                            for j in range(2):
                                S.op("scalar", L("activation", out=JK[:, 0:64], in_=PS[kA][:, 256 + j * 64:320 + j * 64], func=AF.Square,
                                    accum_out=st[:, 24 + j:25 + j]), reads=[RPS[kA]], writes=[rs])
                            S.op("vector", L("tensor_scalar", out=st[:, 26:28], in0=st[:, 24:26], scalar1=1.0 / 64, scalar2=EPS, op0=ALU.mult, op1=ALU.add),
                                 reads=[rs], writes=[rs])
                            pow_rstd(st[:, 28:30], st[:, 26:28], [rs], [rs], 2)
                            for j in range(2):
                                S.op("vector", L("tensor_scalar", out=KBD[:, j * 64:(j + 1) * 64], in0=PS[kA][:, 256 + j * 64:320 + j * 64],
                                                                                     scalar1=st[:, 28 + j:29 + j], scalar2=None, op0=ALU.mult),
                                     reads=[RPS[kA], rs], writes=[rKBD])
                            S.op("tensor", L("transpose", out=PSB6[:, 384:512], in_=KBD[:], identity=IDENT), reads=[rKBD, Rc], writes=[RPS[6]])
                            S.op("scalar", L("copy", out=KBT[:, slot, :], in_=PSB6[:, 384:512]), reads=[RPS[6]], writes=[rKBTp])
                            S.op("scalar", L("copy", out=VB[:, slot, :], in_=PS[kA][:, 384:512]), reads=[RPS[kA]], writes=[rVBp])

                def k_T(p):
                        sb_ = p % 2
                        kA, kB = (0, 1) if p % 2 == 0 else (4, 5)
                        if p not in RP:
                            RP[p] = tuple(S.res() for _ in range(6))
                        rCKVNp, rCKVNTp, rIKTp, rKBTp, rVBp, rRKp = RP[p]
                        st = SK_[sb_]; rs = rSK[sb_]
                        for h in range(6):
                            bank = 2 + (h // 4); off = (h % 4) * 128
                            S.op("scalar", L("activation", out=JK[:, 0:128], in_=PS[bank][:, off:off + 128], func=AF.Square,
                                accum_out=st[:, 4 + h:5 + h]), reads=[RPS[bank]], writes=[rs])
                        S.op("vector", L("tensor_scalar", out=st[:, 10:16], in0=st[:, 4:10], scalar1=128.0 * EPS, scalar2=None, op0=ALU.add),
                             reads=[rs], writes=[rs])
                        S.op("gpsimd", L("tensor_tensor", out=RK[:, p, :], in0=st[:, 10:16], in1=NEGH[:, 0:6], op=ALU.pow),
                             reads=[rs, Rc], writes=[rRKp])


                RP = {}
                k_A(0); k_B(0); k_A(1)
                for p in range(NB):
                    if p >= 1:
                        k_T(p - 1)
                    k_C1a(p)
                    if p + 1 < NB:
                        k_B(p + 1)
                    k_C2(p)
                    k_C1b(p)
                    if p + 2 < NB:
                        k_A(p + 2)
                k_T(NB - 1)
                if debug:
                    for q_ in range(2):
                        out_ops.append(S.dma("sync", DBG[:, 8576 + 32 * q_:8608 + 32 * q_], SK_[q_][:], reads=[rSK[q_]], writes=[S.res()]))

        S.barrier()
        if "A" in phases:
            with contextlib.ExitStack() as es:
                SC = SB(es, "SC", [128, SEQ], F32); rSC = S.res()
                CM = SB(es, "CM", [128, 128], F32); PADM = SB(es, "PADM", [128, 384], F32); BPV = SB(es, "BPV", [128, 1], F32)
                EBA = SB(es, "EBA", [128, 2, 6, 128], BF16); EBB = SB(es, "EBB", [128, 2, 2, 6, 128], BF16); ESK = SB(es, "ESK", [128, 6], F32)
                S.dma("sync", CM[:], cmaskd[:, :], writes=[Rc]); S.dma("sync", PADM[:], padmd[:, :], writes=[Rc]); S.dma("sync", BPV[:], bpvd[:, :], writes=[Rc])
                S.dma("sync", ESK[:], skd[:, :], writes=[Rc])
                S.op("scalar", L("activation", out=ESK[:], in_=ESK[:], func=AF.Exp), reads=[Rc], writes=[Rc])
                with contextlib.ExitStack() as es2:
                    tA = SB(es2, "tA", [128, 2, 6, 128], F32); tC = SB(es2, "tC", [128, 2, 6, 128], F32); tV = SB(es2, "tV", [128, 2, 128], F32)
                    tB = SB(es2, "tB", [128, 2, 2, 6, 128], F32); tW = SB(es2, "tW", [128, 2, 128], F32)
                    r1 = S.res()
                    S.dma("sync", tA[:], bad[:, :, :, :], writes=[r1]); S.dma("sync", tC[:], cad[:, :, :, :], writes=[r1])
                    S.dma("sync", tV[:], vad[:, :, :], writes=[r1]); S.dma("sync", tB[:], bbd[:, :, :, :, :], writes=[r1]); S.dma("sync", tW[:], vb01d[:, :, :], writes=[r1])
                    S.op("vector", L("tensor_tensor", out=tA[:], in0=tA[:], in1=tC[:], op=ALU.subtract), reads=[r1], writes=[r1])
                    S.op("scalar", L("activation", out=tA[:], in_=tA[:], func=AF.Exp), reads=[r1], writes=[r1])
                    S.op("scalar", L("activation", out=tB[:], in_=tB[:], func=AF.Exp), reads=[r1], writes=[r1])
                    for ty in range(2):
                        for h in range(6):
                            S.op("vector", L("tensor_tensor", out=EBA[:, ty, h, :], in0=tA[:, ty, h, :], in1=tV[:, ty, :], op=ALU.mult),
                                 reads=[r1], writes=[Rc])
                        for j in range(2):
                            for c in range(6):
                                S.op("vector", L("tensor_tensor", out=EBB[:, ty, j, c, :], in0=tB[:, ty, j, c, :], in1=tW[:, ty, :], op=ALU.mult),
                                     reads=[r1], writes=[Rc])
                S.barrier()
                IQT = SB(es, "IQT", [128, 8, 2, 128], BF16); rIQT = S.res()
                S.op("gpsimd", L("memset", IQT[:], 0.0), writes=[rIQT])
                IWT = SB(es, "IWT", [16, 128], BF16); rIWT = S.res()
                IWF = SB(es, "IWF", [128, 16], F32); rIWF = S.res()
                DG = SB(es, "DG", [128, 16, 128], BF16); rDG = S.res()
                QAT = SB(es, "QAT", [128, 6, 128], BF16); rQAT = S.res()
                QBT = SB(es, "QBT", [128, 6, 128], BF16); rQBT = S.res()
                QCT = SB(es, "QCT", [128, 4, 128], BF16); rQCT = S.res()
                GA = SB(es, "GA", [128, 6, 128], BF16); rGA = S.res()
                GB = SB(es, "GB", [128, 6, 128], BF16); rGB = S.res()
                GC = SB(es, "GC", [128, 4, 128], BF16); rGC = S.res()
                RL = [SB(es, f"RL{q}", [128, 512], BF16) for q in range(4)]; rRL = [S.res() for _ in range(4)]
                QT = [SB(es, f"QT{q}", [128, 2, 384], BF16) for q in range(2)]; rQT = [S.res() for _ in range(2)]
                PT = [SB(es, f"PT{q}", [128, 3, 128], BF16) for q in range(3)]; rPT = [S.res() for _ in range(3)]
                RD = SB(es, "RD", [128, 384], F32); rRD = S.res()
                DN = SB(es, "DN", [128, 384], F32); rDN = S.res()
                PV32 = SB(es, "PV32", [128, 384], F32); rPV32 = S.res()
                ON = SB(es, "ON", [128, 2, 384], BF16); rON = S.res()
                OAG = [SB(es, f"OAG{q}", [128, 3, 128], BF16) for q in range(2)]; rOAG = [S.res() for _ in range(2)]
                THR = SB(es, "THR", [128, 4], F32); rTHR = S.res()
                J1 = SB(es, "J1", [128, 2], BF16); rJ1 = S.res()
                PTB = SB(es, "PTB", [128, 2, 6, 128], BF16); rPTB = S.res()
                VBP = SB(es, "VBP", [128, 2, 2, 128], BF16); rVBP = S.res()
                RDB = SB(es, "RDB", [128, 768], F32); rRDB = S.res()
                OBG = SB(es, "OBG", [128, 6, 128], BF16); rOBG = S.res()
                PTC = PTB[:, :, 0:4, :]; rPTC = rPTB
                RDC = RDB[:, 0:512]; rRDC = rRDB
                OCG = OBG[:, 0:4, :]; rOCG = rOBG
                S.op("gpsimd", L("memset", VBP[:], 0.0), writes=[rVBP])
                qs_res = lambda c0, n, i: [rQS[(c0 + c, i // 4)] for c in range(n)] if rQS else []

                def qload(dst, rdst, c0, n, i):
                    S.dma("sync", dst[:], QS[c0:c0 + n, i].rearrange("c p t -> p c t"), reads=qs_res(c0, n, i), writes=[rdst])

                F8 = mybir.dt.float8e4
                MBF = SB(es, "MBF", [128, SEQ], F8); rMBF = S.res()
                PSB6 = PS[6][:].bitcast(BF16)
                DVE_HEADS = (1, 3, 5, 7, 9, 11, 13, 15)

                def gen_idx(i):
                    nsb = i + 1
                    S_i = 512 * nsb
                    for half in range(2):
                        S.dma("sync", IQT[half * 64:(half + 1) * 64, :, half, :], QS[C_IQ:C_IQ + 8, i, half * 64:(half + 1) * 64, :].rearrange("c p t -> p c t"),
                              reads=qs_res(C_IQ, 8, i), writes=[rIQT])
                    S.dma("sync", IWT[:], IWS[i], reads=[rIWS[i // 4]] if rIWS else [], writes=[rIWT])
                    S.op("tensor", L("transpose", out=PSB6[:, 0:16], in_=IWT[:], identity=IDENT[0:16, 0:16]), reads=[rIWT, Rc], writes=[RPS[6]])
                    S.op("scalar", L("copy", out=IWF[:], in_=PSB6[:, 0:16]), reads=[RPS[6]], writes=[rIWF])
                    for h in range(16):
                        S.op("vector", L("tensor_scalar", out=DG[:, h, :], in0=IDENT, scalar1=IWF[:, h:h + 1], scalar2=None, op0=ALU.mult),
                             reads=[rIWF, Rc], writes=[rDG])
                    yield 1.0
                    seq = [(sb, h) for sb in range(nsb) for h in range(16)]
                    bcs = {}

                    def score_mm(idx):
                        sb, h = seq[idx]
                        bc = (5, 6, 0, 1, 2, 3)[nxt("psc", 6)]
                        bcs[idx] = bc
                        S.op("tensor", L("matmul", PS[bc][:, :], lhsT=IQT[:, h // 2, h % 2, :], rhs=IKT[:, sb * 512:(sb + 1) * 512], start=True, stop=True),
                             reads=[rIQT, rIKT], writes=[RPS[bc]])

                    score_mm(0)
                    score_mm(1)
                    score_mm(2)
                    bS = 7
                    for idx, (sb, h) in enumerate(seq):
                        if idx + 3 < len(seq):
                            score_mm(idx + 3)
                        bc = bcs[idx]
                        rl = nxt("rl", 4)
                        if h == 0:
                            bS = (7, 4)[nxt("psS", 2)]
                        if h in DVE_HEADS:
                            S.op("vector", L("tensor_scalar", out=RL[rl][:], in0=PS[bc][:, :], scalar1=0.0, scalar2=None, op0=ALU.max), reads=[RPS[bc]], writes=[rRL[rl]])
                        else:
                            S.op("scalar", L("activation", out=RL[rl][:], in_=PS[bc][:, :], func=AF.Relu), reads=[RPS[bc]], writes=[rRL[rl]])
                        yield 0.25
                        S.op("tensor", L("matmul", PS[bS][:, :], lhsT=DG[:, h, :], rhs=RL[rl][:], start=(h == 0), stop=(h == 15)),
                             reads=[rDG, rRL[rl]], writes=[RPS[bS]])
                        if h == 15:
                            if sb == i:
                                pieces = [(0, 384, PADM if sb == 0 else None), (384, 512, CM)]
                            elif sb == 0:
                                pieces = [(0, 384, PADM), (384, 512, None)]
                            else:
                                pieces = [(0, 512, None)]
                            for (c0, c1, mk) in pieces:
                                if mk is None:
                                    S.op("scalar", L("copy", out=SC[:, sb * 512 + c0:sb * 512 + c1], in_=PS[bS][:, c0:c1]), reads=[RPS[bS]], writes=[rSC])
                                else:
                                    S.op("vector", L("tensor_tensor", out=SC[:, sb * 512 + c0:sb * 512 + c1], in0=PS[bS][:, c0:c1], in1=mk[:, 0:c1 - c0], op=ALU.add),
                                         reads=[RPS[bS], Rc], writes=[rSC])
                        yield 0.25
                    yield -1.0
                    S.op("vector", L("memset", THR[:], 0.0), writes=[rTHR])
                    for n in range(NIT):
                        step = LIM / 2.0 ** (n + 1)
                        S.op("vector", L("tensor_scalar", out=J1[:, 0:1].to_broadcast([128, S_i]), in0=SC[:, 0:S_i], scalar1=THR[:, 0:1], scalar2=None,
                                         op0=ALU.is_ge, op1=ALU.add, accum_out=THR[:, 1:2]), reads=[rSC, rTHR], writes=[rJ1, rTHR])
                        S.op("vector", L("tensor_scalar", out=THR[:, 2:3], in0=THR[:, 1:2], scalar1=255.5, scalar2=2.0 * step, op0=ALU.is_gt, op1=ALU.mult),
                             reads=[rTHR], writes=[rTHR])
                        S.op("vector", L("scalar_tensor_tensor", out=THR[:, 0:1], in0=THR[:, 2:3], scalar=-step, in1=THR[:, 0:1], op0=ALU.add, op1=ALU.add),
                             reads=[rTHR], writes=[rTHR])
                        yield 0.5 + 8.6 * S_i / 8192.0
                    S.op("vector", L("tensor_scalar", out=THR[:, 3:4], in0=THR[:, 0:1], scalar1=-LIM / 2.0 ** NIT, scalar2=None, op0=ALU.add),
                         reads=[rTHR], writes=[rTHR])
                    for sb in range(nsb):
                        S.op("vector", L("tensor_scalar", out=MBF[:, sb * 512:(sb + 1) * 512], in0=SC[:, sb * 512:(sb + 1) * 512], scalar1=THR[:, 3:4], scalar2=1.0,
                                         op0=ALU.is_ge, op1=ALU.subtract, saturate=False), reads=[rSC, rTHR], writes=[rMBF])
                        yield 0.4

                def gen_att(i):
                    nsb = i + 1
                    nkb = 4 * nsb
                    qload(QAT, rQAT, C_AQ, 6, i); qload(GA, rGA, C_AG, 6, i)
                    for g in range(2):
                        qt = nxt("qt", 2)
                        for rc in range(2):
                            for hh in range(3):
                                h = 3 * g + hh
                                S.op("tensor", L("matmul", PS[rc][:, hh * 128:(hh + 1) * 128], lhsT=WKT[:, h, rc * 128:(rc + 1) * 128],
                                                 rhs=QAT[:, h, :], start=True, stop=True), reads=[rQAT, rKC], writes=[RPS[rc]])
                            S.op("scalar", L("copy", out=QT[qt][:, rc, :], in_=PS[rc][:, 0:384]), reads=[RPS[rc]], writes=[rQT[qt]])
                        yield 1.0
                        bts = {}

                        def qk_mm(kbi):
                            bt = nxt("pst", 2)
                            bts[kbi] = bt
                            for rc in range(2):
                                S.op("tensor", L("matmul", PS[bt][:, 0:384], lhsT=CKVNT[:, rc, kbi * 128:(kbi + 1) * 128], rhs=QT[qt][:, rc, :],
                                                 start=(rc == 0), stop=False), reads=[rCKVNT, rQT[qt]], writes=[RPS[bt]])
                            S.op("tensor", L("matmul", PS[bt][:, 0:384], lhsT=MBF[:, kbi * 128:(kbi + 1) * 128], rhs=IREP, start=False, stop=True),
                                 reads=[rMBF, Rc], writes=[RPS[bt]])

                        qk_mm(0)
                        for kbi in range(nkb):
                            if kbi + 1 < nkb:
                                qk_mm(kbi + 1)
                            bt = bts[kbi]
                            pt = nxt("pt", 3)
                            for hh in range(3):
                                S.op("scalar", L("activation", out=PT[pt][:, hh, :], in_=PS[bt][:, hh * 128:(hh + 1) * 128], func=AF.Exp,
                                                 scale=RK[:, kbi, 3 * g + hh:3 * g + hh + 1]), reads=[RPS[bt], rRK], writes=[rPT[pt]])
                            if kbi >= 4 * i + 2:
                                ty = 0 if kbi == 4 * i + 3 else 1
                                S.op("vector", L("tensor_tensor", out=PT[pt][:], in0=PT[pt][:], in1=EBA[:, ty, 3 * g:3 * g + 3, :], op=ALU.mult),
                                     reads=[rPT[pt], Rc], writes=[rPT[pt]])
                            yield 0.55
                            ptf = PT[pt][:].rearrange("p a b -> p (a b)")
                            for rc in range(2):
                                S.op("tensor", L("matmul", PS[2 + rc][:, 0:384], lhsT=CKVN[:, kbi, rc * 128:(rc + 1) * 128], rhs=ptf,
                                                 start=(kbi == 0), stop=(kbi == nkb - 1)), reads=[rCKVN, rPT[pt]], writes=[RPS[2 + rc]])
                            S.op("tensor", L("matmul", PS[4][:, 0:384], lhsT=ONES, rhs=ptf, start=(kbi == 0), stop=(kbi == nkb - 1)),
                                 reads=[Rc, rPT[pt]], writes=[RPS[4]])
                            yield 0.55
                        for rc in range(2):
                            S.op("scalar", L("copy", out=ON[:, rc, :], in_=PS[2 + rc][:, 0:384]), reads=[RPS[2 + rc]], writes=[rON])
                        S.op("scalar", L("copy", out=DN[:], in_=PS[4][:, 0:384]), reads=[RPS[4]], writes=[rDN])
                        yield 1.0
                        for hh in range(3):
                            h = 3 * g + hh
                            for rc in range(2):
                                S.op("tensor", L("matmul", PS[0][:, hh * 128:(hh + 1) * 128], lhsT=WV[:, rc, h * 128:(h + 1) * 128],
                                                 rhs=ON[:, rc, hh * 128:(hh + 1) * 128], start=(rc == 0), stop=(rc == 1)),
                                     reads=[rON, rKC], writes=[RPS[0]])
                        S.op("scalar", L("copy", out=PV32[:], in_=PS[0][:, 0:384]), reads=[RPS[0]], writes=[rPV32])
                        og = nxt("oag", 2)
                        S.op("vector", L("reciprocal", out=RD[:], in_=DN[:]), reads=[rDN], writes=[rRD])
                        S.op("vector", L("tensor_tensor", out=PV32[:], in0=PV32[:], in1=RD[:], op=ALU.mult), reads=[rPV32, rRD], writes=[rPV32])
                        S.op("vector", L("tensor_tensor", out=OAG[og][:].rearrange("p a b -> p (a b)"), in0=PV32[:],
                                         in1=GA[:, 3 * g:3 * g + 3, :].rearrange("p a b -> p (a b)"), op=ALU.mult),
                             reads=[rPV32, rGA], writes=[rOAG[og]])
                        rOS[(g, i)] = S.res()
                        S.dma("gpsimd", OS[3 * g:3 * g + 3, i].rearrange("c p t -> p c t"), OAG[og][:], reads=[rOAG[og]], writes=[rOS[(g, i)]])
                        yield 1.0

                def gen_bc(i):
                    qload(QBT, rQBT, C_BQ, 6, i); qload(GB, rGB, C_BG, 6, i)
                    qload(QCT, rQCT, C_CQ, 4, i); qload(GC, rGC, C_CG, 4, i)
                    for blk in range(2):
                        slot = 2 * i + blk
                        S.op("gpsimd", L("tensor_copy", out=VBP[:, blk, 0, 0:64], in_=VB[:, slot, 0:64]), reads=[rVB], writes=[rVBP])
                        S.op("gpsimd", L("tensor_copy", out=VBP[:, blk, 1, 64:128], in_=VB[:, slot, 64:128]), reads=[rVB], writes=[rVBP])
                    yield 0.5
                    for j in range(2):
                        r0, r1_ = j * 64, (j + 1) * 64
                        for hf in range(2):
                            c0 = 3 * hf
                            for blk in range(2):
                                slot = 2 * i + blk
                                S.op("tensor", L("matmul", PS[5][:, 0:384], lhsT=KBT[r0:r1_, slot, :], rhs=QBT[r0:r1_, c0:c0 + 3, :].rearrange("p a b -> p (a b)"),
                                                 start=True, stop=True), reads=[rKBT, rQBT], writes=[RPS[5]])
                                S.op("scalar", L("activation", out=PTB[:, blk, c0:c0 + 3, :].rearrange("p a b -> p (a b)"), in_=PS[5][:, 0:384], func=AF.Exp, scale=0.125),
                                     reads=[RPS[5]], writes=[rPTB])
                                if blk == 0 and i == 0:
                                    S.op("vector", L("scalar_tensor_tensor", out=PTB[:, blk, c0:c0 + 3, :], in0=PTB[:, blk, c0:c0 + 3, :], scalar=BPV[:, 0:1],
                                                     in1=EBB[:, blk, j, c0:c0 + 3, :], op0=ALU.mult, op1=ALU.mult), reads=[rPTB, Rc], writes=[rPTB])
                                else:
                                    S.op("vector", L("tensor_tensor", out=PTB[:, blk, c0:c0 + 3, :], in0=PTB[:, blk, c0:c0 + 3, :], in1=EBB[:, blk, j, c0:c0 + 3, :], op=ALU.mult),
                                         reads=[rPTB, Rc], writes=[rPTB])
                                yield 0.7
                            for (bank, lh_kind) in ((6, "v"), (7, "o")):
                                for blk in range(2):
                                    lh = VBP[:, blk, j, :] if lh_kind == "v" else ONESH[:, j * 128:(j + 1) * 128]
                                    S.op("tensor", L("matmul", PS[bank][:, 0:384], lhsT=lh, rhs=PTB[:, blk, c0:c0 + 3, :].rearrange("p a b -> p (a b)"),
                                                     start=(blk == 0), stop=(blk == 1)), reads=[rVBP, rPTB, Rc], writes=[RPS[bank]])
                            for cc in range(3):
                                c = c0 + cc
                                S.op("vector", L("tensor_scalar", out=RDB[r0:r1_, c * 128:(c + 1) * 128], in0=PS[7][r0:r1_, cc * 128:(cc + 1) * 128],
                                                 scalar1=ESK[r0:r1_, c:c + 1], scalar2=None, op0=ALU.add), reads=[RPS[7], Rc], writes=[rRDB])
                            S.op("vector", L("reciprocal", out=RDB[r0:r1_, c0 * 128:(c0 + 3) * 128], in_=RDB[r0:r1_, c0 * 128:(c0 + 3) * 128]), reads=[rRDB], writes=[rRDB])
                            S.op("vector", L("tensor_tensor", out=RDB[r0:r1_, c0 * 128:(c0 + 3) * 128], in0=PS[6][r0:r1_, 0:384], in1=RDB[r0:r1_, c0 * 128:(c0 + 3) * 128], op=ALU.mult),
                                 reads=[RPS[6], rRDB], writes=[rRDB])
                            S.op("vector", L("tensor_tensor", out=OBG[r0:r1_, c0:c0 + 3, :].rearrange("p a b -> p (a b)"), in0=RDB[r0:r1_, c0 * 128:(c0 + 3) * 128],
                                             in1=GB[r0:r1_, c0:c0 + 3, :].rearrange("p a b -> p (a b)"), op=ALU.mult), reads=[rRDB, rGB], writes=[rOBG])
                            yield 1.0
                    rOS[(2, i)] = S.res()
                    S.dma("gpsimd", OS[6:12, i].rearrange("c p t -> p c t"), OBG[:], reads=[rOBG], writes=[rOS[(2, i)]])
                    for mbk in range(2):
                        for h in range(4):
                            S.op("tensor", L("matmul", PS[5][:, h * 128:(h + 1) * 128], lhsT=KCT[:, h, mbk * 128:(mbk + 1) * 128], rhs=QCT[:, h, :],
                                             start=True, stop=True), reads=[rKC, rQCT], writes=[RPS[5]])
                        S.op("scalar", L("activation", out=PTC[:, mbk, :, :].rearrange("p a b -> p (a b)"), in_=PS[5][:, :], func=AF.Exp, scale=128.0 ** -0.5),
                             reads=[RPS[5]], writes=[rPTC])
                        yield 0.8
                    for h in range(4):
                        for mbk in range(2):
                            S.op("tensor", L("matmul", PS[6][:, h * 128:(h + 1) * 128], lhsT=VC[:, mbk, h * 128:(h + 1) * 128], rhs=PTC[:, mbk, h, :],
                                             start=(mbk == 0), stop=(mbk == 1)), reads=[rKC, rPTC], writes=[RPS[6]])
                    for mbk in range(2):
                        S.op("tensor", L("matmul", PS[7][:, :], lhsT=ONES, rhs=PTC[:, mbk, :, :].rearrange("p a b -> p (a b)"), start=(mbk == 0), stop=(mbk == 1)),
                             reads=[Rc, rPTC], writes=[RPS[7]])
                    S.op("vector", L("reciprocal", out=RDC[:], in_=PS[7][:, :]), reads=[RPS[7]], writes=[rRDC])
                    S.op("vector", L("tensor_tensor", out=RDC[:], in0=PS[6][:, :], in1=RDC[:], op=ALU.mult), reads=[RPS[6], rRDC], writes=[rRDC])
                    S.op("vector", L("tensor_tensor", out=OCG[:].rearrange("p a b -> p (a b)"), in0=RDC[:], in1=GC[:].rearrange("p a b -> p (a b)"), op=ALU.mult),
                         reads=[rRDC, rGC], writes=[rOCG])
                    rOS[(3, i)] = S.res()
                    S.dma("gpsimd", OS[12:16, i].rearrange("c p t -> p c t"), OCG[:], reads=[rOCG], writes=[rOS[(3, i)]])
                    yield 1.0

                def total_idx(i):
                    nsb = i + 1
                    return 1.0 + 0.5 * 16 * nsb + NIT * (0.5 + 8.6 * 512 * nsb / 8192.0) + 0.4 * nsb

                def total_att(i):
                    nkb = 4 * (i + 1)
                    return 2 * (1.0 + 1.1 * nkb + 2.0) + 2 * (2.0 + 2.0) + 3.0

                def drive(tasks):
                    prog = [0.0 for _ in tasks]
                    alive = [True for _ in tasks]
                    while any(alive):
                        k = min((q for q in range(len(tasks)) if alive[q]), key=lambda q: prog[q] / tasks[q][1])
                        try:
                            prog[k] += next(tasks[k][0])
                        except StopIteration:
                            alive[k] = False

                def run_scores(gi):
                    for w in gi:
                        if w < 0:
                            break

                def total_rest(i):
                    nsb = i + 1
                    return NIT * (0.5 + 8.6 * 512 * nsb / 8192.0) + 0.4 * nsb

                g0 = gen_idx(0)
                run_scores(g0)
                drive([(g0, total_rest(0))])
                for i in range(NQ):
                    tasks = [(gen_att(i), total_att(i)), (gen_bc(i), 14.0)]
                    if i + 1 < NQ:
                        gi = gen_idx(i + 1)
                        run_scores(gi)
                        tasks.append((gi, total_rest(i + 1)))
                    drive(tasks)
            S.barrier()
        mid.close()
        if "M" in phases:
            with contextlib.ExitStack() as es:
                WUP = SB(es, "WUP", [128, 16, D], BF16); rWUP = S.res()
                WO = SB(es, "WO", [128, KC, D], BF16); rWO = S.res()
                with contextlib.ExitStack() as es2:
                    stage = mk_stage(es2)
                    load_cast_w(stage, WUP[:, 0:6, :], wupd[0], D, None, rWUP, nk=6)
                    load_cast_w(stage, WUP[:, 6:12, :], wupd[1], D, None, rWUP, nk=6)
                    load_cast_w(stage, WUP[:, 12:16, :], wupd[2], D, None, rWUP, nk=4)
                    load_cast_w(stage, WO, wod, D, None, rWO)
                S.barrier()
                OG = SB(es, "OG", [128, 16, 512], BF16); rOG = S.res()
                GM = [SB(es, f"GM{q}", [128, 3, 512], BF16) for q in range(2)]; rGM = [S.res() for _ in range(2)]
                TT = [SB(es, f"TT{q}", [128, 3, 512], F32) for q in range(2)]; rTT = [S.res() for _ in range(2)]
                MT = SB(es, "MT", [128, 16, 512], BF16); rMT = S.res()
                XR = [SB(es, f"XR{q}", [128, D], F32) for q in range(2)]; rXR = [S.res() for _ in range(2)]
                nbr = (6, 6, 4); cbase = (0, 6, 12)
                for m in range(4):
                    for c in range(16):
                        grp = (0 if c < 3 else 1) if c < 6 else (2 if c < 12 else 3)
                        S.dma("sync", OG[:, c, :].rearrange("p (q t) -> p q t", q=4), OS[c, 4 * m:4 * m + 4].rearrange("q p t -> p q t"),
                              reads=[rOS[(grp, 4 * m + q)] for q in range(4)] if rOS else [], writes=[rOG])
                    for f in range(16):
                        gb = nxt("gm", 2)
                        for br in range(3):
                            ch = C_MIX + br * 16 + f
                            S.dma("sync", GM[gb][:, br, :].rearrange("p (q t) -> p q t", q=4), QS[ch, 4 * m:4 * m + 4].rearrange("q p t -> p q t"),
                                  reads=[rQS[(ch, m)]] if rQS else [], writes=[rGM[gb]])
                        for br in range(3):
                            for cc in range(nbr[br]):
                                c = cbase[br] + cc
                                S.op("tensor", L("matmul", PS[br][:, :], lhsT=WUP[:, c, f * 128:(f + 1) * 128], rhs=OG[:, c, :],
                                                                                        start=(cc == 0), stop=(cc == nbr[br] - 1)), reads=[rWUP, rOG], writes=[RPS[br]])
                        tb = nxt("tt", 2)
                        for br in range(3):
                            S.op("vector", L("tensor_tensor", out=TT[tb][:, br, :], in0=PS[br][:, :], in1=GM[gb][:, br, :], op=ALU.mult),
                                 reads=[RPS[br], rGM[gb]], writes=[rTT[tb]])
                        S.op("gpsimd", L("tensor_tensor", out=TT[tb][:, 0, :], in0=TT[tb][:, 0, :], in1=TT[tb][:, 1, :], op=ALU.add), reads=[rTT[tb]], writes=[rTT[tb]])
                        S.op("gpsimd", L("tensor_tensor", out=MT[:, f, :], in0=TT[tb][:, 0, :], in1=TT[tb][:, 2, :], op=ALU.add), reads=[rTT[tb]], writes=[rMT])
                    for q in range(4):
                        i = 4 * m + q
                        xb = nxt("xr", 2)
                        S.dma("sync", XR[xb][:], xk[(4 * i + 3) * 128:(4 * i + 4) * 128, :], writes=[rXR[xb]])
                        for n in range(4):
                            pb = 3 + nxt("py", 4)
                            for f in range(16):
                                S.op("tensor", L("matmul", PS[pb][:, :], lhsT=MT[:, f, q * 128:(q + 1) * 128], rhs=WO[:, f, n * 512:(n + 1) * 512],
                                                                                      start=(f == 0), stop=(f == 15)), reads=[rMT, rWO], writes=[RPS[pb]])
                            S.op("vector", L("tensor_tensor", out=XR[xb][:, n * 512:(n + 1) * 512], in0=PS[pb][:, :], in1=XR[xb][:, n * 512:(n + 1) * 512], op=ALU.add),
                                 reads=[RPS[pb], rXR[xb]], writes=[rXR[xb]])
                        S.dma("gpsimd", outd[i * 128:(i + 1) * 128, :], XR[xb][:], reads=[rXR[xb]], writes=[S.res()])
            S.barrier()
        if debug and "K" in phases:
            dl = [("CKVNT", CKVNT[:, 0, 0:2048], 0, 2048, rCKVNT), ("CKVN", CKVN[:, 5, :], 2048, 256, rCKVN),
                  ("IKT", IKT[:, 0:2048], 2304, 2048, rIKT), ("KBT", KBT[:, 3, :], 4352, 128, rKBT), ("VB", VB[:, 3, :], 4480, 128, rVB),
                  ("RK", RK[:].rearrange("p a b -> p (a b)"), 4608, 384, rRK), ("KCT", KCT[:].rearrange("p a b -> p (a b)"), 4992, 1024, rKC),
                  ("VC", VC[:].rearrange("p a b -> p (a b)"), 6016, 1024, rKC), ("WKT", WKT[:].rearrange("p a b -> p (a b)"), 7040, 1536, rKC)]
            with contextlib.ExitStack() as es:
                dt_ = SB(es, "dbgt", [128, 2048], F32); rd = S.res()
                for nm, ap, off, n, rr in dl:
                    S.op("vector", L("tensor_copy", out=dt_[:, 0:n], in_=ap), reads=[rr], writes=[rd])
                    out_ops.append(S.dma("sync", DBG[:, off:off + n], dt_[:, 0:n], reads=[rd], writes=[S.res()]))
        for e_ in ("gpsimd",):
            out_ops += [o for o in S.ops[e_] if o.dma]
        S.wait_all("sync", out_ops)
        S.emit()
    return nc


_NC_CACHE = {}


def kernel(**inputs):
    sh, per = prep_inputs(inputs)
    if "nc" not in _NC_CACHE:
        _NC_CACHE["nc"] = build(debug=False)
    nc = _NC_CACHE["nc"]
    res = run_bass_kernel_spmd(nc, [dict(sh, **per[c]) for c in range(8)], core_ids=list(range(8)))
    out = np.zeros((2, SEQ, D), np.float32)
    for c in range(8):
        b, j = c // 4, c % 4
        o = np.asarray(res.results[c]["out"], dtype=np.float32).reshape(NQ, 128, D)
        out[b].reshape(NB, 128, D)[j::4] = o
    return out
```

```python
import numpy as np
import concourse.bass as bass
import concourse.mybir as mybir
from concourse.bass_utils import run_bass_kernel_spmd

F32 = mybir.dt.float32
BF16 = mybir.dt.bfloat16
AF = mybir.ActivationFunctionType
ALU = mybir.AluOpType
AX = mybir.AxisListType


class Res:
    __slots__ = ("name", "w", "r")

    def __init__(self, name):
        self.name = name
        self.w = None
        self.r = {}


class Op:
    __slots__ = ("eng", "fn", "deps", "dma", "sig", "sem", "val", "idx")

    def __init__(self, eng, fn, deps, dma):
        self.eng = eng
        self.fn = fn
        self.deps = deps
        self.dma = dma
        self.sig = False
        self.sem = None
        self.val = 0


def L(name, *a, **k):
    return lambda e: getattr(e, name)(*a, **k)


class Sched:
    ENGS = ("sync", "scalar", "vector", "gpsimd", "tensor")
    NPOOL = 12

    def __init__(self, nc):
        self.nc = nc
        self.ops = {e: [] for e in self.ENGS}
        self.nres = 0

    def res(self, name=None):
        self.nres += 1
        return Res(name or f"r{self.nres}")

    def _deps(self, eng, reads, writes):
        deps = []
        for r in list(reads) + list(writes):
            if r.w is not None:
                deps.append(r.w)
        for w in writes:
            deps.extend(w.r.values())
        out = []
        seen = set()
        for d in deps:
            if id(d) in seen:
                continue
            seen.add(id(d))
            if d.eng == eng and eng == "tensor" and not d.dma:
                continue
            out.append(d)
        return out

    def _record(self, op, reads, writes):
        self.ops[op.eng].append(op)
        for r in reads:
            r.r[(op.eng, id(op)) if op.dma else (op.eng,)] = op
        for w in writes:
            w.w = op
            w.r = {}
        return op

    def op(self, eng, fn, reads=(), writes=()):
        o = Op(eng, fn, self._deps(eng, reads, writes), False)
        return self._record(o, reads, writes)

    def dma(self, eng, out, in_, reads=(), writes=(), **kw):
        o = Op(eng, L("dma_start", out=out, in_=in_, **kw),
               self._deps(eng, reads, writes), True)
        return self._record(o, reads, writes)

    def wait_all(self, eng, ops):
        o = Op(eng, None, list(ops), False)
        self.ops[eng].append(o)
        return o

    def barrier(self):
        lasts = []
        for e in self.ENGS:
            ops = self.ops[e]
            comp = [o for o in ops if (not o.dma) and o.fn is not None]
            if comp:
                lasts.append(comp[-1])
            lasts += [o for o in ops if o.dma][-self.NPOOL:]
        for e in self.ENGS:
            self.wait_all(e, lasts)

    def emit(self):
        nc = self.nc
        for e in self.ENGS:
            dmas = [o for o in self.ops[e] if o.dma]
            for k, o in enumerate(dmas):
                o.sig = True
                if k >= self.NPOOL:
                    o.deps.append(dmas[k - self.NPOOL])
        for e in self.ENGS:
            for o in self.ops[e]:
                for d in o.deps:
                    d.sig = True
        import contextlib
        with contextlib.ExitStack() as st:
            esem = {e: st.enter_context(nc.semaphore(f"s_{e}")) for e in self.ENGS}
            pools = {e: [st.enter_context(nc.semaphore(f"d_{e}{i}")) for i in range(self.NPOOL)]
                     for e in self.ENGS if any(o.dma for o in self.ops[e])}
            for e in self.ENGS:
                c = 0
                k = 0
                for o in self.ops[e]:
                    if o.dma:
                        o.sem = pools[e][k % self.NPOOL]
                        o.val = 16 * (k // self.NPOOL + 1)
                        k += 1
                    elif o.sig:
                        c += 1
                        o.sem = esem[e]
                        o.val = c
            block = st.enter_context(nc.Block())

            def run(e_name):
                def body(e):
                    waited = {}
                    for o in self.ops[e_name]:
                        for d in o.deps:
                            key = id(d.sem)
                            if waited.get(key, 0) < d.val:
                                e.wait_ge(d.sem, d.val)
                                waited[key] = d.val
                        if o.fn is None:
                            continue
                        inst = o.fn(e)
                        if o.sig:
                            inst.then_inc(o.sem, 16 if o.dma else 1)
                return body

            for e_name in self.ENGS:
                if self.ops[e_name]:
                    getattr(block, e_name)(run(e_name))


D = 2048
SEQ = 8192
NB = 64
NQ = 16
KC = 16
EPS = 1e-6
NIT = 20
LIM = 16.0
NEG = -1.0e30
O_AQ, O_CKV, O_IQ, O_IK, O_IW, O_AG, O_BQ, O_BK, O_BV, O_BG, O_CQ, O_CG, O_MIX = (
    0, 768, 1024, 2048, 2112, 2128, 2896, 3664, 3792, 3920, 4688, 5200, 5712)
KINDS = (["aq"] * 6 + ["cq"] * 4 + ["bq"] * 6 + ["iq"] * 8 + ["silu"] * 16 + ["mix"] * 48)
C_AQ, C_CQ, C_BQ, C_IQ, C_AG, C_BG, C_CG, C_MIX = 0, 6, 10, 16, 24, 30, 36, 40
NCH = 88


def _bucket(dist):
    n = np.maximum(dist, 0)
    nf = np.maximum(n, 1).astype(np.float32)
    large = 16 + (np.log(nf / np.float32(16)) / np.float32(np.log(128 / 16)) * np.float32(16)).astype(np.int32)
    large = np.minimum(large, 31)
    return np.where(n < 16, n, large)


def _bperm():
    idx = []
    for c in range(6):
        idx += list(range(c * 64, c * 64 + 64)) + list(range((6 + c) * 64, (6 + c) * 64 + 64))
    return np.array(idx)


def prep_inputs(inp):
    f = lambda a: np.ascontiguousarray(np.asarray(a, dtype=np.float32))
    x = f(inp["x"]); mem = f(inp["mem"]); w_in = f(inp["w_in"])[0]
    bp = _bperm()
    cols = np.concatenate([
        np.arange(O_AQ, O_AQ + 768), np.arange(O_CQ, O_CQ + 512), O_BQ + bp,
        np.arange(O_IQ, O_IQ + 1024), np.arange(O_AG, O_AG + 768), O_BG + bp,
        np.arange(O_CG, O_CG + 512), np.arange(O_MIX, O_MIX + 6144)])
    sh = {}
    sh["wq"] = f(w_in[:, cols])
    sh["wiw"] = f(w_in[:, O_IW:O_IW + 16])
    kcols = np.concatenate([np.arange(O_CKV, O_CKV + 256), np.arange(O_BK, O_BK + 128),
                            np.arange(O_BV, O_BV + 128), np.arange(O_IK, O_IK + 64)])
    sh["wk"] = f(w_in[:, kcols])
    pk = lambda v: f(np.asarray(v).reshape(-1, 128).T)
    sh["normg"] = pk(inp["norm_g"][0]); sh["memg"] = pk(inp["mem_norm_g"][0])
    sh["wmem"] = f(inp["w_mem_kv"][0])
    wkv = f(inp["w_kv_up"][0])
    sh["wkvu"] = wkv
    sh["wkT"] = f(wkv[:, :768].reshape(256, 6, 128).transpose(2, 1, 0))
    sh["kvg"] = pk(inp["kv_norm_g"][0])
    sh["gkvbc"] = f(np.broadcast_to(np.asarray(inp["kv_norm_g"][0])[None, :], (128, 256)))
    col = lambda v: f(np.asarray(v).reshape(128, 1))
    t2 = lambda v: np.concatenate([np.asarray(v), np.asarray(v)])
    sh["gv"] = f(np.concatenate([col(inp["q_norm_a"][0]), col(inp["k_norm_a"][0]),
                                 col(t2(inp["q_norm_b"][0])), col(t2(inp["k_norm_b"][0])),
                                 col(inp["q_norm_c"][0]), col(inp["k_norm_c"][0])], axis=1))
    sh["lng"] = f(np.broadcast_to(np.asarray(inp["idx_k_ln_g"][0])[None, :], (128, 64)))
    sh["lnb"] = f(np.broadcast_to(np.asarray(inp["idx_k_ln_b"][0])[None, :], (128, 64)))
    sh["gbias"] = pk(inp["gate_bias"][0])
    sk = np.asarray(inp["sinks_b"][0])
    sh["sk"] = f(np.concatenate([np.broadcast_to(sk[None, 0:6], (64, 6)),
                                 np.broadcast_to(sk[None, 6:12], (64, 6))], axis=0))
    rb = np.asarray(inp["rel_bias"], dtype=np.float32)
    s = np.arange(128)[:, None]; t = np.arange(128)[None, :]
    d0 = t - s; d1 = 128 + t - s
    bk = np.stack([_bucket(d0), _bucket(d1)], axis=1)
    sh["ba"] = f(rb[bk][:, :, :, :6].transpose(0, 1, 3, 2))
    sh["ca"] = f(np.broadcast_to(rb[31, :6][None, None, :, None], (128, 2, 6, 128)))
    sh["va"] = f(np.stack([(d0 >= 0), np.ones_like(d0, bool)], axis=1))
    bkb = np.stack([_bucket(d1), _bucket(d0)], axis=1)
    sh["bb"] = f(rb[bkb][:, :, :, 6:18].reshape(128, 2, 128, 2, 6).transpose(0, 1, 3, 4, 2))
    sh["vb01"] = f(np.stack([(d1 < 128), (d0 >= 0)], axis=1))
    sh["cmask"] = f(np.where(np.arange(128)[None, :] <= np.arange(128)[:, None], 0.0, NEG))
    ident = np.eye(128, dtype=np.float32)
    blk64 = np.kron(np.eye(2, dtype=np.float32), np.ones((64, 64), np.float32))
    onesh = np.zeros((128, 2, 128), np.float32); onesh[:, 0, :64] = 1; onesh[:, 1, 64:] = 1
    sh["cst"] = f(np.concatenate([ident, np.ones((128, 128), np.float32), blk64,
                                  onesh.reshape(128, 256), np.tile(ident * 32768.0, (1, 3))], axis=1))
    sh["wupa"] = f(inp["w_up_a"][0]); sh["wupb"] = f(inp["w_up_b"][0][bp]); sh["wupc"] = f(inp["w_up_c"][0])
    sh["wo"] = f(inp["w_o"][0])
    per = []
    for c in range(8):
        b, j = c // 4, c % 4
        npad = (3 - j) * 128
        xk = np.zeros((SEQ, D), np.float32)
        xk[npad:] = x[b, :SEQ - npad]
        padm = np.zeros((128, 384), np.float32); padm[:, :npad] = NEG
        bpv = np.full((128, 1), 0.0 if j == 0 else 1.0, np.float32)
        per.append({"xk": xk, "mem": f(mem[b]), "padm": padm, "bpv": bpv})
    return sh, per


def build(debug=False, phases="QKAM"):
    import contextlib
    nc = bass.Bass("TRN2", target_bir_lowering=False)
    dbg_kind = "ExternalOutput" if debug else "Internal"

    def din(name, shape, dt=F32):
        return nc.dram_tensor(name, list(shape), dt, kind="ExternalInput").ap()

    xk = din("xk", [SEQ, D]); memd = din("mem", [256, D])
    wq = din("wq", [D, NCH * 128]); wiw = din("wiw", [D, 16]); wk = din("wk", [D, 576])
    normg = din("normg", [128, 16]); memg = din("memg", [128, 16]); wmem = din("wmem", [D, 1024])
    wkvu = din("wkvu", [256, 1536]); wkTd = din("wkT", [128, 6, 256]); kvg = din("kvg", [128, 2])
    gkvbc = din("gkvbc", [128, 256]); gvd = din("gv", [128, 6]); lngd = din("lng", [128, 64]); lnbd = din("lnb", [128, 64])
    gbiasd = din("gbias", [128, 48]); skd = din("sk", [128, 6])
    bad = din("ba", [128, 2, 6, 128]); cad = din("ca", [128, 2, 6, 128]); vad = din("va", [128, 2, 128])
    bbd = din("bb", [128, 2, 2, 6, 128]); vb01d = din("vb01", [128, 2, 128]); cmaskd = din("cmask", [128, 128])
    cstd = din("cst", [128, 1024]); padmd = din("padm", [128, 384]); bpvd = din("bpv", [128, 1])
    wupd = [din("wupa", [768, D]), din("wupb", [768, D]), din("wupc", [512, D])]
    wod = din("wo", [D, D])
    outd = nc.dram_tensor("out", [NQ * 128, D], F32, kind="ExternalOutput").ap()
    QS = nc.dram_tensor("QS", [NCH, NQ, 128, 128], BF16, kind=dbg_kind).ap()
    IWS = nc.dram_tensor("IWS", [NQ, 16, 128], BF16, kind=dbg_kind).ap()
    OS = nc.dram_tensor("OS", [16, NQ, 128, 128], BF16, kind=dbg_kind).ap()
    DBG = nc.dram_tensor("DBG", [128, 16384], F32, kind=dbg_kind).ap() if debug else None

    S = Sched(nc)
    top = contextlib.ExitStack()
    with top:
        uid = [0]

        def SB(es, name, shape, dt):
            uid[0] += 1
            return es.enter_context(nc.sbuf_tensor(f"{name}_{uid[0]}", list(shape), dt))

        PS = [top.enter_context(nc.psum_tensor(f"ps{b}", [128, 512], F32)) for b in range(8)]
        RPS = [S.res(f"ps{b}") for b in range(8)]
        PSB = PS[7][:].bitcast(BF16)

        CST = SB(top, "CST", [128, 1024], BF16)
        NEGH = SB(top, "NEGH", [128, 8], F32)
        GV = SB(top, "GV", [128, 8], F32)
        KCT = SB(top, "KCT", [128, 4, 256], BF16)
        VC = SB(top, "VC", [128, 2, 512], BF16)
        Rc = S.res("const")
        IDENT = CST[:, 0:128]; ONES = CST[:, 128:256]; BLK64 = CST[:, 256:384]
        ONESH = CST[:, 384:640]; IREP = CST[:, 640:1024]
        with contextlib.ExitStack() as es:
            stg = SB(es, "cstg", [128, 1024], F32)
            r = S.res()
            S.dma("sync", stg[:], cstd[:, :], writes=[r])
            S.op("vector", L("tensor_copy", out=CST[:], in_=stg[:]), reads=[r], writes=[Rc])
            S.op("vector", L("memset", NEGH[:], -0.5), writes=[Rc])
            S.dma("sync", GV[:, 0:6], gvd[:, :], reads=[], writes=[Rc])
            S.op("vector", L("tensor_tensor", out=GV[:, 6:7], in0=GV[:, 2:3], in1=GV[:, 3:4], op=ALU.mult), reads=[Rc], writes=[Rc])
            S.op("vector", L("tensor_tensor", out=GV[:, 7:8], in0=GV[:, 4:5], in1=GV[:, 5:6], op=ALU.mult), reads=[Rc], writes=[Rc])

        rot = {}

        def nxt(key, n):
            rot[key] = (rot.get(key, -1) + 1) % n
            return rot[key]

        def pow_rstd(out_ap, in_ap, rres, wres, n):
            shp = in_ap.shape
            S.op("gpsimd", L("tensor_tensor", out=out_ap, in0=in_ap, in1=NEGH[:shp[0], 0:n], op=ALU.pow),
                 reads=rres + [Rc], writes=wres)

        class NT:
            def __init__(self, es):
                self.xs = [SB(es, f"nt_xs{i}", [128, D], F32) for i in range(2)]
                self.hb = [SB(es, f"nt_hb{i}", [128, D], BF16) for i in range(2)]
                self.st = [SB(es, f"nt_st{i}", [128, 4], F32) for i in range(2)]
                self.rx = [S.res() for _ in range(2)]; self.rh = [S.res() for _ in range(2)]
                self.rs = [S.res() for _ in range(2)]

            def run(self, src_rows, dst, rdst):
                b = nxt("nt", 2)
                xs, hb, st = self.xs[b], self.hb[b], self.st[b]
                rx, rh, rs = self.rx[b], self.rh[b], self.rs[b]
                S.dma("sync", xs[:], src_rows, writes=[rx])
                S.op("vector", L("scalar_tensor_tensor", out=hb[:], in0=xs[:], scalar=1.0, in1=xs[:], op0=ALU.mult,
                                                                op1=ALU.mult, accum_out=st[:, 0:1]), reads=[rx], writes=[rh, rs])
                S.op("vector", L("tensor_scalar", out=st[:, 1:2], in0=st[:, 0:1], scalar1=1.0 / D, scalar2=EPS,
                                                         op0=ALU.mult, op1=ALU.add), reads=[rs], writes=[rs])
                pow_rstd(st[:, 2:3], st[:, 1:2], [rs], [rs], 1)
                S.op("vector", L("tensor_scalar", out=hb[:], in0=xs[:], scalar1=st[:, 2:3], scalar2=None, op0=ALU.mult),
                     reads=[rx, rs], writes=[rh])
                for half in range(2):
                    for k in range(8):
                        kk = half * 8 + k
                        S.op("tensor", L("transpose", out=PSB[:, k * 128:(k + 1) * 128], in_=hb[:, kk * 128:(kk + 1) * 128],
                                                                         identity=IDENT), reads=[rh, Rc], writes=[RPS[7]])
                    S.op("scalar", L("copy", out=dst[:, half * 8:(half + 1) * 8, :],
                                                               in_=PSB.rearrange("p (k t) -> p k t", k=8)),
                         reads=[RPS[7]], writes=[rdst])

        def load_cast_w(es_stage, dst, src_dram_rows, ncols, gscale, rdst, nk=KC, chunkc=256, eng="vector"):
            stg = es_stage["stg"]; rstg = es_stage["rstg"]
            for c0 in range(0, ncols, chunkc):
                cw = min(chunkc, ncols - c0)
                b = nxt("wstg", 2)
                S.dma("sync", stg[b][:, 0:nk, 0:cw], src_dram_rows[:, c0:c0 + cw].rearrange("(k p) c -> p k c", p=128), writes=[rstg[b]])
                for k in range(nk):
                    if gscale is None:
                        S.op(eng, L("tensor_copy", out=dst[:, k, c0:c0 + cw], in_=stg[b][:, k, 0:cw]),
                             reads=[rstg[b]], writes=[rdst])
                    else:
                        S.op(eng, L("tensor_scalar", out=dst[:, k, c0:c0 + cw], in0=stg[b][:, k, 0:cw],
                                                                                   scalar1=gscale[:, k:k + 1], scalar2=None, op0=ALU.mult),
                             reads=[rstg[b], Rc], writes=[rdst])

        def mk_stage(es):
            return {"stg": [SB(es, f"wstg{i}", [128, KC, 256], F32) for i in range(2)], "rstg": [S.res() for _ in range(2)]}

        out_ops = []
        rKC = S.res()
        rQS = {}; rIWS = {}; rOS = {}

        if "Q" in phases:
            with contextlib.ExitStack() as es:
                HT = SB(es, "HT", [128, KC, NQ * 128], BF16); rHT = S.res()
                GN = SB(es, "GN", [128, 16], F32); MG = SB(es, "MG", [128, 16], F32); GBI = SB(es, "GBI", [128, 48], F32)
                S.dma("sync", GN[:], normg[:, :], writes=[Rc]); S.dma("sync", MG[:], memg[:, :], writes=[Rc])
                S.dma("sync", GBI[:], gbiasd[:, :], writes=[Rc])
                nt = NT(es)
                stage = mk_stage(es)
                SQ = [SB(es, f"SQ{i}", [128, 512], BF16) for i in range(2)]; rSQ = [S.res() for _ in range(2)]
                T1 = [SB(es, f"T1{i}", [128, 512], F32) for i in range(2)]; rT1 = [S.res() for _ in range(2)]
                T2 = [SB(es, f"T2{i}", [128, 512], F32) for i in range(2)]; rT2 = [S.res() for _ in range(2)]
                OT = [SB(es, f"OT{i}", [128, 512], BF16) for i in range(3)]; rOT = [S.res() for _ in range(3)]
                ST4 = SB(es, "ST4", [128, 16], F32); rST4 = S.res()
                with contextlib.ExitStack() as es2:
                    WM = SB(es2, "WM", [128, KC, 1024], BF16); rWM = S.res()
                    HTM = SB(es2, "HTM", [128, KC, 128], BF16); rHTM = S.res()
                    KCN = SB(es2, "KCN", [128, 512], BF16); rKCN = S.res()
                    load_cast_w(stage, WM, wmem, 1024, MG, rWM)
                    for mb in range(2):
                        nt.run(memd[mb * 128:(mb + 1) * 128, :], HTM, rHTM)
                        for n in range(2):
                            for k in range(KC):
                                S.op("tensor", L("matmul", PS[n][:, :], lhsT=HTM[:, k, :], rhs=WM[:, k, n * 512:(n + 1) * 512],
                                                                            start=(k == 0), stop=(k == KC - 1)), reads=[rHTM, rWM], writes=[RPS[n]])
                        for h in range(4):
                            S.op("scalar", L("activation", out=T1[0][:, 0:128], in_=PS[0][:, h * 128:(h + 1) * 128], func=AF.Square,
                                                                       accum_out=ST4[:, h:h + 1]), reads=[RPS[0]], writes=[rT1[0], rST4])
                        S.op("vector", L("tensor_scalar", out=ST4[:, 4:8], in0=ST4[:, 0:4], scalar1=1.0 / 128, scalar2=EPS, op0=ALU.mult, op1=ALU.add),
                             reads=[rST4], writes=[rST4])
                        pow_rstd(ST4[:, 8:12], ST4[:, 4:8], [rST4], [rST4], 4)
                        for h in range(4):
                            S.op("vector", L("tensor_scalar", out=KCN[:, h * 128:(h + 1) * 128], in0=PS[0][:, h * 128:(h + 1) * 128],
                                                                          scalar1=ST4[:, 8 + h:9 + h], scalar2=None, op0=ALU.mult),
                                 reads=[RPS[0], rST4], writes=[rKCN])
                        for h in range(4):
                            S.op("tensor", L("transpose", out=PSB[:, h * 128:(h + 1) * 128], in_=KCN[:, h * 128:(h + 1) * 128], identity=IDENT),
                                 reads=[rKCN, Rc], writes=[RPS[7]])
                        S.op("scalar", L("copy", out=KCT[:, :, mb * 128:(mb + 1) * 128], in_=PSB[:, 0:512].rearrange("p (h t) -> p h t", h=4)),
                             reads=[RPS[7]], writes=[rKC])
                        S.op("scalar", L("copy", out=VC[:, mb, :], in_=PS[1][:, :]), reads=[RPS[1]], writes=[rKC])
                S.barrier()
                for i in range(NQ):
                    p = 4 * i + 3
                    nt.run(xk[p * 128:(p + 1) * 128, :], HT[:, :, i * 128:(i + 1) * 128], rHT)
                WB = [SB(es, f"WB{i}", [128, KC, 256], BF16) for i in range(2)]; rWB = [S.res() for _ in range(2)]
                pending = []
                for g in range(NCH // 2):
                    b = nxt("wb", 2)
                    load_cast_w(stage, WB[b], wq[:, g * 256:(g + 1) * 256], 256, GN, rWB[b])
                    for cc in range(2):
                        ch = 2 * g + cc
                        kind = KINDS[ch]
                        for m in range(4):
                            pb = nxt("qacc", 3)
                            for k in range(KC):
                                S.op("tensor", L("matmul", PS[pb][:, :], lhsT=WB[b][:, k, cc * 128:(cc + 1) * 128], rhs=HT[:, k, m * 512:(m + 1) * 512],
                                    start=(k == 0), stop=(k == KC - 1)), reads=[rWB[b], rHT], writes=[RPS[pb]])
                            while pending:
                                pending.pop(0)()
                            ob = nxt("ot", 3)
                            if kind == "iq":
                                S.op("scalar", L("copy", out=OT[ob][:], in_=PS[pb][:, :]), reads=[RPS[pb]], writes=[rOT[ob]])
                            elif kind == "silu":
                                S.op("scalar", L("activation", out=OT[ob][:], in_=PS[pb][:, :], func=AF.Silu),
                                     reads=[RPS[pb]], writes=[rOT[ob]])
                            elif kind == "mix":
                                S.op("scalar", L("activation", out=OT[ob][:], in_=PS[pb][:, :], func=AF.Sigmoid,
                                                                                          bias=GBI[:, ch - C_MIX:ch - C_MIX + 1]),
                                     reads=[RPS[pb], Rc], writes=[rOT[ob]])
                            else:
                                dd = 64 if kind == "bq" else 128
                                gcol = {"aq": 0, "cq": 7, "bq": 6}[kind]
                                lh = BLK64 if kind == "bq" else ONES
                                sb_ = nxt("sq", 2)
                                sbank = 3 + nxt("ss", 2)
                                S.op("scalar", L("activation", out=SQ[sb_][:], in_=PS[pb][:, :], func=AF.Square),
                                     reads=[RPS[pb]], writes=[rSQ[sb_]])
                                def _tail(pb=pb, ob=ob, sb_=sb_, sbank=sbank, lh=lh, gcol=gcol, dd=dd, ch=ch, m=m):
                                    S.op("tensor", L("matmul", PS[sbank][:, :], lhsT=lh, rhs=SQ[sb_][:], start=True, stop=True),
                                         reads=[rSQ[sb_], Rc], writes=[RPS[sbank]])
                                    S.op("scalar", L("activation", out=T1[sb_][:], in_=PS[sbank][:, :], func=AF.Ln, bias=EPS, scale=1.0 / dd),
                                         reads=[RPS[sbank]], writes=[rT1[sb_]])
                                    S.op("scalar", L("activation", out=T2[sb_][:], in_=T1[sb_][:], func=AF.Exp, scale=-0.5), reads=[rT1[sb_]], writes=[rT2[sb_]])
                                    S.op("vector", L("scalar_tensor_tensor", out=OT[ob][:], in0=PS[pb][:, :], scalar=GV[:, gcol:gcol + 1], in1=T2[sb_][:], op0=ALU.mult, op1=ALU.mult),
                                        reads=[RPS[pb], rT2[sb_], Rc], writes=[rOT[ob]])
                                    rQS[(ch, m)] = S.res()
                                    S.dma("gpsimd", QS[ch, 4 * m:4 * m + 4].rearrange("q p t -> p q t"), OT[ob][:].rearrange("p (q t) -> p q t", q=4),
                                          reads=[rOT[ob]], writes=[rQS[(ch, m)]])
                                pending.append(_tail)
                                continue
                            rQS[(ch, m)] = S.res()
                            S.dma("gpsimd", QS[ch, 4 * m:4 * m + 4].rearrange("q p t -> p q t"), OT[ob][:].rearrange("p (q t) -> p q t", q=4),
                                  reads=[rOT[ob]], writes=[rQS[(ch, m)]])
                while pending:
                    pending.pop(0)()
                WIW = SB(es, "WIW", [128, KC, 16], BF16); rWIW = S.res()
                OTW = SB(es, "OTW", [16, 512], BF16); rOTW = S.res()
                load_cast_w(stage, WIW, wiw, 16, GN, rWIW)
                for m in range(4):
                    pb = nxt("qacc", 3)
                    for k in range(KC):
                        S.op("tensor", L("matmul", PS[pb][0:16, :], lhsT=WIW[:, k, :], rhs=HT[:, k, m * 512:(m + 1) * 512],
                                                                           start=(k == 0), stop=(k == KC - 1)), reads=[rWIW, rHT], writes=[RPS[pb]])
                    S.op("scalar", L("mul", out=OTW[:], in_=PS[pb][0:16, :], mul=1.0 / 32.0), reads=[RPS[pb]], writes=[rOTW])
                    rIWS[m] = S.res()
                    S.dma("gpsimd", IWS[4 * m:4 * m + 4].rearrange("q h t -> h q t"), OTW[:].rearrange("h (q t) -> h q t", q=4),
                          reads=[rOTW], writes=[rIWS[m]])

        S.barrier()
        mid = contextlib.ExitStack()
        top.enter_context(mid)
        CKVNT = SB(mid, "CKVNT", [128, 2, SEQ], BF16); rCKVNT = S.res()
        CKVN = SB(mid, "CKVN", [128, NB, 256], BF16); rCKVN = S.res()
        IKT = SB(mid, "IKT", [128, SEQ], BF16); rIKT = S.res()
        KBT = SB(mid, "KBT", [128, 32, 128], BF16); rKBT = S.res()
        VB = SB(mid, "VB", [128, 32, 128], BF16); rVB = S.res()
        RK = SB(mid, "RK", [128, NB, 6], F32); rRK = S.res()
        WV = SB(mid, "WV", [128, 2, 768], BF16)
        WKT = SB(mid, "WKT", [128, 6, 256], BF16)

        if "K" in phases:
            with contextlib.ExitStack() as es:
                nt = NT(es)
                stage = mk_stage(es)
                GN = SB(es, "GNk", [128, 16], F32); KVG = SB(es, "KVG", [128, 2], F32)
                LNG = SB(es, "LNG", [128, 64], F32); LNB = SB(es, "LNB", [128, 64], F32); GKV = SB(es, "GKV", [128, 256], F32)
                S.dma("sync", GN[:], normg[:, :], writes=[Rc]); S.dma("sync", KVG[:], kvg[:, :], writes=[Rc])
                S.dma("sync", LNG[:], lngd[:, :], writes=[Rc]); S.dma("sync", LNB[:], lnbd[:, :], writes=[Rc])
                S.dma("sync", GKV[:], gkvbc[:, :], writes=[Rc])
                WKb = SB(es, "WKb", [128, KC, 576], BF16); rWKb = S.res()
                WKUK = SB(es, "WKUK", [128, 2, 768], BF16); rWKU = S.res()
                load_cast_w(stage, WKb, wk, 576, GN, rWKb, chunkc=192)
                load_cast_w(stage, WKUK, wkvu[:, 0:768], 768, KVG, rWKU, nk=2)
                load_cast_w(stage, WV, wkvu[:, 768:1536], 768, KVG, rWKU, nk=2)
                WT32 = SB(es, "WT32", [128, 6, 256], F32); rWT = S.res()
                S.dma("sync", WT32[:], wkTd[:, :, :], writes=[rWT])
                for h in range(6):
                    S.op("vector", L("scalar_tensor_tensor", out=WKT[:, h, :], in0=WT32[:, h, :], scalar=GV[:, 1:2], in1=GKV[:],
                                                                         op0=ALU.mult, op1=ALU.mult), reads=[rWT, Rc], writes=[rWKU])
                HTB = [SB(es, f"HTB{i}", [128, KC, 128], BF16) for i in range(2)]; rHTB = [S.res() for _ in range(2)]
                JK = SB(es, "JK", [128, 256], BF16); rJK = S.res()
                SK_ = [SB(es, f"SKt{i}", [128, 32], F32) for i in range(2)]; rSK = [S.res() for _ in range(2)]
                dbgK = (SK_, rSK, JK, rJK)
                IK32 = SB(es, "IK32", [128, 64], F32); rIK32 = S.res()
                IKD = SB(es, "IKD", [128, 128], BF16); rIKD = S.res()
                KBD = SB(es, "KBD", [128, 128], BF16); rKBD = S.res()
                PSB6 = PS[6][:].bitcast(BF16)

                def k_A(p):
                        b = p % 2
                        kA, kB = (0, 1) if p % 2 == 0 else (4, 5)
                        hT = HTB[b]; rh = rHTB[b]
                        nt.run(xk[p * 128:(p + 1) * 128, :], hT, rh)

                def k_B(p):
                        b = p % 2
                        kA, kB = (0, 1) if p % 2 == 0 else (4, 5)
                        hT = HTB[b]; rh = rHTB[b]
                        for k in range(KC):
                            S.op("tensor", L("matmul", PS[kA][:, :], lhsT=hT[:, k, :], rhs=WKb[:, k, 0:512], start=(k == 0), stop=(k == KC - 1)),
                                 reads=[rh, rWKb], writes=[RPS[kA]])
                        for k in range(KC):
                            S.op("tensor", L("matmul", PS[kB][:, 0:64], lhsT=hT[:, k, :], rhs=WKb[:, k, 512:576], start=(k == 0), stop=(k == KC - 1)),
                                 reads=[rh, rWKb], writes=[RPS[kB]])


                def k_C1a(p):
                        sb_ = p % 2
                        kA, kB = (0, 1) if p % 2 == 0 else (4, 5)
                        if p not in RP:
                            RP[p] = tuple(S.res() for _ in range(6))
                        rCKVNp, rCKVNTp, rIKTp, rKBTp, rVBp, rRKp = RP[p]
                        st = SK_[sb_]; rs = rSK[sb_]
                        S.op("scalar", L("activation", out=JK[:], in_=PS[kA][:, 0:256], func=AF.Square, accum_out=st[:, 0:1]),
                             reads=[RPS[kA]], writes=[rs])
                        S.op("vector", L("tensor_scalar", out=st[:, 1:2], in0=st[:, 0:1], scalar1=1.0 / 256, scalar2=EPS, op0=ALU.mult, op1=ALU.add),
                             reads=[rs], writes=[rs])
                        pow_rstd(st[:, 2:3], st[:, 1:2], [rs], [rs], 1)
                        S.op("vector", L("tensor_scalar", out=CKVN[:, p, :], in0=PS[kA][:, 0:256], scalar1=st[:, 2:3], scalar2=None, op0=ALU.mult),
                             reads=[RPS[kA], rs], writes=[rCKVNp])
                        for rc in range(2):
                            S.op("tensor", L("transpose", out=PSB6[:, rc * 128:(rc + 1) * 128], in_=CKVN[:, p, rc * 128:(rc + 1) * 128], identity=IDENT),
                                 reads=[rCKVNp, Rc], writes=[RPS[6]])

                def k_C1b(p):
                        sb_ = p % 2
                        kA, kB = (0, 1) if p % 2 == 0 else (4, 5)
                        if p not in RP:
                            RP[p] = tuple(S.res() for _ in range(6))
                        rCKVNp, rCKVNTp, rIKTp, rKBTp, rVBp, rRKp = RP[p]
                        st = SK_[sb_]; rs = rSK[sb_]
                        S.op("scalar", L("copy", out=CKVNT[:, :, p * 128:(p + 1) * 128], in_=PSB6[:, 0:256].rearrange("q (c t) -> q c t", c=2)),
                             reads=[RPS[6]], writes=[rCKVNTp])
                        for n, (c0, cw) in enumerate(((0, 512), (512, 256))):
                            for rc in range(2):
                                S.op("tensor", L("matmul", PS[2 + n][:, 0:cw], lhsT=CKVNT[:, rc, p * 128:(p + 1) * 128],
                                                                                               rhs=WKUK[:, rc, c0:c0 + cw], start=(rc == 0), stop=(rc == 1)),
                                     reads=[rCKVNTp, rWKU], writes=[RPS[2 + n]])


                def k_C2(p):
                        sb_ = p % 2
                        kA, kB = (0, 1) if p % 2 == 0 else (4, 5)
                        if p not in RP:
                            RP[p] = tuple(S.res() for _ in range(6))
                        rCKVNp, rCKVNTp, rIKTp, rKBTp, rVBp, rRKp = RP[p]
                        st = SK_[sb_]; rs = rSK[sb_]
                        S.op("vector", L("tensor_scalar", out=JK[:, 0:64], in0=PS[kB][:, 0:64], scalar1=1.0, scalar2=None, op0=ALU.mult, op1=ALU.add,
                                                                        accum_out=st[:, 16:17]), reads=[RPS[kB]], writes=[rs])
                        S.op("scalar", L("activation", out=JK[:, 0:64], in_=PS[kB][:, 0:64], func=AF.Square, accum_out=st[:, 17:18]),
                             reads=[RPS[kB]], writes=[rs])
                        S.op("vector", L("tensor_scalar", out=st[:, 18:20], in0=st[:, 16:18], scalar1=1.0 / 64, scalar2=None, op0=ALU.mult),
                             reads=[rs], writes=[rs])
                        S.op("vector", L("tensor_tensor", out=st[:, 20:21], in0=st[:, 18:19], in1=st[:, 18:19], op=ALU.mult), reads=[rs], writes=[rs])
                        S.op("vector", L("scalar_tensor_tensor", out=st[:, 21:22], in0=st[:, 19:20], scalar=EPS, in1=st[:, 20:21], op0=ALU.add,
                                                                               op1=ALU.subtract), reads=[rs], writes=[rs])
                        pow_rstd(st[:, 22:23], st[:, 21:22], [rs], [rs], 1)
                        S.op("vector", L("tensor_scalar", out=IK32[:], in0=PS[kB][:, 0:64], scalar1=st[:, 18:19], scalar2=st[:, 22:23],
                                                                        op0=ALU.subtract, op1=ALU.mult), reads=[RPS[kB], rs], writes=[rIK32])
                        S.op("vector", L("tensor_tensor", out=IK32[:], in0=IK32[:], in1=LNG[:], op=ALU.mult), reads=[rIK32, Rc], writes=[rIK32])
                        S.op("vector", L("tensor_tensor", out=IKD[:, 0:64], in0=IK32[:], in1=LNB[:], op=ALU.add), reads=[rIK32, Rc], writes=[rIKD])
                        S.op("vector", L("tensor_tensor", out=IKD[:, 64:128], in0=IK32[:], in1=LNB[:], op=ALU.add), reads=[rIK32, Rc], writes=[rIKD])
                        S.op("tensor", L("transpose", out=PSB6[:, 256:384], in_=IKD[:], identity=IDENT), reads=[rIKD, Rc], writes=[RPS[6]])
                        S.op("scalar", L("copy", out=IKT[:, p * 128:(p + 1) * 128], in_=PSB6[:, 256:384]), reads=[RPS[6]], writes=[rIKTp])
                        if p % 4 >= 2:
                            slot = (p // 4) * 2 + (p % 4 - 2)
                            for j in range(2):
                                S.op("scalar", L("activation", out=JK[:, 0:64], in_=PS[kA][:, 256 + j * 64:320 + j * 64], func=AF.Square,
                                    accum_out=st[:, 24 + j:25 + j]), reads=[RPS[kA]], writes=[rs])
                            S.op("vector", L("tensor_scalar", out=st[:, 26:28], in0=st[:, 24:26], scalar1=1.0 / 64, scalar2=EPS, op0=ALU.mult, op1=ALU.add),
                                 reads=[rs], writes=[rs])
                            pow_rstd(st[:, 28:30], st[:, 26:28], [rs], [rs], 2)
                            for j in range(2):
                                S.op("vector", L("tensor_scalar", out=KBD[:, j * 64:(j + 1) * 64], in0=PS[kA][:, 256 + j * 64:320 + j * 64],
                                                                                     scalar1=st[:, 28 + j:29 + j], scalar2=None, op0=ALU.mult),
                                     reads=[RPS[kA], rs], writes=[rKBD])
                            S.op("tensor", L("transpose", out=PSB6[:, 384:512], in_=KBD[:], identity=IDENT), reads=[rKBD, Rc], writes=[RPS[6]])
                            S.op("scalar", L("copy", out=KBT[:, slot, :], in_=PSB6[:, 384:512]), reads=[RPS[6]], writes=[rKBTp])
                            S.op("scalar", L("copy", out=VB[:, slot, :], in_=PS[kA][:, 384:512]), reads=[RPS[kA]], writes=[rVBp])

                def k_T(p):
                        sb_ = p % 2
                        kA, kB = (0, 1) if p % 2 == 0 else (4, 5)
                        if p not in RP:
                            RP[p] = tuple(S.res() for _ in range(6))
                        rCKVNp, rCKVNTp, rIKTp, rKBTp, rVBp, rRKp = RP[p]
                        st = SK_[sb_]; rs = rSK[sb_]
                        for h in range(6):
                            bank = 2 + (h // 4); off = (h % 4) * 128
                            S.op("scalar", L("activation", out=JK[:, 0:128], in_=PS[bank][:, off:off + 128], func=AF.Square,
                                accum_out=st[:, 4 + h:5 + h]), reads=[RPS[bank]], writes=[rs])
                        S.op("vector", L("tensor_scalar", out=st[:, 10:16], in0=st[:, 4:10], scalar1=128.0 * EPS, scalar2=None, op0=ALU.add),
                             reads=[rs], writes=[rs])
                        S.op("gpsimd", L("tensor_tensor", out=RK[:, p, :], in0=st[:, 10:16], in1=NEGH[:, 0:6], op=ALU.pow),
                             reads=[rs, Rc], writes=[rRKp])


                RP = {}
                k_A(0); k_B(0); k_A(1)
                for p in range(NB):
                    if p >= 1:
                        k_T(p - 1)
                    k_C1a(p)
                    if p + 1 < NB:
                        k_B(p + 1)
                    k_C2(p)
                    k_C1b(p)
                    if p + 2 < NB:
                        k_A(p + 2)
                k_T(NB - 1)
                if debug:
                    for q_ in range(2):
                        out_ops.append(S.dma("sync", DBG[:, 8576 + 32 * q_:8608 + 32 * q_], SK_[q_][:], reads=[rSK[q_]], writes=[S.res()]))

        S.barrier()
        if "A" in phases:
            with contextlib.ExitStack() as es:
                SC = SB(es, "SC", [128, SEQ], F32); rSC = S.res()
                CM = SB(es, "CM", [128, 128], F32); PADM = SB(es, "PADM", [128, 384], F32); BPV = SB(es, "BPV", [128, 1], F32)
                EBA = SB(es, "EBA", [128, 2, 6, 128], BF16); EBB = SB(es, "EBB", [128, 2, 2, 6, 128], BF16); ESK = SB(es, "ESK", [128, 6], F32)
                S.dma("sync", CM[:], cmaskd[:, :], writes=[Rc]); S.dma("sync", PADM[:], padmd[:, :], writes=[Rc]); S.dma("sync", BPV[:], bpvd[:, :], writes=[Rc])
                S.dma("sync", ESK[:], skd[:, :], writes=[Rc])
                S.op("scalar", L("activation", out=ESK[:], in_=ESK[:], func=AF.Exp), reads=[Rc], writes=[Rc])
                with contextlib.ExitStack() as es2:
                    tA = SB(es2, "tA", [128, 2, 6, 128], F32); tC = SB(es2, "tC", [128, 2, 6, 128], F32); tV = SB(es2, "tV", [128, 2, 128], F32)
                    tB = SB(es2, "tB", [128, 2, 2, 6, 128], F32); tW = SB(es2, "tW", [128, 2, 128], F32)
                    r1 = S.res()
                    S.dma("sync", tA[:], bad[:, :, :, :], writes=[r1]); S.dma("sync", tC[:], cad[:, :, :, :], writes=[r1])
                    S.dma("sync", tV[:], vad[:, :, :], writes=[r1]); S.dma("sync", tB[:], bbd[:, :, :, :, :], writes=[r1]); S.dma("sync", tW[:], vb01d[:, :, :], writes=[r1])
                    S.op("vector", L("tensor_tensor", out=tA[:], in0=tA[:], in1=tC[:], op=ALU.subtract), reads=[r1], writes=[r1])
                    S.op("scalar", L("activation", out=tA[:], in_=tA[:], func=AF.Exp), reads=[r1], writes=[r1])
                    S.op("scalar", L("activation", out=tB[:], in_=tB[:], func=AF.Exp), reads=[r1], writes=[r1])
                    for ty in range(2):
                        for h in range(6):
                            S.op("vector", L("tensor_tensor", out=EBA[:, ty, h, :], in0=tA[:, ty, h, :], in1=tV[:, ty, :], op=ALU.mult),
                                 reads=[r1], writes=[Rc])
                        for j in range(2):
                            for c in range(6):
                                S.op("vector", L("tensor_tensor", out=EBB[:, ty, j, c, :], in0=tB[:, ty, j, c, :], in1=tW[:, ty, :], op=ALU.mult),
                                     reads=[r1], writes=[Rc])
                S.barrier()
                IQT = SB(es, "IQT", [128, 8, 2, 128], BF16); rIQT = S.res()
                S.op("gpsimd", L("memset", IQT[:], 0.0), writes=[rIQT])
                IWT = SB(es, "IWT", [16, 128], BF16); rIWT = S.res()
                IWF = SB(es, "IWF", [128, 16], F32); rIWF = S.res()
                DG = SB(es, "DG", [128, 16, 128], BF16); rDG = S.res()
                QAT = SB(es, "QAT", [128, 6, 128], BF16); rQAT = S.res()
                QBT = SB(es, "QBT", [128, 6, 128], BF16); rQBT = S.res()
                QCT = SB(es, "QCT", [128, 4, 128], BF16); rQCT = S.res()
                GA = SB(es, "GA", [128, 6, 128], BF16); rGA = S.res()
                GB = SB(es, "GB", [128, 6, 128], BF16); rGB = S.res()
                GC = SB(es, "GC", [128, 4, 128], BF16); rGC = S.res()
                RL = [SB(es, f"RL{q}", [128, 512], BF16) for q in range(4)]; rRL = [S.res() for _ in range(4)]
                QT = [SB(es, f"QT{q}", [128, 2, 384], BF16) for q in range(2)]; rQT = [S.res() for _ in range(2)]
                PT = [SB(es, f"PT{q}", [128, 3, 128], BF16) for q in range(3)]; rPT = [S.res() for _ in range(3)]
                RD = SB(es, "RD", [128, 384], F32); rRD = S.res()
                DN = SB(es, "DN", [128, 384], F32); rDN = S.res()
                PV32 = SB(es, "PV32", [128, 384], F32); rPV32 = S.res()
                ON = SB(es, "ON", [128, 2, 384], BF16); rON = S.res()
                OAG = [SB(es, f"OAG{q}", [128, 3, 128], BF16) for q in range(2)]; rOAG = [S.res() for _ in range(2)]
                THR = SB(es, "THR", [128, 4], F32); rTHR = S.res()
                J1 = SB(es, "J1", [128, 2], BF16); rJ1 = S.res()
                PTB = SB(es, "PTB", [128, 2, 6, 128], BF16); rPTB = S.res()
                VBP = SB(es, "VBP", [128, 2, 2, 128], BF16); rVBP = S.res()
                RDB = SB(es, "RDB", [128, 768], F32); rRDB = S.res()
                OBG = SB(es, "OBG", [128, 6, 128], BF16); rOBG = S.res()
                PTC = PTB[:, :, 0:4, :]; rPTC = rPTB
                RDC = RDB[:, 0:512]; rRDC = rRDB
                OCG = OBG[:, 0:4, :]; rOCG = rOBG
                S.op("gpsimd", L("memset", VBP[:], 0.0), writes=[rVBP])
                qs_res = lambda c0, n, i: [rQS[(c0 + c, i // 4)] for c in range(n)] if rQS else []

                def qload(dst, rdst, c0, n, i):
                    S.dma("sync", dst[:], QS[c0:c0 + n, i].rearrange("c p t -> p c t"), reads=qs_res(c0, n, i), writes=[rdst])

                F8 = mybir.dt.float8e4
                MBF = SB(es, "MBF", [128, SEQ], F8); rMBF = S.res()
                PSB6 = PS[6][:].bitcast(BF16)
                DVE_HEADS = (1, 3, 5, 7, 9, 11, 13, 15)

                def gen_idx(i):
                    nsb = i + 1
                    S_i = 512 * nsb
                    for half in range(2):
                        S.dma("sync", IQT[half * 64:(half + 1) * 64, :, half, :], QS[C_IQ:C_IQ + 8, i, half * 64:(half + 1) * 64, :].rearrange("c p t -> p c t"),
                              reads=qs_res(C_IQ, 8, i), writes=[rIQT])
                    S.dma("sync", IWT[:], IWS[i], reads=[rIWS[i // 4]] if rIWS else [], writes=[rIWT])
                    S.op("tensor", L("transpose", out=PSB6[:, 0:16], in_=IWT[:], identity=IDENT[0:16, 0:16]), reads=[rIWT, Rc], writes=[RPS[6]])
                    S.op("scalar", L("copy", out=IWF[:], in_=PSB6[:, 0:16]), reads=[RPS[6]], writes=[rIWF])
                    for h in range(16):
                        S.op("vector", L("tensor_scalar", out=DG[:, h, :], in0=IDENT, scalar1=IWF[:, h:h + 1], scalar2=None, op0=ALU.mult),
                             reads=[rIWF, Rc], writes=[rDG])
                    yield 1.0
                    seq = [(sb, h) for sb in range(nsb) for h in range(16)]
                    bcs = {}

                    def score_mm(idx):
                        sb, h = seq[idx]
                        bc = (5, 6, 0, 1, 2, 3)[nxt("psc", 6)]
                        bcs[idx] = bc
                        S.op("tensor", L("matmul", PS[bc][:, :], lhsT=IQT[:, h // 2, h % 2, :], rhs=IKT[:, sb * 512:(sb + 1) * 512], start=True, stop=True),
                             reads=[rIQT, rIKT], writes=[RPS[bc]])

                    score_mm(0)
                    score_mm(1)
                    score_mm(2)
                    bS = 7
                    for idx, (sb, h) in enumerate(seq):
                        if idx + 3 < len(seq):
                            score_mm(idx + 3)
                        bc = bcs[idx]
                        rl = nxt("rl", 4)
                        if h == 0:
                            bS = (7, 4)[nxt("psS", 2)]
                        if h in DVE_HEADS:
                            S.op("vector", L("tensor_scalar", out=RL[rl][:], in0=PS[bc][:, :], scalar1=0.0, scalar2=None, op0=ALU.max), reads=[RPS[bc]], writes=[rRL[rl]])
                        else:
                            S.op("scalar", L("activation", out=RL[rl][:], in_=PS[bc][:, :], func=AF.Relu), reads=[RPS[bc]], writes=[rRL[rl]])
                        yield 0.25
                        S.op("tensor", L("matmul", PS[bS][:, :], lhsT=DG[:, h, :], rhs=RL[rl][:], start=(h == 0), stop=(h == 15)),
                             reads=[rDG, rRL[rl]], writes=[RPS[bS]])
                        if h == 15:
                            if sb == i:
                                pieces = [(0, 384, PADM if sb == 0 else None), (384, 512, CM)]
                            elif sb == 0:
                                pieces = [(0, 384, PADM), (384, 512, None)]
                            else:
                                pieces = [(0, 512, None)]
                            for (c0, c1, mk) in pieces:
                                if mk is None:
                                    S.op("scalar", L("copy", out=SC[:, sb * 512 + c0:sb * 512 + c1], in_=PS[bS][:, c0:c1]), reads=[RPS[bS]], writes=[rSC])
                                else:
                                    S.op("vector", L("tensor_tensor", out=SC[:, sb * 512 + c0:sb * 512 + c1], in0=PS[bS][:, c0:c1], in1=mk[:, 0:c1 - c0], op=ALU.add),
                                         reads=[RPS[bS], Rc], writes=[rSC])
                        yield 0.25
                    yield -1.0
                    S.op("vector", L("memset", THR[:], 0.0), writes=[rTHR])
                    for n in range(NIT):
                        step = LIM / 2.0 ** (n + 1)
                        S.op("vector", L("tensor_scalar", out=J1[:, 0:1].to_broadcast([128, S_i]), in0=SC[:, 0:S_i], scalar1=THR[:, 0:1], scalar2=None,
                                         op0=ALU.is_ge, op1=ALU.add, accum_out=THR[:, 1:2]), reads=[rSC, rTHR], writes=[rJ1, rTHR])
                        S.op("vector", L("tensor_scalar", out=THR[:, 2:3], in0=THR[:, 1:2], scalar1=255.5, scalar2=2.0 * step, op0=ALU.is_gt, op1=ALU.mult),
                             reads=[rTHR], writes=[rTHR])
                        S.op("vector", L("scalar_tensor_tensor", out=THR[:, 0:1], in0=THR[:, 2:3], scalar=-step, in1=THR[:, 0:1], op0=ALU.add, op1=ALU.add),
                             reads=[rTHR], writes=[rTHR])
                        yield 0.5 + 8.6 * S_i / 8192.0
                    S.op("vector", L("tensor_scalar", out=THR[:, 3:4], in0=THR[:, 0:1], scalar1=-LIM / 2.0 ** NIT, scalar2=None, op0=ALU.add),
                         reads=[rTHR], writes=[rTHR])
                    for sb in range(nsb):
                        S.op("vector", L("tensor_scalar", out=MBF[:, sb * 512:(sb + 1) * 512], in0=SC[:, sb * 512:(sb + 1) * 512], scalar1=THR[:, 3:4], scalar2=1.0,
                                         op0=ALU.is_ge, op1=ALU.subtract, saturate=False), reads=[rSC, rTHR], writes=[rMBF])
                        yield 0.4

                def gen_att(i):
                    nsb = i + 1
                    nkb = 4 * nsb
                    qload(QAT, rQAT, C_AQ, 6, i); qload(GA, rGA, C_AG, 6, i)
                    for g in range(2):
                        qt = nxt("qt", 2)
                        for rc in range(2):
                            for hh in range(3):
                                h = 3 * g + hh
                                S.op("tensor", L("matmul", PS[rc][:, hh * 128:(hh + 1) * 128], lhsT=WKT[:, h, rc * 128:(rc + 1) * 128],
                                                 rhs=QAT[:, h, :], start=True, stop=True), reads=[rQAT, rKC], writes=[RPS[rc]])
                            S.op("scalar", L("copy", out=QT[qt][:, rc, :], in_=PS[rc][:, 0:384]), reads=[RPS[rc]], writes=[rQT[qt]])
                        yield 1.0
                        bts = {}

                        def qk_mm(kbi):
                            bt = nxt("pst", 2)
                            bts[kbi] = bt
                            for rc in range(2):
                                S.op("tensor", L("matmul", PS[bt][:, 0:384], lhsT=CKVNT[:, rc, kbi * 128:(kbi + 1) * 128], rhs=QT[qt][:, rc, :],
                                                 start=(rc == 0), stop=False), reads=[rCKVNT, rQT[qt]], writes=[RPS[bt]])
                            S.op("tensor", L("matmul", PS[bt][:, 0:384], lhsT=MBF[:, kbi * 128:(kbi + 1) * 128], rhs=IREP, start=False, stop=True),
                                 reads=[rMBF, Rc], writes=[RPS[bt]])

                        qk_mm(0)
                        for kbi in range(nkb):
                            if kbi + 1 < nkb:
                                qk_mm(kbi + 1)
                            bt = bts[kbi]
                            pt = nxt("pt", 3)
                            for hh in range(3):
                                S.op("scalar", L("activation", out=PT[pt][:, hh, :], in_=PS[bt][:, hh * 128:(hh + 1) * 128], func=AF.Exp,
                                                 scale=RK[:, kbi, 3 * g + hh:3 * g + hh + 1]), reads=[RPS[bt], rRK], writes=[rPT[pt]])
                            if kbi >= 4 * i + 2:
                                ty = 0 if kbi == 4 * i + 3 else 1
                                S.op("vector", L("tensor_tensor", out=PT[pt][:], in0=PT[pt][:], in1=EBA[:, ty, 3 * g:3 * g + 3, :], op=ALU.mult),
                                     reads=[rPT[pt], Rc], writes=[rPT[pt]])
                            yield 0.55
                            ptf = PT[pt][:].rearrange("p a b -> p (a b)")
                            for rc in range(2):
                                S.op("tensor", L("matmul", PS[2 + rc][:, 0:384], lhsT=CKVN[:, kbi, rc * 128:(rc + 1) * 128], rhs=ptf,
                                                 start=(kbi == 0), stop=(kbi == nkb - 1)), reads=[rCKVN, rPT[pt]], writes=[RPS[2 + rc]])
                            S.op("tensor", L("matmul", PS[4][:, 0:384], lhsT=ONES, rhs=ptf, start=(kbi == 0), stop=(kbi == nkb - 1)),
                                 reads=[Rc, rPT[pt]], writes=[RPS[4]])
                            yield 0.55
                        for rc in range(2):
                            S.op("scalar", L("copy", out=ON[:, rc, :], in_=PS[2 + rc][:, 0:384]), reads=[RPS[2 + rc]], writes=[rON])
                        S.op("scalar", L("copy", out=DN[:], in_=PS[4][:, 0:384]), reads=[RPS[4]], writes=[rDN])
                        yield 1.0
                        for hh in range(3):
                            h = 3 * g + hh
                            for rc in range(2):
                                S.op("tensor", L("matmul", PS[0][:, hh * 128:(hh + 1) * 128], lhsT=WV[:, rc, h * 128:(h + 1) * 128],
                                                 rhs=ON[:, rc, hh * 128:(hh + 1) * 128], start=(rc == 0), stop=(rc == 1)),
                                     reads=[rON, rKC], writes=[RPS[0]])
                        S.op("scalar", L("copy", out=PV32[:], in_=PS[0][:, 0:384]), reads=[RPS[0]], writes=[rPV32])
                        og = nxt("oag", 2)
                        S.op("vector", L("reciprocal", out=RD[:], in_=DN[:]), reads=[rDN], writes=[rRD])
                        S.op("vector", L("tensor_tensor", out=PV32[:], in0=PV32[:], in1=RD[:], op=ALU.mult), reads=[rPV32, rRD], writes=[rPV32])
                        S.op("vector", L("tensor_tensor", out=OAG[og][:].rearrange("p a b -> p (a b)"), in0=PV32[:],
                                         in1=GA[:, 3 * g:3 * g + 3, :].rearrange("p a b -> p (a b)"), op=ALU.mult),
                             reads=[rPV32, rGA], writes=[rOAG[og]])
                        rOS[(g, i)] = S.res()
                        S.dma("gpsimd", OS[3 * g:3 * g + 3, i].rearrange("c p t -> p c t"), OAG[og][:], reads=[rOAG[og]], writes=[rOS[(g, i)]])
                        yield 1.0

                def gen_bc(i):
                    qload(QBT, rQBT, C_BQ, 6, i); qload(GB, rGB, C_BG, 6, i)
                    qload(QCT, rQCT, C_CQ, 4, i); qload(GC, rGC, C_CG, 4, i)
                    for blk in range(2):
                        slot = 2 * i + blk
                        S.op("gpsimd", L("tensor_copy", out=VBP[:, blk, 0, 0:64], in_=VB[:, slot, 0:64]), reads=[rVB], writes=[rVBP])
                        S.op("gpsimd", L("tensor_copy", out=VBP[:, blk, 1, 64:128], in_=VB[:, slot, 64:128]), reads=[rVB], writes=[rVBP])
                    yield 0.5
                    for j in range(2):
                        r0, r1_ = j * 64, (j + 1) * 64
                        for hf in range(2):
                            c0 = 3 * hf
                            for blk in range(2):
                                slot = 2 * i + blk
                                S.op("tensor", L("matmul", PS[5][:, 0:384], lhsT=KBT[r0:r1_, slot, :], rhs=QBT[r0:r1_, c0:c0 + 3, :].rearrange("p a b -> p (a b)"),
                                                 start=True, stop=True), reads=[rKBT, rQBT], writes=[RPS[5]])
                                S.op("scalar", L("activation", out=PTB[:, blk, c0:c0 + 3, :].rearrange("p a b -> p (a b)"), in_=PS[5][:, 0:384], func=AF.Exp, scale=0.125),
                                     reads=[RPS[5]], writes=[rPTB])
                                if blk == 0 and i == 0:
                                    S.op("vector", L("scalar_tensor_tensor", out=PTB[:, blk, c0:c0 + 3, :], in0=PTB[:, blk, c0:c0 + 3, :], scalar=BPV[:, 0:1],
                                                     in1=EBB[:, blk, j, c0:c0 + 3, :], op0=ALU.mult, op1=ALU.mult), reads=[rPTB, Rc], writes=[rPTB])
                                else:
                                    S.op("vector", L("tensor_tensor", out=PTB[:, blk, c0:c0 + 3, :], in0=PTB[:, blk, c0:c0 + 3, :], in1=EBB[:, blk, j, c0:c0 + 3, :], op=ALU.mult),
                                         reads=[rPTB, Rc], writes=[rPTB])
                                yield 0.7
                            for (bank, lh_kind) in ((6, "v"), (7, "o")):
                                for blk in range(2):
                                    lh = VBP[:, blk, j, :] if lh_kind == "v" else ONESH[:, j * 128:(j + 1) * 128]
                                    S.op("tensor", L("matmul", PS[bank][:, 0:384], lhsT=lh, rhs=PTB[:, blk, c0:c0 + 3, :].rearrange("p a b -> p (a b)"),
                                                     start=(blk == 0), stop=(blk == 1)), reads=[rVBP, rPTB, Rc], writes=[RPS[bank]])
                            for cc in range(3):
                                c = c0 + cc
                                S.op("vector", L("tensor_scalar", out=RDB[r0:r1_, c * 128:(c + 1) * 128], in0=PS[7][r0:r1_, cc * 128:(cc + 1) * 128],
                                                 scalar1=ESK[r0:r1_, c:c + 1], scalar2=None, op0=ALU.add), reads=[RPS[7], Rc], writes=[rRDB])
                            S.op("vector", L("reciprocal", out=RDB[r0:r1_, c0 * 128:(c0 + 3) * 128], in_=RDB[r0:r1_, c0 * 128:(c0 + 3) * 128]), reads=[rRDB], writes=[rRDB])
                            S.op("vector", L("tensor_tensor", out=RDB[r0:r1_, c0 * 128:(c0 + 3) * 128], in0=PS[6][r0:r1_, 0:384], in1=RDB[r0:r1_, c0 * 128:(c0 + 3) * 128], op=ALU.mult),
                                 reads=[RPS[6], rRDB], writes=[rRDB])
                            S.op("vector", L("tensor_tensor", out=OBG[r0:r1_, c0:c0 + 3, :].rearrange("p a b -> p (a b)"), in0=RDB[r0:r1_, c0 * 128:(c0 + 3) * 128],
                                             in1=GB[r0:r1_, c0:c0 + 3, :].rearrange("p a b -> p (a b)"), op=ALU.mult), reads=[rRDB, rGB], writes=[rOBG])
                            yield 1.0
                    rOS[(2, i)] = S.res()
                    S.dma("gpsimd", OS[6:12, i].rearrange("c p t -> p c t"), OBG[:], reads=[rOBG], writes=[rOS[(2, i)]])
                    for mbk in range(2):
                        for h in range(4):
                            S.op("tensor", L("matmul", PS[5][:, h * 128:(h + 1) * 128], lhsT=KCT[:, h, mbk * 128:(mbk + 1) * 128], rhs=QCT[:, h, :],
                                             start=True, stop=True), reads=[rKC, rQCT], writes=[RPS[5]])
                        S.op("scalar", L("activation", out=PTC[:, mbk, :, :].rearrange("p a b -> p (a b)"), in_=PS[5][:, :], func=AF.Exp, scale=128.0 ** -0.5),
                             reads=[RPS[5]], writes=[rPTC])
                        yield 0.8
                    for h in range(4):
                        for mbk in range(2):
                            S.op("tensor", L("matmul", PS[6][:, h * 128:(h + 1) * 128], lhsT=VC[:, mbk, h * 128:(h + 1) * 128], rhs=PTC[:, mbk, h, :],
                                             start=(mbk == 0), stop=(mbk == 1)), reads=[rKC, rPTC], writes=[RPS[6]])
                    for mbk in range(2):
                        S.op("tensor", L("matmul", PS[7][:, :], lhsT=ONES, rhs=PTC[:, mbk, :, :].rearrange("p a b -> p (a b)"), start=(mbk == 0), stop=(mbk == 1)),
                             reads=[Rc, rPTC], writes=[RPS[7]])
                    S.op("vector", L("reciprocal", out=RDC[:], in_=PS[7][:, :]), reads=[RPS[7]], writes=[rRDC])
                    S.op("vector", L("tensor_tensor", out=RDC[:], in0=PS[6][:, :], in1=RDC[:], op=ALU.mult), reads=[RPS[6], rRDC], writes=[rRDC])
                    S.op("vector", L("tensor_tensor", out=OCG[:].rearrange("p a b -> p (a b)"), in0=RDC[:], in1=GC[:].rearrange("p a b -> p (a b)"), op=ALU.mult),
                         reads=[rRDC, rGC], writes=[rOCG])
                    rOS[(3, i)] = S.res()
                    S.dma("gpsimd", OS[12:16, i].rearrange("c p t -> p c t"), OCG[:], reads=[rOCG], writes=[rOS[(3, i)]])
                    yield 1.0

                def total_idx(i):
                    nsb = i + 1
                    return 1.0 + 0.5 * 16 * nsb + NIT * (0.5 + 8.6 * 512 * nsb / 8192.0) + 0.4 * nsb

                def total_att(i):
                    nkb = 4 * (i + 1)
                    return 2 * (1.0 + 1.1 * nkb + 2.0) + 2 * (2.0 + 2.0) + 3.0

                def drive(tasks):
                    prog = [0.0 for _ in tasks]
                    alive = [True for _ in tasks]
                    while any(alive):
                        k = min((q for q in range(len(tasks)) if alive[q]), key=lambda q: prog[q] / tasks[q][1])
                        try:
                            prog[k] += next(tasks[k][0])
                        except StopIteration:
                            alive[k] = False

                def run_scores(gi):
                    for w in gi:
                        if w < 0:
                            break

                def total_rest(i):
                    nsb = i + 1
                    return NIT * (0.5 + 8.6 * 512 * nsb / 8192.0) + 0.4 * nsb

                g0 = gen_idx(0)
                run_scores(g0)
                drive([(g0, total_rest(0))])
                for i in range(NQ):
                    tasks = [(gen_att(i), total_att(i)), (gen_bc(i), 14.0)]
                    if i + 1 < NQ:
                        gi = gen_idx(i + 1)
                        run_scores(gi)
                        tasks.append((gi, total_rest(i + 1)))
                    drive(tasks)
            S.barrier()
        mid.close()
        if "M" in phases:
            with contextlib.ExitStack() as es:
                WUP = SB(es, "WUP", [128, 16, D], BF16); rWUP = S.res()
                WO = SB(es, "WO", [128, KC, D], BF16); rWO = S.res()
                with contextlib.ExitStack() as es2:
                    stage = mk_stage(es2)
                    load_cast_w(stage, WUP[:, 0:6, :], wupd[0], D, None, rWUP, nk=6)
                    load_cast_w(stage, WUP[:, 6:12, :], wupd[1], D, None, rWUP, nk=6)
                    load_cast_w(stage, WUP[:, 12:16, :], wupd[2], D, None, rWUP, nk=4)
                    load_cast_w(stage, WO, wod, D, None, rWO)
                S.barrier()
                OG = SB(es, "OG", [128, 16, 512], BF16); rOG = S.res()
                GM = [SB(es, f"GM{q}", [128, 3, 512], BF16) for q in range(2)]; rGM = [S.res() for _ in range(2)]
                TT = [SB(es, f"TT{q}", [128, 3, 512], F32) for q in range(2)]; rTT = [S.res() for _ in range(2)]
                MT = SB(es, "MT", [128, 16, 512], BF16); rMT = S.res()
                XR = [SB(es, f"XR{q}", [128, D], F32) for q in range(2)]; rXR = [S.res() for _ in range(2)]
                nbr = (6, 6, 4); cbase = (0, 6, 12)
                for m in range(4):
                    for c in range(16):
                        grp = (0 if c < 3 else 1) if c < 6 else (2 if c < 12 else 3)
                        S.dma("sync", OG[:, c, :].rearrange("p (q t) -> p q t", q=4), OS[c, 4 * m:4 * m + 4].rearrange("q p t -> p q t"),
                              reads=[rOS[(grp, 4 * m + q)] for q in range(4)] if rOS else [], writes=[rOG])
                    for f in range(16):
                        gb = nxt("gm", 2)
                        for br in range(3):
                            ch = C_MIX + br * 16 + f
                            S.dma("sync", GM[gb][:, br, :].rearrange("p (q t) -> p q t", q=4), QS[ch, 4 * m:4 * m + 4].rearrange("q p t -> p q t"),
                                  reads=[rQS[(ch, m)]] if rQS else [], writes=[rGM[gb]])
                        for br in range(3):
                            for cc in range(nbr[br]):
                                c = cbase[br] + cc
                                S.op("tensor", L("matmul", PS[br][:, :], lhsT=WUP[:, c, f * 128:(f + 1) * 128], rhs=OG[:, c, :],
                                                                                        start=(cc == 0), stop=(cc == nbr[br] - 1)), reads=[rWUP, rOG], writes=[RPS[br]])
                        tb = nxt("tt", 2)
                        for br in range(3):
                            S.op("vector", L("tensor_tensor", out=TT[tb][:, br, :], in0=PS[br][:, :], in1=GM[gb][:, br, :], op=ALU.mult),
                                 reads=[RPS[br], rGM[gb]], writes=[rTT[tb]])
                        S.op("gpsimd", L("tensor_tensor", out=TT[tb][:, 0, :], in0=TT[tb][:, 0, :], in1=TT[tb][:, 1, :], op=ALU.add), reads=[rTT[tb]], writes=[rTT[tb]])
                        S.op("gpsimd", L("tensor_tensor", out=MT[:, f, :], in0=TT[tb][:, 0, :], in1=TT[tb][:, 2, :], op=ALU.add), reads=[rTT[tb]], writes=[rMT])
                    for q in range(4):
                        i = 4 * m + q
                        xb = nxt("xr", 2)
                        S.dma("sync", XR[xb][:], xk[(4 * i + 3) * 128:(4 * i + 4) * 128, :], writes=[rXR[xb]])
                        for n in range(4):
                            pb = 3 + nxt("py", 4)
                            for f in range(16):
                                S.op("tensor", L("matmul", PS[pb][:, :], lhsT=MT[:, f, q * 128:(q + 1) * 128], rhs=WO[:, f, n * 512:(n + 1) * 512],
                                                                                      start=(f == 0), stop=(f == 15)), reads=[rMT, rWO], writes=[RPS[pb]])
                            S.op("vector", L("tensor_tensor", out=XR[xb][:, n * 512:(n + 1) * 512], in0=PS[pb][:, :], in1=XR[xb][:, n * 512:(n + 1) * 512], op=ALU.add),
                                 reads=[RPS[pb], rXR[xb]], writes=[rXR[xb]])
                        S.dma("gpsimd", outd[i * 128:(i + 1) * 128, :], XR[xb][:], reads=[rXR[xb]], writes=[S.res()])
            S.barrier()
        if debug and "K" in phases:
            dl = [("CKVNT", CKVNT[:, 0, 0:2048], 0, 2048, rCKVNT), ("CKVN", CKVN[:, 5, :], 2048, 256, rCKVN),
                  ("IKT", IKT[:, 0:2048], 2304, 2048, rIKT), ("KBT", KBT[:, 3, :], 4352, 128, rKBT), ("VB", VB[:, 3, :], 4480, 128, rVB),
                  ("RK", RK[:].rearrange("p a b -> p (a b)"), 4608, 384, rRK), ("KCT", KCT[:].rearrange("p a b -> p (a b)"), 4992, 1024, rKC),
                  ("VC", VC[:].rearrange("p a b -> p (a b)"), 6016, 1024, rKC), ("WKT", WKT[:].rearrange("p a b -> p (a b)"), 7040, 1536, rKC)]
            with contextlib.ExitStack() as es:
                dt_ = SB(es, "dbgt", [128, 2048], F32); rd = S.res()
                for nm, ap, off, n, rr in dl:
                    S.op("vector", L("tensor_copy", out=dt_[:, 0:n], in_=ap), reads=[rr], writes=[rd])
                    out_ops.append(S.dma("sync", DBG[:, off:off + n], dt_[:, 0:n], reads=[rd], writes=[S.res()]))
        for e_ in ("gpsimd",):
            out_ops += [o for o in S.ops[e_] if o.dma]
        S.wait_all("sync", out_ops)
        S.emit()
    return nc


_NC_CACHE = {}


def kernel(**inputs):
    sh, per = prep_inputs(inputs)
    if "nc" not in _NC_CACHE:
        _NC_CACHE["nc"] = build(debug=False)
    nc = _NC_CACHE["nc"]
    res = run_bass_kernel_spmd(nc, [dict(sh, **per[c]) for c in range(8)], core_ids=list(range(8)))
    out = np.zeros((2, SEQ, D), np.float32)
    for c in range(8):
        b, j = c // 4, c % 4
        o = np.asarray(res.results[c]["out"], dtype=np.float32).reshape(NQ, 128, D)
        out[b].reshape(NB, 128, D)[j::4] = o
    return out
```

```python
import numpy as np
import concourse.bass as bass
import concourse.mybir as mybir
from concourse.bass_utils import run_bass_kernel_spmd

F32 = mybir.dt.float32
BF16 = mybir.dt.bfloat16
AF = mybir.ActivationFunctionType
ALU = mybir.AluOpType
AX = mybir.AxisListType


class Res:
    __slots__ = ("name", "w", "r")

    def __init__(self, name):
        self.name = name
        self.w = None
        self.r = {}


class Op:
    __slots__ = ("eng", "fn", "deps", "dma", "sig", "sem", "val", "idx")

    def __init__(self, eng, fn, deps, dma):
        self.eng = eng
        self.fn = fn
        self.deps = deps
        self.dma = dma
        self.sig = False
        self.sem = None
        self.val = 0


def L(name, *a, **k):
    return lambda e: getattr(e, name)(*a, **k)


class Sched:
    ENGS = ("sync", "scalar", "vector", "gpsimd", "tensor")
    NPOOL = 12

    def __init__(self, nc):
        self.nc = nc
        self.ops = {e: [] for e in self.ENGS}
        self.nres = 0

    def res(self, name=None):
        self.nres += 1
        return Res(name or f"r{self.nres}")

    def _deps(self, eng, reads, writes):
        deps = []
        for r in list(reads) + list(writes):
            if r.w is not None:
                deps.append(r.w)
        for w in writes:
            deps.extend(w.r.values())
        out = []
        seen = set()
        for d in deps:
            if id(d) in seen:
                continue
            seen.add(id(d))
            if d.eng == eng and eng == "tensor" and not d.dma:
                continue
            out.append(d)
        return out

    def _record(self, op, reads, writes):
        self.ops[op.eng].append(op)
        for r in reads:
            r.r[(op.eng, id(op)) if op.dma else (op.eng,)] = op
        for w in writes:
            w.w = op
            w.r = {}
        return op

    def op(self, eng, fn, reads=(), writes=()):
        o = Op(eng, fn, self._deps(eng, reads, writes), False)
        return self._record(o, reads, writes)

    def dma(self, eng, out, in_, reads=(), writes=(), **kw):
        o = Op(eng, L("dma_start", out=out, in_=in_, **kw),
               self._deps(eng, reads, writes), True)
        return self._record(o, reads, writes)

    def wait_all(self, eng, ops):
        o = Op(eng, None, list(ops), False)
        self.ops[eng].append(o)
        return o

    def barrier(self):
        lasts = []
        for e in self.ENGS:
            ops = self.ops[e]
            comp = [o for o in ops if (not o.dma) and o.fn is not None]
            if comp:
                lasts.append(comp[-1])
            lasts += [o for o in ops if o.dma][-self.NPOOL:]
        for e in self.ENGS:
            self.wait_all(e, lasts)

    def emit(self):
        nc = self.nc
        for e in self.ENGS:
            dmas = [o for o in self.ops[e] if o.dma]
            for k, o in enumerate(dmas):
                o.sig = True
                if k >= self.NPOOL:
                    o.deps.append(dmas[k - self.NPOOL])
        for e in self.ENGS:
            for o in self.ops[e]:
                for d in o.deps:
                    d.sig = True
        import contextlib
        with contextlib.ExitStack() as st:
            esem = {e: st.enter_context(nc.semaphore(f"s_{e}")) for e in self.ENGS}
            pools = {e: [st.enter_context(nc.semaphore(f"d_{e}{i}")) for i in range(self.NPOOL)]
                     for e in self.ENGS if any(o.dma for o in self.ops[e])}
            for e in self.ENGS:
                c = 0
                k = 0
                for o in self.ops[e]:
                    if o.dma:
                        o.sem = pools[e][k % self.NPOOL]
                        o.val = 16 * (k // self.NPOOL + 1)
                        k += 1
                    elif o.sig:
                        c += 1
                        o.sem = esem[e]
                        o.val = c
            block = st.enter_context(nc.Block())

            def run(e_name):
                def body(e):
                    waited = {}
                    for o in self.ops[e_name]:
                        for d in o.deps:
                            key = id(d.sem)
                            if waited.get(key, 0) < d.val:
                                e.wait_ge(d.sem, d.val)
                                waited[key] = d.val
                        if o.fn is None:
                            continue
                        inst = o.fn(e)
                        if o.sig:
                            inst.then_inc(o.sem, 16 if o.dma else 1)
                return body

            for e_name in self.ENGS:
                if self.ops[e_name]:
                    getattr(block, e_name)(run(e_name))


D = 2048
SEQ = 8192
NB = 64
NQ = 16
KC = 16
EPS = 1e-6
NIT = 19
LIM = 8.0
NEG = -1.0e30
O_AQ, O_CKV, O_IQ, O_IK, O_IW, O_AG, O_BQ, O_BK, O_BV, O_BG, O_CQ, O_CG, O_MIX = (
    0, 768, 1024, 2048, 2112, 2128, 2896, 3664, 3792, 3920, 4688, 5200, 5712)
KINDS = (["aq"] * 6 + ["cq"] * 4 + ["bq"] * 6 + ["iq"] * 8 + ["silu"] * 16 + ["mix"] * 48)
C_AQ, C_CQ, C_BQ, C_IQ, C_AG, C_BG, C_CG, C_MIX = 0, 6, 10, 16, 24, 30, 36, 40
NCH = 88


def _bucket(dist):
    n = np.maximum(dist, 0)
    nf = np.maximum(n, 1).astype(np.float32)
    large = 16 + (np.log(nf / np.float32(16)) / np.float32(np.log(128 / 16)) * np.float32(16)).astype(np.int32)
    large = np.minimum(large, 31)
    return np.where(n < 16, n, large)


def _bperm():
    idx = []
    for c in range(6):
        idx += list(range(c * 64, c * 64 + 64)) + list(range((6 + c) * 64, (6 + c) * 64 + 64))
    return np.array(idx)


def prep_inputs(inp):
    f = lambda a: np.ascontiguousarray(np.asarray(a, dtype=np.float32))
    x = f(inp["x"]); mem = f(inp["mem"]); w_in = f(inp["w_in"])[0]
    bp = _bperm()
    cols = np.concatenate([
        np.arange(O_AQ, O_AQ + 768), np.arange(O_CQ, O_CQ + 512), O_BQ + bp,
        np.arange(O_IQ, O_IQ + 1024), np.arange(O_AG, O_AG + 768), O_BG + bp,
        np.arange(O_CG, O_CG + 512), np.arange(O_MIX, O_MIX + 6144)])
    sh = {}
    sh["wq"] = f(w_in[:, cols])
    sh["wiw"] = f(w_in[:, O_IW:O_IW + 16])
    kcols = np.concatenate([np.arange(O_CKV, O_CKV + 256), np.arange(O_BK, O_BK + 128),
                            np.arange(O_BV, O_BV + 128), np.arange(O_IK, O_IK + 64)])
    sh["wk"] = f(w_in[:, kcols])
    pk = lambda v: f(np.asarray(v).reshape(-1, 128).T)
    sh["normg"] = pk(inp["norm_g"][0]); sh["memg"] = pk(inp["mem_norm_g"][0])
    sh["wmem"] = f(inp["w_mem_kv"][0])
    wkv = f(inp["w_kv_up"][0])
    sh["wkvu"] = wkv
    sh["wkT"] = f(wkv[:, :768].reshape(256, 6, 128).transpose(2, 1, 0))
    sh["kvg"] = pk(inp["kv_norm_g"][0])
    sh["gkvbc"] = f(np.broadcast_to(np.asarray(inp["kv_norm_g"][0])[None, :], (128, 256)))
    col = lambda v: f(np.asarray(v).reshape(128, 1))
    t2 = lambda v: np.concatenate([np.asarray(v), np.asarray(v)])
    sh["gv"] = f(np.concatenate([col(inp["q_norm_a"][0]), col(inp["k_norm_a"][0]),
                                 col(t2(inp["q_norm_b"][0])), col(t2(inp["k_norm_b"][0])),
                                 col(inp["q_norm_c"][0]), col(inp["k_norm_c"][0])], axis=1))
    sh["lng"] = f(np.broadcast_to(np.asarray(inp["idx_k_ln_g"][0])[None, :], (128, 64)))
    sh["lnb"] = f(np.broadcast_to(np.asarray(inp["idx_k_ln_b"][0])[None, :], (128, 64)))
    sh["gbias"] = pk(inp["gate_bias"][0])
    sk = np.asarray(inp["sinks_b"][0])
    sh["sk"] = f(np.concatenate([np.broadcast_to(sk[None, 0:6], (64, 6)),
                                 np.broadcast_to(sk[None, 6:12], (64, 6))], axis=0))
    rb = np.asarray(inp["rel_bias"], dtype=np.float32)
    s = np.arange(128)[:, None]; t = np.arange(128)[None, :]
    d0 = t - s; d1 = 128 + t - s
    bk = np.stack([_bucket(d0), _bucket(d1)], axis=1)
    sh["ba"] = f(rb[bk][:, :, :, :6].transpose(0, 1, 3, 2))
    sh["ca"] = f(np.broadcast_to(rb[31, :6][None, None, :, None], (128, 2, 6, 128)))
    sh["va"] = f(np.stack([(d0 >= 0), np.ones_like(d0, bool)], axis=1))
    bkb = np.stack([_bucket(d1), _bucket(d0)], axis=1)
    sh["bb"] = f(rb[bkb][:, :, :, 6:18].reshape(128, 2, 128, 2, 6).transpose(0, 1, 3, 4, 2))
    sh["vb01"] = f(np.stack([(d1 < 128), (d0 >= 0)], axis=1))
    sh["cmask"] = f(np.where(np.arange(128)[None, :] <= np.arange(128)[:, None], 0.0, NEG))
    ident = np.eye(128, dtype=np.float32)
    blk64 = np.kron(np.eye(2, dtype=np.float32), np.ones((64, 64), np.float32))
    onesh = np.zeros((128, 2, 128), np.float32); onesh[:, 0, :64] = 1; onesh[:, 1, 64:] = 1
    sh["cst"] = f(np.concatenate([ident, np.ones((128, 128), np.float32), blk64,
                                  onesh.reshape(128, 256), np.tile(ident * 32768.0, (1, 3))], axis=1))
    sh["wupa"] = f(inp["w_up_a"][0]); sh["wupb"] = f(inp["w_up_b"][0][bp]); sh["wupc"] = f(inp["w_up_c"][0])
    sh["wo"] = f(inp["w_o"][0])
    per = []
    for c in range(8):
        b, j = c // 4, c % 4
        npad = (3 - j) * 128
        xk = np.zeros((SEQ, D), np.float32)
        xk[npad:] = x[b, :SEQ - npad]
        padm = np.zeros((128, 384), np.float32); padm[:, :npad] = NEG
        bpv = np.full((128, 1), 0.0 if j == 0 else 1.0, np.float32)
        per.append({"xk": xk, "mem": f(mem[b]), "padm": padm, "bpv": bpv})
    return sh, per


def build(debug=False, phases="QKAM"):
    import contextlib
    nc = bass.Bass("TRN2", target_bir_lowering=False)
    dbg_kind = "ExternalOutput" if debug else "Internal"

    def din(name, shape, dt=F32):
        return nc.dram_tensor(name, list(shape), dt, kind="ExternalInput").ap()

    xk = din("xk", [SEQ, D]); memd = din("mem", [256, D])
    wq = din("wq", [D, NCH * 128]); wiw = din("wiw", [D, 16]); wk = din("wk", [D, 576])
    normg = din("normg", [128, 16]); memg = din("memg", [128, 16]); wmem = din("wmem", [D, 1024])
    wkvu = din("wkvu", [256, 1536]); wkTd = din("wkT", [128, 6, 256]); kvg = din("kvg", [128, 2])
    gkvbc = din("gkvbc", [128, 256]); gvd = din("gv", [128, 6]); lngd = din("lng", [128, 64]); lnbd = din("lnb", [128, 64])
    gbiasd = din("gbias", [128, 48]); skd = din("sk", [128, 6])
    bad = din("ba", [128, 2, 6, 128]); cad = din("ca", [128, 2, 6, 128]); vad = din("va", [128, 2, 128])
    bbd = din("bb", [128, 2, 2, 6, 128]); vb01d = din("vb01", [128, 2, 128]); cmaskd = din("cmask", [128, 128])
    cstd = din("cst", [128, 1024]); padmd = din("padm", [128, 384]); bpvd = din("bpv", [128, 1])
    wupd = [din("wupa", [768, D]), din("wupb", [768, D]), din("wupc", [512, D])]
    wod = din("wo", [D, D])
    outd = nc.dram_tensor("out", [NQ * 128, D], F32, kind="ExternalOutput").ap()
    QS = nc.dram_tensor("QS", [NCH, NQ, 128, 128], BF16, kind=dbg_kind).ap()
    IWS = nc.dram_tensor("IWS", [NQ, 16, 128], BF16, kind=dbg_kind).ap()
    OS = nc.dram_tensor("OS", [16, NQ, 128, 128], BF16, kind=dbg_kind).ap()
    DBG = nc.dram_tensor("DBG", [128, 16384], F32, kind=dbg_kind).ap() if debug else None

    S = Sched(nc)
    top = contextlib.ExitStack()
    with top:
        uid = [0]

        def SB(es, name, shape, dt):
            uid[0] += 1
            return es.enter_context(nc.sbuf_tensor(f"{name}_{uid[0]}", list(shape), dt))

        PS = [top.enter_context(nc.psum_tensor(f"ps{b}", [128, 512], F32)) for b in range(8)]
        RPS = [S.res(f"ps{b}") for b in range(8)]
        PSB = PS[7][:].bitcast(BF16)

        CST = SB(top, "CST", [128, 1024], BF16)
        NEGH = SB(top, "NEGH", [128, 8], F32)
        GV = SB(top, "GV", [128, 8], F32)
        KCT = SB(top, "KCT", [128, 4, 256], BF16)
        VC = SB(top, "VC", [128, 2, 512], BF16)
        Rc = S.res("const")
        IDENT = CST[:, 0:128]; ONES = CST[:, 128:256]; BLK64 = CST[:, 256:384]
        ONESH = CST[:, 384:640]; IREP = CST[:, 640:1024]
        with contextlib.ExitStack() as es:
            stg = SB(es, "cstg", [128, 1024], F32)
            r = S.res()
            S.dma("sync", stg[:], cstd[:, :], writes=[r])
            S.op("vector", L("tensor_copy", out=CST[:], in_=stg[:]), reads=[r], writes=[Rc])
            S.op("vector", L("memset", NEGH[:], -0.5), writes=[Rc])
            S.dma("sync", GV[:, 0:6], gvd[:, :], reads=[], writes=[Rc])
            S.op("vector", L("tensor_tensor", out=GV[:, 6:7], in0=GV[:, 2:3], in1=GV[:, 3:4], op=ALU.mult), reads=[Rc], writes=[Rc])
            S.op("vector", L("tensor_tensor", out=GV[:, 7:8], in0=GV[:, 4:5], in1=GV[:, 5:6], op=ALU.mult), reads=[Rc], writes=[Rc])

        rot = {}

        def nxt(key, n):
            rot[key] = (rot.get(key, -1) + 1) % n
            return rot[key]

        def pow_rstd(out_ap, in_ap, rres, wres, n):
            shp = in_ap.shape
            S.op("gpsimd", L("tensor_tensor", out=out_ap, in0=in_ap, in1=NEGH[:shp[0], 0:n], op=ALU.pow),
                 reads=rres + [Rc], writes=wres)

        class NT:
            def __init__(self, es):
                self.xs = [SB(es, f"nt_xs{i}", [128, D], F32) for i in range(2)]
                self.hb = [SB(es, f"nt_hb{i}", [128, D], BF16) for i in range(2)]
                self.st = [SB(es, f"nt_st{i}", [128, 4], F32) for i in range(2)]
                self.rx = [S.res() for _ in range(2)]; self.rh = [S.res() for _ in range(2)]
                self.rs = [S.res() for _ in range(2)]

            def run(self, src_rows, dst, rdst):
                b = nxt("nt", 2)
                xs, hb, st = self.xs[b], self.hb[b], self.st[b]
                rx, rh, rs = self.rx[b], self.rh[b], self.rs[b]
                S.dma("sync", xs[:], src_rows, writes=[rx])
                S.op("vector", L("scalar_tensor_tensor", out=hb[:], in0=xs[:], scalar=1.0, in1=xs[:], op0=ALU.mult,
                                                                op1=ALU.mult, accum_out=st[:, 0:1]), reads=[rx], writes=[rh, rs])
                S.op("vector", L("tensor_scalar", out=st[:, 1:2], in0=st[:, 0:1], scalar1=1.0 / D, scalar2=EPS,
                                                         op0=ALU.mult, op1=ALU.add), reads=[rs], writes=[rs])
                pow_rstd(st[:, 2:3], st[:, 1:2], [rs], [rs], 1)
                S.op("vector", L("tensor_scalar", out=hb[:], in0=xs[:], scalar1=st[:, 2:3], scalar2=None, op0=ALU.mult),
                     reads=[rx, rs], writes=[rh])
                for half in range(2):
                    for k in range(8):
                        kk = half * 8 + k
                        S.op("tensor", L("transpose", out=PSB[:, k * 128:(k + 1) * 128], in_=hb[:, kk * 128:(kk + 1) * 128],
                                                                         identity=IDENT), reads=[rh, Rc], writes=[RPS[7]])
                    S.op("scalar", L("copy", out=dst[:, half * 8:(half + 1) * 8, :],
                                                               in_=PSB.rearrange("p (k t) -> p k t", k=8)),
                         reads=[RPS[7]], writes=[rdst])

        def load_cast_w(es_stage, dst, src_dram_rows, ncols, gscale, rdst, nk=KC, chunkc=256, eng="vector"):
            stg = es_stage["stg"]; rstg = es_stage["rstg"]
            for c0 in range(0, ncols, chunkc):
                cw = min(chunkc, ncols - c0)
                b = nxt("wstg", 2)
                S.dma("sync", stg[b][:, 0:nk, 0:cw], src_dram_rows[:, c0:c0 + cw].rearrange("(k p) c -> p k c", p=128), writes=[rstg[b]])
                for k in range(nk):
                    if gscale is None:
                        S.op(eng, L("tensor_copy", out=dst[:, k, c0:c0 + cw], in_=stg[b][:, k, 0:cw]),
                             reads=[rstg[b]], writes=[rdst])
                    else:
                        S.op(eng, L("tensor_scalar", out=dst[:, k, c0:c0 + cw], in0=stg[b][:, k, 0:cw],
                                                                                   scalar1=gscale[:, k:k + 1], scalar2=None, op0=ALU.mult),
                             reads=[rstg[b], Rc], writes=[rdst])

        def mk_stage(es):
            return {"stg": [SB(es, f"wstg{i}", [128, KC, 256], F32) for i in range(2)], "rstg": [S.res() for _ in range(2)]}

        out_ops = []
        rKC = S.res()
        rQS = {}; rIWS = {}; rOS = {}

        if "Q" in phases:
            with contextlib.ExitStack() as es:
                HT = SB(es, "HT", [128, KC, NQ * 128], BF16); rHT = S.res()
                GN = SB(es, "GN", [128, 16], F32); MG = SB(es, "MG", [128, 16], F32); GBI = SB(es, "GBI", [128, 48], F32)
                S.dma("sync", GN[:], normg[:, :], writes=[Rc]); S.dma("sync", MG[:], memg[:, :], writes=[Rc])
                S.dma("sync", GBI[:], gbiasd[:, :], writes=[Rc])
                nt = NT(es)
                stage = mk_stage(es)
                SQ = [SB(es, f"SQ{i}", [128, 512], BF16) for i in range(2)]; rSQ = [S.res() for _ in range(2)]
                T1 = [SB(es, f"T1{i}", [128, 512], F32) for i in range(2)]; rT1 = [S.res() for _ in range(2)]
                T2 = [SB(es, f"T2{i}", [128, 512], F32) for i in range(2)]; rT2 = [S.res() for _ in range(2)]
                OT = [SB(es, f"OT{i}", [128, 512], BF16) for i in range(3)]; rOT = [S.res() for _ in range(3)]
                ST4 = SB(es, "ST4", [128, 16], F32); rST4 = S.res()
                with contextlib.ExitStack() as es2:
                    WM = SB(es2, "WM", [128, KC, 1024], BF16); rWM = S.res()
                    HTM = SB(es2, "HTM", [128, KC, 128], BF16); rHTM = S.res()
                    KCN = SB(es2, "KCN", [128, 512], BF16); rKCN = S.res()
                    load_cast_w(stage, WM, wmem, 1024, MG, rWM)
                    for mb in range(2):
                        nt.run(memd[mb * 128:(mb + 1) * 128, :], HTM, rHTM)
                        for n in range(2):
                            for k in range(KC):
                                S.op("tensor", L("matmul", PS[n][:, :], lhsT=HTM[:, k, :], rhs=WM[:, k, n * 512:(n + 1) * 512],
                                                                            start=(k == 0), stop=(k == KC - 1)), reads=[rHTM, rWM], writes=[RPS[n]])
                        for h in range(4):
                            S.op("scalar", L("activation", out=T1[0][:, 0:128], in_=PS[0][:, h * 128:(h + 1) * 128], func=AF.Square,
                                                                       accum_out=ST4[:, h:h + 1]), reads=[RPS[0]], writes=[rT1[0], rST4])
                        S.op("vector", L("tensor_scalar", out=ST4[:, 4:8], in0=ST4[:, 0:4], scalar1=1.0 / 128, scalar2=EPS, op0=ALU.mult, op1=ALU.add),
                             reads=[rST4], writes=[rST4])
                        pow_rstd(ST4[:, 8:12], ST4[:, 4:8], [rST4], [rST4], 4)
                        for h in range(4):
                            S.op("vector", L("tensor_scalar", out=KCN[:, h * 128:(h + 1) * 128], in0=PS[0][:, h * 128:(h + 1) * 128],
                                                                          scalar1=ST4[:, 8 + h:9 + h], scalar2=None, op0=ALU.mult),
                                 reads=[RPS[0], rST4], writes=[rKCN])
                        for h in range(4):
                            S.op("tensor", L("transpose", out=PSB[:, h * 128:(h + 1) * 128], in_=KCN[:, h * 128:(h + 1) * 128], identity=IDENT),
                                 reads=[rKCN, Rc], writes=[RPS[7]])
                        S.op("scalar", L("copy", out=KCT[:, :, mb * 128:(mb + 1) * 128], in_=PSB[:, 0:512].rearrange("p (h t) -> p h t", h=4)),
                             reads=[RPS[7]], writes=[rKC])
                        S.op("scalar", L("copy", out=VC[:, mb, :], in_=PS[1][:, :]), reads=[RPS[1]], writes=[rKC])
                S.barrier()
                for i in range(NQ):
                    p = 4 * i + 3
                    nt.run(xk[p * 128:(p + 1) * 128, :], HT[:, :, i * 128:(i + 1) * 128], rHT)
                WB = [SB(es, f"WB{i}", [128, KC, 256], BF16) for i in range(2)]; rWB = [S.res() for _ in range(2)]
                for g in range(NCH // 2):
                    b = nxt("wb", 2)
                    load_cast_w(stage, WB[b], wq[:, g * 256:(g + 1) * 256], 256, GN, rWB[b])
                    for cc in range(2):
                        ch = 2 * g + cc
                        kind = KINDS[ch]
                        for m in range(4):
                            pb = nxt("qacc", 3)
                            for k in range(KC):
                                S.op("tensor", L("matmul", PS[pb][:, :], lhsT=WB[b][:, k, cc * 128:(cc + 1) * 128], rhs=HT[:, k, m * 512:(m + 1) * 512],
                                    start=(k == 0), stop=(k == KC - 1)), reads=[rWB[b], rHT], writes=[RPS[pb]])
                            ob = nxt("ot", 3)
                            if kind == "iq":
                                S.op("scalar", L("copy", out=OT[ob][:], in_=PS[pb][:, :]), reads=[RPS[pb]], writes=[rOT[ob]])
                            elif kind == "silu":
                                S.op("scalar", L("activation", out=OT[ob][:], in_=PS[pb][:, :], func=AF.Silu),
                                     reads=[RPS[pb]], writes=[rOT[ob]])
                            elif kind == "mix":
                                S.op("scalar", L("activation", out=OT[ob][:], in_=PS[pb][:, :], func=AF.Sigmoid,
                                                                                          bias=GBI[:, ch - C_MIX:ch - C_MIX + 1]),
                                     reads=[RPS[pb], Rc], writes=[rOT[ob]])
                            else:
                                dd = 64 if kind == "bq" else 128
                                gcol = {"aq": 0, "cq": 7, "bq": 6}[kind]
                                lh = BLK64 if kind == "bq" else ONES
                                sb_ = nxt("sq", 2)
                                sbank = 3 + nxt("ss", 2)
                                S.op("scalar", L("activation", out=SQ[sb_][:], in_=PS[pb][:, :], func=AF.Square),
                                     reads=[RPS[pb]], writes=[rSQ[sb_]])
                                S.op("tensor", L("matmul", PS[sbank][:, :], lhsT=lh, rhs=SQ[sb_][:], start=True, stop=True),
                                     reads=[rSQ[sb_], Rc], writes=[RPS[sbank]])
                                S.op("scalar", L("activation", out=T1[sb_][:], in_=PS[sbank][:, :], func=AF.Sqrt, bias=EPS, scale=1.0 / dd),
                                     reads=[RPS[sbank]], writes=[rT1[sb_]])
                                S.op("vector", L("reciprocal", out=T2[sb_][:], in_=T1[sb_][:]), reads=[rT1[sb_]], writes=[rT2[sb_]])
                                S.op("vector", L("scalar_tensor_tensor", out=OT[ob][:], in0=PS[pb][:, :], scalar=GV[:, gcol:gcol + 1], in1=T2[sb_][:], op0=ALU.mult, op1=ALU.mult),
                                    reads=[RPS[pb], rT2[sb_], Rc], writes=[rOT[ob]])
                            rQS[(ch, m)] = S.res()
                            S.dma("gpsimd", QS[ch, 4 * m:4 * m + 4].rearrange("q p t -> p q t"), OT[ob][:].rearrange("p (q t) -> p q t", q=4),
                                  reads=[rOT[ob]], writes=[rQS[(ch, m)]])
                WIW = SB(es, "WIW", [128, KC, 16], BF16); rWIW = S.res()
                OTW = SB(es, "OTW", [16, 512], BF16); rOTW = S.res()
                load_cast_w(stage, WIW, wiw, 16, GN, rWIW)
                for m in range(4):
                    pb = nxt("qacc", 3)
                    for k in range(KC):
                        S.op("tensor", L("matmul", PS[pb][0:16, :], lhsT=WIW[:, k, :], rhs=HT[:, k, m * 512:(m + 1) * 512],
                                                                           start=(k == 0), stop=(k == KC - 1)), reads=[rWIW, rHT], writes=[RPS[pb]])
                    S.op("scalar", L("mul", out=OTW[:], in_=PS[pb][0:16, :], mul=1.0 / 32.0), reads=[RPS[pb]], writes=[rOTW])
                    rIWS[m] = S.res()
                    S.dma("gpsimd", IWS[4 * m:4 * m + 4].rearrange("q h t -> h q t"), OTW[:].rearrange("h (q t) -> h q t", q=4),
                          reads=[rOTW], writes=[rIWS[m]])

        S.barrier()
        mid = contextlib.ExitStack()
        top.enter_context(mid)
        CKVNT = SB(mid, "CKVNT", [128, 2, SEQ], BF16); rCKVNT = S.res()
        CKVN = SB(mid, "CKVN", [128, NB, 256], BF16); rCKVN = S.res()
        IKT = SB(mid, "IKT", [128, SEQ], BF16); rIKT = S.res()
        KBT = SB(mid, "KBT", [128, 32, 128], BF16); rKBT = S.res()
        VB = SB(mid, "VB", [128, 32, 128], BF16); rVB = S.res()
        RK = SB(mid, "RK", [128, NB, 6], F32); rRK = S.res()
        WV = SB(mid, "WV", [128, 2, 768], BF16)
        WKT = SB(mid, "WKT", [128, 6, 256], BF16)

        if "K" in phases:
            with contextlib.ExitStack() as es:
                nt = NT(es)
                stage = mk_stage(es)
                GN = SB(es, "GNk", [128, 16], F32); KVG = SB(es, "KVG", [128, 2], F32)
                LNG = SB(es, "LNG", [128, 64], F32); LNB = SB(es, "LNB", [128, 64], F32); GKV = SB(es, "GKV", [128, 256], F32)
                S.dma("sync", GN[:], normg[:, :], writes=[Rc]); S.dma("sync", KVG[:], kvg[:, :], writes=[Rc])
                S.dma("sync", LNG[:], lngd[:, :], writes=[Rc]); S.dma("sync", LNB[:], lnbd[:, :], writes=[Rc])
                S.dma("sync", GKV[:], gkvbc[:, :], writes=[Rc])
                WKb = SB(es, "WKb", [128, KC, 576], BF16); rWKb = S.res()
                WKUK = SB(es, "WKUK", [128, 2, 768], BF16); rWKU = S.res()
                load_cast_w(stage, WKb, wk, 576, GN, rWKb, chunkc=192)
                load_cast_w(stage, WKUK, wkvu[:, 0:768], 768, KVG, rWKU, nk=2)
                load_cast_w(stage, WV, wkvu[:, 768:1536], 768, KVG, rWKU, nk=2)
                WT32 = SB(es, "WT32", [128, 6, 256], F32); rWT = S.res()
                S.dma("sync", WT32[:], wkTd[:, :, :], writes=[rWT])
                for h in range(6):
                    S.op("vector", L("scalar_tensor_tensor", out=WKT[:, h, :], in0=WT32[:, h, :], scalar=GV[:, 1:2], in1=GKV[:],
                                                                         op0=ALU.mult, op1=ALU.mult), reads=[rWT, Rc], writes=[rWKU])
                HTB = [SB(es, f"HTB{i}", [128, KC, 128], BF16) for i in range(2)]; rHTB = [S.res() for _ in range(2)]
                JK = SB(es, "JK", [128, 256], BF16); rJK = S.res()
                SK_ = [SB(es, f"SKt{i}", [128, 32], F32) for i in range(2)]; rSK = [S.res() for _ in range(2)]
                dbgK = (SK_, rSK, JK, rJK)
                IK32 = SB(es, "IK32", [128, 64], F32); rIK32 = S.res()
                IKD = SB(es, "IKD", [128, 128], BF16); rIKD = S.res()
                KBD = SB(es, "KBD", [128, 128], BF16); rKBD = S.res()
                PSB6 = PS[6][:].bitcast(BF16)

                def k_A(p):
                        b = p % 2
                        kA, kB = (0, 1) if p % 2 == 0 else (4, 5)
                        hT = HTB[b]; rh = rHTB[b]
                        nt.run(xk[p * 128:(p + 1) * 128, :], hT, rh)

                def k_B(p):
                        b = p % 2
                        kA, kB = (0, 1) if p % 2 == 0 else (4, 5)
                        hT = HTB[b]; rh = rHTB[b]
                        for k in range(KC):
                            S.op("tensor", L("matmul", PS[kA][:, :], lhsT=hT[:, k, :], rhs=WKb[:, k, 0:512], start=(k == 0), stop=(k == KC - 1)),
                                 reads=[rh, rWKb], writes=[RPS[kA]])
                        for k in range(KC):
                            S.op("tensor", L("matmul", PS[kB][:, 0:64], lhsT=hT[:, k, :], rhs=WKb[:, k, 512:576], start=(k == 0), stop=(k == KC - 1)),
                                 reads=[rh, rWKb], writes=[RPS[kB]])


                def k_C1a(p):
                        sb_ = p % 2
                        kA, kB = (0, 1) if p % 2 == 0 else (4, 5)
                        if p not in RP:
                            RP[p] = tuple(S.res() for _ in range(6))
                        rCKVNp, rCKVNTp, rIKTp, rKBTp, rVBp, rRKp = RP[p]
                        st = SK_[sb_]; rs = rSK[sb_]
                        S.op("scalar", L("activation", out=JK[:], in_=PS[kA][:, 0:256], func=AF.Square, accum_out=st[:, 0:1]),
                             reads=[RPS[kA]], writes=[rs])
                        S.op("vector", L("tensor_scalar", out=st[:, 1:2], in0=st[:, 0:1], scalar1=1.0 / 256, scalar2=EPS, op0=ALU.mult, op1=ALU.add),
                             reads=[rs], writes=[rs])
                        pow_rstd(st[:, 2:3], st[:, 1:2], [rs], [rs], 1)
                        S.op("vector", L("tensor_scalar", out=CKVN[:, p, :], in0=PS[kA][:, 0:256], scalar1=st[:, 2:3], scalar2=None, op0=ALU.mult),
                             reads=[RPS[kA], rs], writes=[rCKVNp])
                        for rc in range(2):
                            S.op("tensor", L("transpose", out=PSB6[:, rc * 128:(rc + 1) * 128], in_=CKVN[:, p, rc * 128:(rc + 1) * 128], identity=IDENT),
                                 reads=[rCKVNp, Rc], writes=[RPS[6]])

                def k_C1b(p):
                        sb_ = p % 2
                        kA, kB = (0, 1) if p % 2 == 0 else (4, 5)
                        if p not in RP:
                            RP[p] = tuple(S.res() for _ in range(6))
                        rCKVNp, rCKVNTp, rIKTp, rKBTp, rVBp, rRKp = RP[p]
                        st = SK_[sb_]; rs = rSK[sb_]
                        S.op("scalar", L("copy", out=CKVNT[:, :, p * 128:(p + 1) * 128], in_=PSB6[:, 0:256].rearrange("q (c t) -> q c t", c=2)),
                             reads=[RPS[6]], writes=[rCKVNTp])
                        for n, (c0, cw) in enumerate(((0, 512), (512, 256))):
                            for rc in range(2):
                                S.op("tensor", L("matmul", PS[2 + n][:, 0:cw], lhsT=CKVNT[:, rc, p * 128:(p + 1) * 128],
                                                                                               rhs=WKUK[:, rc, c0:c0 + cw], start=(rc == 0), stop=(rc == 1)),
                                     reads=[rCKVNTp, rWKU], writes=[RPS[2 + n]])


                def k_C2(p):
                        sb_ = p % 2
                        kA, kB = (0, 1) if p % 2 == 0 else (4, 5)
                        if p not in RP:
                            RP[p] = tuple(S.res() for _ in range(6))
                        rCKVNp, rCKVNTp, rIKTp, rKBTp, rVBp, rRKp = RP[p]
                        st = SK_[sb_]; rs = rSK[sb_]
                        S.op("vector", L("tensor_scalar", out=JK[:, 0:64], in0=PS[kB][:, 0:64], scalar1=1.0, scalar2=None, op0=ALU.mult, op1=ALU.add,
                                                                        accum_out=st[:, 16:17]), reads=[RPS[kB]], writes=[rs])
                        S.op("scalar", L("activation", out=JK[:, 0:64], in_=PS[kB][:, 0:64], func=AF.Square, accum_out=st[:, 17:18]),
                             reads=[RPS[kB]], writes=[rs])
                        S.op("vector", L("tensor_scalar", out=st[:, 18:20], in0=st[:, 16:18], scalar1=1.0 / 64, scalar2=None, op0=ALU.mult),
                             reads=[rs], writes=[rs])
                        S.op("vector", L("tensor_tensor", out=st[:, 20:21], in0=st[:, 18:19], in1=st[:, 18:19], op=ALU.mult), reads=[rs], writes=[rs])
                        S.op("vector", L("scalar_tensor_tensor", out=st[:, 21:22], in0=st[:, 19:20], scalar=EPS, in1=st[:, 20:21], op0=ALU.add,
                                                                               op1=ALU.subtract), reads=[rs], writes=[rs])
                        pow_rstd(st[:, 22:23], st[:, 21:22], [rs], [rs], 1)
                        S.op("vector", L("tensor_scalar", out=IK32[:], in0=PS[kB][:, 0:64], scalar1=st[:, 18:19], scalar2=st[:, 22:23],
                                                                        op0=ALU.subtract, op1=ALU.mult), reads=[RPS[kB], rs], writes=[rIK32])
                        S.op("vector", L("tensor_tensor", out=IK32[:], in0=IK32[:], in1=LNG[:], op=ALU.mult), reads=[rIK32, Rc], writes=[rIK32])
                        S.op("vector", L("tensor_tensor", out=IKD[:, 0:64], in0=IK32[:], in1=LNB[:], op=ALU.add), reads=[rIK32, Rc], writes=[rIKD])
                        S.op("vector", L("tensor_tensor", out=IKD[:, 64:128], in0=IK32[:], in1=LNB[:], op=ALU.add), reads=[rIK32, Rc], writes=[rIKD])
                        S.op("tensor", L("transpose", out=PSB6[:, 256:384], in_=IKD[:], identity=IDENT), reads=[rIKD, Rc], writes=[RPS[6]])
                        S.op("scalar", L("copy", out=IKT[:, p * 128:(p + 1) * 128], in_=PSB6[:, 256:384]), reads=[RPS[6]], writes=[rIKTp])
                        if p % 4 >= 2:
                            slot = (p // 4) * 2 + (p % 4 - 2)
                            for j in range(2):
                                S.op("scalar", L("activation", out=JK[:, 0:64], in_=PS[kA][:, 256 + j * 64:320 + j * 64], func=AF.Square,
                                    accum_out=st[:, 24 + j:25 + j]), reads=[RPS[kA]], writes=[rs])
                            S.op("vector", L("tensor_scalar", out=st[:, 26:28], in0=st[:, 24:26], scalar1=1.0 / 64, scalar2=EPS, op0=ALU.mult, op1=ALU.add),
                                 reads=[rs], writes=[rs])
                            pow_rstd(st[:, 28:30], st[:, 26:28], [rs], [rs], 2)
                            for j in range(2):
                                S.op("vector", L("tensor_scalar", out=KBD[:, j * 64:(j + 1) * 64], in0=PS[kA][:, 256 + j * 64:320 + j * 64],
                                                                                     scalar1=st[:, 28 + j:29 + j], scalar2=None, op0=ALU.mult),
                                     reads=[RPS[kA], rs], writes=[rKBD])
                            S.op("tensor", L("transpose", out=PSB6[:, 384:512], in_=KBD[:], identity=IDENT), reads=[rKBD, Rc], writes=[RPS[6]])
                            S.op("scalar", L("copy", out=KBT[:, slot, :], in_=PSB6[:, 384:512]), reads=[RPS[6]], writes=[rKBTp])
                            S.op("scalar", L("copy", out=VB[:, slot, :], in_=PS[kA][:, 384:512]), reads=[RPS[kA]], writes=[rVBp])

                def k_T(p):
                        sb_ = p % 2
                        kA, kB = (0, 1) if p % 2 == 0 else (4, 5)
                        if p not in RP:
                            RP[p] = tuple(S.res() for _ in range(6))
                        rCKVNp, rCKVNTp, rIKTp, rKBTp, rVBp, rRKp = RP[p]
                        st = SK_[sb_]; rs = rSK[sb_]
                        for h in range(6):
                            bank = 2 + (h // 4); off = (h % 4) * 128
                            S.op("scalar", L("activation", out=JK[:, 0:128], in_=PS[bank][:, off:off + 128], func=AF.Square,
                                accum_out=st[:, 4 + h:5 + h]), reads=[RPS[bank]], writes=[rs])
                        S.op("vector", L("tensor_scalar", out=st[:, 10:16], in0=st[:, 4:10], scalar1=128.0 * EPS, scalar2=None, op0=ALU.add),
                             reads=[rs], writes=[rs])
                        S.op("gpsimd", L("tensor_tensor", out=RK[:, p, :], in0=st[:, 10:16], in1=NEGH[:, 0:6], op=ALU.pow),
                             reads=[rs, Rc], writes=[rRKp])


                RP = {}
                k_A(0); k_B(0); k_A(1)
                for p in range(NB):
                    if p >= 1:
                        k_T(p - 1)
                    k_C1a(p)
                    if p + 1 < NB:
                        k_B(p + 1)
                    k_C2(p)
                    k_C1b(p)
                    if p + 2 < NB:
                        k_A(p + 2)
                k_T(NB - 1)
                if debug:
                    for q_ in range(2):
                        out_ops.append(S.dma("sync", DBG[:, 8576 + 32 * q_:8608 + 32 * q_], SK_[q_][:], reads=[rSK[q_]], writes=[S.res()]))

        S.barrier()
        if "A" in phases:
            with contextlib.ExitStack() as es:
                SC = SB(es, "SC", [128, SEQ], F32); rSC = S.res()
                CM = SB(es, "CM", [128, 128], F32); PADM = SB(es, "PADM", [128, 384], F32); BPV = SB(es, "BPV", [128, 1], F32)
                EBA = SB(es, "EBA", [128, 2, 6, 128], BF16); EBB = SB(es, "EBB", [128, 2, 2, 6, 128], BF16); ESK = SB(es, "ESK", [128, 6], F32)
                S.dma("sync", CM[:], cmaskd[:, :], writes=[Rc]); S.dma("sync", PADM[:], padmd[:, :], writes=[Rc]); S.dma("sync", BPV[:], bpvd[:, :], writes=[Rc])
                S.dma("sync", ESK[:], skd[:, :], writes=[Rc])
                S.op("scalar", L("activation", out=ESK[:], in_=ESK[:], func=AF.Exp), reads=[Rc], writes=[Rc])
                with contextlib.ExitStack() as es2:
                    tA = SB(es2, "tA", [128, 2, 6, 128], F32); tC = SB(es2, "tC", [128, 2, 6, 128], F32); tV = SB(es2, "tV", [128, 2, 128], F32)
                    tB = SB(es2, "tB", [128, 2, 2, 6, 128], F32); tW = SB(es2, "tW", [128, 2, 128], F32)
                    r1 = S.res()
                    S.dma("sync", tA[:], bad[:, :, :, :], writes=[r1]); S.dma("sync", tC[:], cad[:, :, :, :], writes=[r1])
                    S.dma("sync", tV[:], vad[:, :, :], writes=[r1]); S.dma("sync", tB[:], bbd[:, :, :, :, :], writes=[r1]); S.dma("sync", tW[:], vb01d[:, :, :], writes=[r1])
                    S.op("vector", L("tensor_tensor", out=tA[:], in0=tA[:], in1=tC[:], op=ALU.subtract), reads=[r1], writes=[r1])
                    S.op("scalar", L("activation", out=tA[:], in_=tA[:], func=AF.Exp), reads=[r1], writes=[r1])
                    S.op("scalar", L("activation", out=tB[:], in_=tB[:], func=AF.Exp), reads=[r1], writes=[r1])
                    for ty in range(2):
                        for h in range(6):
                            S.op("vector", L("tensor_tensor", out=EBA[:, ty, h, :], in0=tA[:, ty, h, :], in1=tV[:, ty, :], op=ALU.mult),
                                 reads=[r1], writes=[Rc])
                        for j in range(2):
                            for c in range(6):
                                S.op("vector", L("tensor_tensor", out=EBB[:, ty, j, c, :], in0=tB[:, ty, j, c, :], in1=tW[:, ty, :], op=ALU.mult),
                                     reads=[r1], writes=[Rc])
                S.barrier()
                IQT = SB(es, "IQT", [128, 8, 2, 128], BF16); rIQT = S.res()
                S.op("gpsimd", L("memset", IQT[:], 0.0), writes=[rIQT])
                IWT = SB(es, "IWT", [16, 128], BF16); rIWT = S.res()
                IWF = SB(es, "IWF", [128, 16], F32); rIWF = S.res()
                DG = SB(es, "DG", [128, 16, 128], BF16); rDG = S.res()
                QAT = SB(es, "QAT", [128, 6, 128], BF16); rQAT = S.res()
                QBT = SB(es, "QBT", [128, 6, 128], BF16); rQBT = S.res()
                QCT = SB(es, "QCT", [128, 4, 128], BF16); rQCT = S.res()
                GA = SB(es, "GA", [128, 6, 128], BF16); rGA = S.res()
                GB = SB(es, "GB", [128, 6, 128], BF16); rGB = S.res()
                GC = SB(es, "GC", [128, 4, 128], BF16); rGC = S.res()
                RL = [SB(es, f"RL{q}", [128, 512], BF16) for q in range(4)]; rRL = [S.res() for _ in range(4)]
                QT = [SB(es, f"QT{q}", [128, 2, 384], BF16) for q in range(2)]; rQT = [S.res() for _ in range(2)]
                PT = [SB(es, f"PT{q}", [128, 3, 128], BF16) for q in range(3)]; rPT = [S.res() for _ in range(3)]
                RD = SB(es, "RD", [128, 384], F32); rRD = S.res()
                DN = SB(es, "DN", [128, 384], F32); rDN = S.res()
                PV32 = SB(es, "PV32", [128, 384], F32); rPV32 = S.res()
                ON = SB(es, "ON", [128, 2, 384], BF16); rON = S.res()
                OAG = [SB(es, f"OAG{q}", [128, 3, 128], BF16) for q in range(2)]; rOAG = [S.res() for _ in range(2)]
                THR = SB(es, "THR", [128, 4], F32); rTHR = S.res()
                J1 = SB(es, "J1", [128, 2], BF16); rJ1 = S.res()
                PTB = SB(es, "PTB", [128, 2, 6, 128], BF16); rPTB = S.res()
                VBP = SB(es, "VBP", [128, 2, 2, 128], BF16); rVBP = S.res()
                RDB = SB(es, "RDB", [128, 768], F32); rRDB = S.res()
                OBG = SB(es, "OBG", [128, 6, 128], BF16); rOBG = S.res()
                PTC = PTB[:, :, 0:4, :]; rPTC = rPTB
                RDC = RDB[:, 0:512]; rRDC = rRDB
                OCG = OBG[:, 0:4, :]; rOCG = rOBG
                S.op("gpsimd", L("memset", VBP[:], 0.0), writes=[rVBP])
                qs_res = lambda c0, n, i: [rQS[(c0 + c, i // 4)] for c in range(n)] if rQS else []

                def qload(dst, rdst, c0, n, i):
                    S.dma("sync", dst[:], QS[c0:c0 + n, i].rearrange("c p t -> p c t"), reads=qs_res(c0, n, i), writes=[rdst])

                F8 = mybir.dt.float8e4
                MBF = SB(es, "MBF", [128, SEQ], F8); rMBF = S.res()
                PSB6 = PS[6][:].bitcast(BF16)
                DVE_HEADS = (1, 3, 5, 7, 9, 11, 13, 15)

                def gen_idx(i):
                    nsb = i + 1
                    S_i = 512 * nsb
                    for half in range(2):
                        S.dma("sync", IQT[half * 64:(half + 1) * 64, :, half, :], QS[C_IQ:C_IQ + 8, i, half * 64:(half + 1) * 64, :].rearrange("c p t -> p c t"),
                              reads=qs_res(C_IQ, 8, i), writes=[rIQT])
                    S.dma("sync", IWT[:], IWS[i], reads=[rIWS[i // 4]] if rIWS else [], writes=[rIWT])
                    S.op("tensor", L("transpose", out=PSB6[:, 0:16], in_=IWT[:], identity=IDENT[0:16, 0:16]), reads=[rIWT, Rc], writes=[RPS[6]])
                    S.op("scalar", L("copy", out=IWF[:], in_=PSB6[:, 0:16]), reads=[RPS[6]], writes=[rIWF])
                    for h in range(16):
                        S.op("vector", L("tensor_scalar", out=DG[:, h, :], in0=IDENT, scalar1=IWF[:, h:h + 1], scalar2=None, op0=ALU.mult),
                             reads=[rIWF, Rc], writes=[rDG])
                    yield 1.0
                    seq = [(sb, h) for sb in range(nsb) for h in range(16)]
                    bcs = {}

                    def score_mm(idx):
                        sb, h = seq[idx]
                        bc = (5, 6, 0, 1, 2, 3)[nxt("psc", 6)]
                        bcs[idx] = bc
                        S.op("tensor", L("matmul", PS[bc][:, :], lhsT=IQT[:, h // 2, h % 2, :], rhs=IKT[:, sb * 512:(sb + 1) * 512], start=True, stop=True),
                             reads=[rIQT, rIKT], writes=[RPS[bc]])

                    score_mm(0)
                    score_mm(1)
                    score_mm(2)
                    bS = 7
                    for idx, (sb, h) in enumerate(seq):
                        if idx + 3 < len(seq):
                            score_mm(idx + 3)
                        bc = bcs[idx]
                        rl = nxt("rl", 4)
                        if h == 0:
                            bS = (7, 4)[nxt("psS", 2)]
                        if h in DVE_HEADS:
                            S.op("vector", L("tensor_scalar", out=RL[rl][:], in0=PS[bc][:, :], scalar1=0.0, scalar2=None, op0=ALU.max), reads=[RPS[bc]], writes=[rRL[rl]])
                        else:
                            S.op("scalar", L("activation", out=RL[rl][:], in_=PS[bc][:, :], func=AF.Relu), reads=[RPS[bc]], writes=[rRL[rl]])
                        yield 0.25
                        S.op("tensor", L("matmul", PS[bS][:, :], lhsT=DG[:, h, :], rhs=RL[rl][:], start=(h == 0), stop=(h == 15)),
                             reads=[rDG, rRL[rl]], writes=[RPS[bS]])
                        if h == 15:
                            if sb == i:
                                pieces = [(0, 384, PADM if sb == 0 else None), (384, 512, CM)]
                            elif sb == 0:
                                pieces = [(0, 384, PADM), (384, 512, None)]
                            else:
                                pieces = [(0, 512, None)]
                            for (c0, c1, mk) in pieces:
                                if mk is None:
                                    S.op("scalar", L("copy", out=SC[:, sb * 512 + c0:sb * 512 + c1], in_=PS[bS][:, c0:c1]), reads=[RPS[bS]], writes=[rSC])
                                else:
                                    S.op("vector", L("tensor_tensor", out=SC[:, sb * 512 + c0:sb * 512 + c1], in0=PS[bS][:, c0:c1], in1=mk[:, 0:c1 - c0], op=ALU.add),
                                         reads=[RPS[bS], Rc], writes=[rSC])
                        yield 0.25
                    yield -1.0
                    S.op("vector", L("memset", THR[:], 0.0), writes=[rTHR])
                    for n in range(NIT):
                        step = LIM / 2.0 ** (n + 1)
                        S.op("vector", L("tensor_scalar", out=J1[:, 0:1].to_broadcast([128, S_i]), in0=SC[:, 0:S_i], scalar1=THR[:, 0:1], scalar2=None,
                                         op0=ALU.is_ge, op1=ALU.add, accum_out=THR[:, 1:2]), reads=[rSC, rTHR], writes=[rJ1, rTHR])
                        S.op("vector", L("tensor_scalar", out=THR[:, 2:3], in0=THR[:, 1:2], scalar1=255.5, scalar2=2.0 * step, op0=ALU.is_gt, op1=ALU.mult),
                             reads=[rTHR], writes=[rTHR])
                        S.op("vector", L("scalar_tensor_tensor", out=THR[:, 0:1], in0=THR[:, 2:3], scalar=-step, in1=THR[:, 0:1], op0=ALU.add, op1=ALU.add),
                             reads=[rTHR], writes=[rTHR])
                        yield 0.5 + 8.6 * S_i / 8192.0
                    S.op("vector", L("tensor_scalar", out=THR[:, 3:4], in0=THR[:, 0:1], scalar1=-LIM / 2.0 ** NIT, scalar2=None, op0=ALU.add),
                         reads=[rTHR], writes=[rTHR])
                    for sb in range(nsb):
                        S.op("vector", L("tensor_scalar", out=MBF[:, sb * 512:(sb + 1) * 512], in0=SC[:, sb * 512:(sb + 1) * 512], scalar1=THR[:, 3:4], scalar2=1.0,
                                         op0=ALU.is_ge, op1=ALU.subtract, saturate=False), reads=[rSC, rTHR], writes=[rMBF])
                        yield 0.4

                def gen_att(i):
                    nsb = i + 1
                    nkb = 4 * nsb
                    qload(QAT, rQAT, C_AQ, 6, i); qload(GA, rGA, C_AG, 6, i)
                    for g in range(2):
                        qt = nxt("qt", 2)
                        for rc in range(2):
                            for hh in range(3):
                                h = 3 * g + hh
                                S.op("tensor", L("matmul", PS[rc][:, hh * 128:(hh + 1) * 128], lhsT=WKT[:, h, rc * 128:(rc + 1) * 128],
                                                 rhs=QAT[:, h, :], start=True, stop=True), reads=[rQAT, rKC], writes=[RPS[rc]])
                            S.op("scalar", L("copy", out=QT[qt][:, rc, :], in_=PS[rc][:, 0:384]), reads=[RPS[rc]], writes=[rQT[qt]])
                        yield 1.0
                        bts = {}

                        def qk_mm(kbi):
                            bt = nxt("pst", 2)
                            bts[kbi] = bt
                            for rc in range(2):
                                S.op("tensor", L("matmul", PS[bt][:, 0:384], lhsT=CKVNT[:, rc, kbi * 128:(kbi + 1) * 128], rhs=QT[qt][:, rc, :],
                                                 start=(rc == 0), stop=False), reads=[rCKVNT, rQT[qt]], writes=[RPS[bt]])
                            S.op("tensor", L("matmul", PS[bt][:, 0:384], lhsT=MBF[:, kbi * 128:(kbi + 1) * 128], rhs=IREP, start=False, stop=True),
                                 reads=[rMBF, Rc], writes=[RPS[bt]])

                        qk_mm(0)
                        for kbi in range(nkb):
                            if kbi + 1 < nkb:
                                qk_mm(kbi + 1)
                            bt = bts[kbi]
                            pt = nxt("pt", 3)
                            for hh in range(3):
                                S.op("scalar", L("activation", out=PT[pt][:, hh, :], in_=PS[bt][:, hh * 128:(hh + 1) * 128], func=AF.Exp,
                                                 scale=RK[:, kbi, 3 * g + hh:3 * g + hh + 1]), reads=[RPS[bt], rRK], writes=[rPT[pt]])
                            if kbi >= 4 * i + 2:
                                ty = 0 if kbi == 4 * i + 3 else 1
                                S.op("vector", L("tensor_tensor", out=PT[pt][:], in0=PT[pt][:], in1=EBA[:, ty, 3 * g:3 * g + 3, :], op=ALU.mult),
                                     reads=[rPT[pt], Rc], writes=[rPT[pt]])
                            yield 0.55
                            ptf = PT[pt][:].rearrange("p a b -> p (a b)")
                            for rc in range(2):
                                S.op("tensor", L("matmul", PS[2 + rc][:, 0:384], lhsT=CKVN[:, kbi, rc * 128:(rc + 1) * 128], rhs=ptf,
                                                 start=(kbi == 0), stop=(kbi == nkb - 1)), reads=[rCKVN, rPT[pt]], writes=[RPS[2 + rc]])
                            S.op("tensor", L("matmul", PS[4][:, 0:384], lhsT=ONES, rhs=ptf, start=(kbi == 0), stop=(kbi == nkb - 1)),
                                 reads=[Rc, rPT[pt]], writes=[RPS[4]])
                            yield 0.55
                        for rc in range(2):
                            S.op("scalar", L("copy", out=ON[:, rc, :], in_=PS[2 + rc][:, 0:384]), reads=[RPS[2 + rc]], writes=[rON])
                        S.op("scalar", L("copy", out=DN[:], in_=PS[4][:, 0:384]), reads=[RPS[4]], writes=[rDN])
                        yield 1.0
                        for hh in range(3):
                            h = 3 * g + hh
                            for rc in range(2):
                                S.op("tensor", L("matmul", PS[0][:, hh * 128:(hh + 1) * 128], lhsT=WV[:, rc, h * 128:(h + 1) * 128],
                                                 rhs=ON[:, rc, hh * 128:(hh + 1) * 128], start=(rc == 0), stop=(rc == 1)),
                                     reads=[rON, rKC], writes=[RPS[0]])
                        S.op("scalar", L("copy", out=PV32[:], in_=PS[0][:, 0:384]), reads=[RPS[0]], writes=[rPV32])
                        og = nxt("oag", 2)
                        S.op("vector", L("reciprocal", out=RD[:], in_=DN[:]), reads=[rDN], writes=[rRD])
                        S.op("vector", L("tensor_tensor", out=PV32[:], in0=PV32[:], in1=RD[:], op=ALU.mult), reads=[rPV32, rRD], writes=[rPV32])
                        S.op("vector", L("tensor_tensor", out=OAG[og][:].rearrange("p a b -> p (a b)"), in0=PV32[:],
                                         in1=GA[:, 3 * g:3 * g + 3, :].rearrange("p a b -> p (a b)"), op=ALU.mult),
                             reads=[rPV32, rGA], writes=[rOAG[og]])
                        rOS[(g, i)] = S.res()
                        S.dma("gpsimd", OS[3 * g:3 * g + 3, i].rearrange("c p t -> p c t"), OAG[og][:], reads=[rOAG[og]], writes=[rOS[(g, i)]])
                        yield 1.0

                def gen_bc(i):
                    qload(QBT, rQBT, C_BQ, 6, i); qload(GB, rGB, C_BG, 6, i)
                    qload(QCT, rQCT, C_CQ, 4, i); qload(GC, rGC, C_CG, 4, i)
                    for blk in range(2):
                        slot = 2 * i + blk
                        S.op("gpsimd", L("tensor_copy", out=VBP[:, blk, 0, 0:64], in_=VB[:, slot, 0:64]), reads=[rVB], writes=[rVBP])
                        S.op("gpsimd", L("tensor_copy", out=VBP[:, blk, 1, 64:128], in_=VB[:, slot, 64:128]), reads=[rVB], writes=[rVBP])
                    yield 0.5
                    for j in range(2):
                        r0, r1_ = j * 64, (j + 1) * 64
                        for hf in range(2):
                            c0 = 3 * hf
                            for blk in range(2):
                                slot = 2 * i + blk
                                S.op("tensor", L("matmul", PS[5][:, 0:384], lhsT=KBT[r0:r1_, slot, :], rhs=QBT[r0:r1_, c0:c0 + 3, :].rearrange("p a b -> p (a b)"),
                                                 start=True, stop=True), reads=[rKBT, rQBT], writes=[RPS[5]])
                                S.op("scalar", L("activation", out=PTB[:, blk, c0:c0 + 3, :].rearrange("p a b -> p (a b)"), in_=PS[5][:, 0:384], func=AF.Exp, scale=0.125),
                                     reads=[RPS[5]], writes=[rPTB])
                                if blk == 0 and i == 0:
                                    S.op("vector", L("scalar_tensor_tensor", out=PTB[:, blk, c0:c0 + 3, :], in0=PTB[:, blk, c0:c0 + 3, :], scalar=BPV[:, 0:1],
                                                     in1=EBB[:, blk, j, c0:c0 + 3, :], op0=ALU.mult, op1=ALU.mult), reads=[rPTB, Rc], writes=[rPTB])
                                else:
                                    S.op("vector", L("tensor_tensor", out=PTB[:, blk, c0:c0 + 3, :], in0=PTB[:, blk, c0:c0 + 3, :], in1=EBB[:, blk, j, c0:c0 + 3, :], op=ALU.mult),
                                         reads=[rPTB, Rc], writes=[rPTB])
                                yield 0.7
                            for (bank, lh_kind) in ((6, "v"), (7, "o")):
                                for blk in range(2):
                                    lh = VBP[:, blk, j, :] if lh_kind == "v" else ONESH[:, j * 128:(j + 1) * 128]
                                    S.op("tensor", L("matmul", PS[bank][:, 0:384], lhsT=lh, rhs=PTB[:, blk, c0:c0 + 3, :].rearrange("p a b -> p (a b)"),
                                                     start=(blk == 0), stop=(blk == 1)), reads=[rVBP, rPTB, Rc], writes=[RPS[bank]])
                            for cc in range(3):
                                c = c0 + cc
                                S.op("vector", L("tensor_scalar", out=RDB[r0:r1_, c * 128:(c + 1) * 128], in0=PS[7][r0:r1_, cc * 128:(cc + 1) * 128],
                                                 scalar1=ESK[r0:r1_, c:c + 1], scalar2=None, op0=ALU.add), reads=[RPS[7], Rc], writes=[rRDB])
                            S.op("vector", L("reciprocal", out=RDB[r0:r1_, c0 * 128:(c0 + 3) * 128], in_=RDB[r0:r1_, c0 * 128:(c0 + 3) * 128]), reads=[rRDB], writes=[rRDB])
                            S.op("vector", L("tensor_tensor", out=RDB[r0:r1_, c0 * 128:(c0 + 3) * 128], in0=PS[6][r0:r1_, 0:384], in1=RDB[r0:r1_, c0 * 128:(c0 + 3) * 128], op=ALU.mult),
                                 reads=[RPS[6], rRDB], writes=[rRDB])
                            S.op("vector", L("tensor_tensor", out=OBG[r0:r1_, c0:c0 + 3, :].rearrange("p a b -> p (a b)"), in0=RDB[r0:r1_, c0 * 128:(c0 + 3) * 128],
                                             in1=GB[r0:r1_, c0:c0 + 3, :].rearrange("p a b -> p (a b)"), op=ALU.mult), reads=[rRDB, rGB], writes=[rOBG])
                            yield 1.0
                    rOS[(2, i)] = S.res()
                    S.dma("gpsimd", OS[6:12, i].rearrange("c p t -> p c t"), OBG[:], reads=[rOBG], writes=[rOS[(2, i)]])
                    for mbk in range(2):
                        for h in range(4):
                            S.op("tensor", L("matmul", PS[5][:, h * 128:(h + 1) * 128], lhsT=KCT[:, h, mbk * 128:(mbk + 1) * 128], rhs=QCT[:, h, :],
                                             start=True, stop=True), reads=[rKC, rQCT], writes=[RPS[5]])
                        S.op("scalar", L("activation", out=PTC[:, mbk, :, :].rearrange("p a b -> p (a b)"), in_=PS[5][:, :], func=AF.Exp, scale=128.0 ** -0.5),
                             reads=[RPS[5]], writes=[rPTC])
                        yield 0.8
                    for h in range(4):
                        for mbk in range(2):
                            S.op("tensor", L("matmul", PS[6][:, h * 128:(h + 1) * 128], lhsT=VC[:, mbk, h * 128:(h + 1) * 128], rhs=PTC[:, mbk, h, :],
                                             start=(mbk == 0), stop=(mbk == 1)), reads=[rKC, rPTC], writes=[RPS[6]])
                    for mbk in range(2):
                        S.op("tensor", L("matmul", PS[7][:, :], lhsT=ONES, rhs=PTC[:, mbk, :, :].rearrange("p a b -> p (a b)"), start=(mbk == 0), stop=(mbk == 1)),
                             reads=[Rc, rPTC], writes=[RPS[7]])
                    S.op("vector", L("reciprocal", out=RDC[:], in_=PS[7][:, :]), reads=[RPS[7]], writes=[rRDC])
                    S.op("vector", L("tensor_tensor", out=RDC[:], in0=PS[6][:, :], in1=RDC[:], op=ALU.mult), reads=[RPS[6], rRDC], writes=[rRDC])
                    S.op("vector", L("tensor_tensor", out=OCG[:].rearrange("p a b -> p (a b)"), in0=RDC[:], in1=GC[:].rearrange("p a b -> p (a b)"), op=ALU.mult),
                         reads=[rRDC, rGC], writes=[rOCG])
                    rOS[(3, i)] = S.res()
                    S.dma("gpsimd", OS[12:16, i].rearrange("c p t -> p c t"), OCG[:], reads=[rOCG], writes=[rOS[(3, i)]])
                    yield 1.0

                def total_idx(i):
                    nsb = i + 1
                    return 1.0 + 0.5 * 16 * nsb + NIT * (0.5 + 8.6 * 512 * nsb / 8192.0) + 0.4 * nsb

                def total_att(i):
                    nkb = 4 * (i + 1)
                    return 2 * (1.0 + 1.1 * nkb + 2.0) + 2 * (2.0 + 2.0) + 3.0

                def drive(tasks):
                    prog = [0.0 for _ in tasks]
                    alive = [True for _ in tasks]
                    while any(alive):
                        k = min((q for q in range(len(tasks)) if alive[q]), key=lambda q: prog[q] / tasks[q][1])
                        try:
                            prog[k] += next(tasks[k][0])
                        except StopIteration:
                            alive[k] = False

                def run_scores(gi):
                    for w in gi:
                        if w < 0:
                            break

                def total_rest(i):
                    nsb = i + 1
                    return NIT * (0.5 + 8.6 * 512 * nsb / 8192.0) + 0.4 * nsb

                g0 = gen_idx(0)
                run_scores(g0)
                drive([(g0, total_rest(0))])
                for i in range(NQ):
                    tasks = [(gen_att(i), total_att(i)), (gen_bc(i), 14.0)]
                    if i + 1 < NQ:
                        gi = gen_idx(i + 1)
                        run_scores(gi)
                        tasks.append((gi, total_rest(i + 1)))
                    drive(tasks)
            S.barrier()
        mid.close()
        if "M" in phases:
            with contextlib.ExitStack() as es:
                WUP = SB(es, "WUP", [128, 16, D], BF16); rWUP = S.res()
                WO = SB(es, "WO", [128, KC, D], BF16); rWO = S.res()
                with contextlib.ExitStack() as es2:
                    stage = mk_stage(es2)
                    load_cast_w(stage, WUP[:, 0:6, :], wupd[0], D, None, rWUP, nk=6)
                    load_cast_w(stage, WUP[:, 6:12, :], wupd[1], D, None, rWUP, nk=6)
                    load_cast_w(stage, WUP[:, 12:16, :], wupd[2], D, None, rWUP, nk=4)
                    load_cast_w(stage, WO, wod, D, None, rWO)
                S.barrier()
                OG = SB(es, "OG", [128, 16, 512], BF16); rOG = S.res()
                GM = [SB(es, f"GM{q}", [128, 3, 512], BF16) for q in range(2)]; rGM = [S.res() for _ in range(2)]
                TT = [SB(es, f"TT{q}", [128, 3, 512], F32) for q in range(2)]; rTT = [S.res() for _ in range(2)]
                MT = SB(es, "MT", [128, 16, 512], BF16); rMT = S.res()
                XR = [SB(es, f"XR{q}", [128, D], F32) for q in range(2)]; rXR = [S.res() for _ in range(2)]
                nbr = (6, 6, 4); cbase = (0, 6, 12)
                for m in range(4):
                    for c in range(16):
                        grp = (0 if c < 3 else 1) if c < 6 else (2 if c < 12 else 3)
                        S.dma("sync", OG[:, c, :].rearrange("p (q t) -> p q t", q=4), OS[c, 4 * m:4 * m + 4].rearrange("q p t -> p q t"),
                              reads=[rOS[(grp, 4 * m + q)] for q in range(4)] if rOS else [], writes=[rOG])
                    for f in range(16):
                        gb = nxt("gm", 2)
                        for br in range(3):
                            ch = C_MIX + br * 16 + f
                            S.dma("sync", GM[gb][:, br, :].rearrange("p (q t) -> p q t", q=4), QS[ch, 4 * m:4 * m + 4].rearrange("q p t -> p q t"),
                                  reads=[rQS[(ch, m)]] if rQS else [], writes=[rGM[gb]])
                        for br in range(3):
                            for cc in range(nbr[br]):
                                c = cbase[br] + cc
                                S.op("tensor", L("matmul", PS[br][:, :], lhsT=WUP[:, c, f * 128:(f + 1) * 128], rhs=OG[:, c, :],
                                                                                        start=(cc == 0), stop=(cc == nbr[br] - 1)), reads=[rWUP, rOG], writes=[RPS[br]])
                        tb = nxt("tt", 2)
                        for br in range(3):
                            S.op("vector", L("tensor_tensor", out=TT[tb][:, br, :], in0=PS[br][:, :], in1=GM[gb][:, br, :], op=ALU.mult),
                                 reads=[RPS[br], rGM[gb]], writes=[rTT[tb]])
                        S.op("gpsimd", L("tensor_tensor", out=TT[tb][:, 0, :], in0=TT[tb][:, 0, :], in1=TT[tb][:, 1, :], op=ALU.add), reads=[rTT[tb]], writes=[rTT[tb]])
                        S.op("gpsimd", L("tensor_tensor", out=MT[:, f, :], in0=TT[tb][:, 0, :], in1=TT[tb][:, 2, :], op=ALU.add), reads=[rTT[tb]], writes=[rMT])
                    for q in range(4):
                        i = 4 * m + q
                        xb = nxt("xr", 2)
                        S.dma("sync", XR[xb][:], xk[(4 * i + 3) * 128:(4 * i + 4) * 128, :], writes=[rXR[xb]])
                        for n in range(4):
                            pb = 3 + nxt("py", 4)
                            for f in range(16):
                                S.op("tensor", L("matmul", PS[pb][:, :], lhsT=MT[:, f, q * 128:(q + 1) * 128], rhs=WO[:, f, n * 512:(n + 1) * 512],
                                                                                      start=(f == 0), stop=(f == 15)), reads=[rMT, rWO], writes=[RPS[pb]])
                            S.op("vector", L("tensor_tensor", out=XR[xb][:, n * 512:(n + 1) * 512], in0=PS[pb][:, :], in1=XR[xb][:, n * 512:(n + 1) * 512], op=ALU.add),
                                 reads=[RPS[pb], rXR[xb]], writes=[rXR[xb]])
                        S.dma("gpsimd", outd[i * 128:(i + 1) * 128, :], XR[xb][:], reads=[rXR[xb]], writes=[S.res()])
            S.barrier()
        if debug and "K" in phases:
            dl = [("CKVNT", CKVNT[:, 0, 0:2048], 0, 2048, rCKVNT), ("CKVN", CKVN[:, 5, :], 2048, 256, rCKVN),
                  ("IKT", IKT[:, 0:2048], 2304, 2048, rIKT), ("KBT", KBT[:, 3, :], 4352, 128, rKBT), ("VB", VB[:, 3, :], 4480, 128, rVB),
                  ("RK", RK[:].rearrange("p a b -> p (a b)"), 4608, 384, rRK), ("KCT", KCT[:].rearrange("p a b -> p (a b)"), 4992, 1024, rKC),
                  ("VC", VC[:].rearrange("p a b -> p (a b)"), 6016, 1024, rKC), ("WKT", WKT[:].rearrange("p a b -> p (a b)"), 7040, 1536, rKC)]
            with contextlib.ExitStack() as es:
                dt_ = SB(es, "dbgt", [128, 2048], F32); rd = S.res()
                for nm, ap, off, n, rr in dl:
                    S.op("vector", L("tensor_copy", out=dt_[:, 0:n], in_=ap), reads=[rr], writes=[rd])
                    out_ops.append(S.dma("sync", DBG[:, off:off + n], dt_[:, 0:n], reads=[rd], writes=[S.res()]))
        for e_ in ("gpsimd",):
            out_ops += [o for o in S.ops[e_] if o.dma]
        S.wait_all("sync", out_ops)
        S.emit()
    return nc


_NC_CACHE = {}


def kernel(**inputs):
    sh, per = prep_inputs(inputs)
    if "nc" not in _NC_CACHE:
        _NC_CACHE["nc"] = build(debug=False)
    nc = _NC_CACHE["nc"]
    res = run_bass_kernel_spmd(nc, [dict(sh, **per[c]) for c in range(8)], core_ids=list(range(8)))
    out = np.zeros((2, SEQ, D), np.float32)
    for c in range(8):
        b, j = c // 4, c % 4
        o = np.asarray(res.results[c]["out"], dtype=np.float32).reshape(NQ, 128, D)
        out[b].reshape(NB, 128, D)[j::4] = o
    return out
```
